# Optimizing a Trainium2 kernel written in Bass

```python
import math
import jax, jax.numpy as jnp
from jax import lax
import numpy as np

D_MODEL = 2048
BATCH = 8
SEQ = 2048
DEPTH = 2

GDN_HEAD_DIM = 128
GDN_HEADS = D_MODEL // 256
GDN_WIDTH = GDN_HEADS * GDN_HEAD_DIM
GDN_CONV = 4
GDN_CHUNK = 64
SB_HEAD_DIM = 128
SB_HEADS = D_MODEL // 256
SB_WIDTH = SB_HEADS * SB_HEAD_DIM
SB_BLOCK = 128
MIX_WIDTH = GDN_WIDTH + SB_WIDTH
IN_SIZES = (GDN_WIDTH, GDN_WIDTH, GDN_WIDTH, GDN_WIDTH, GDN_HEADS, GDN_HEADS,
            SB_WIDTH, SB_WIDTH, SB_WIDTH)
IN_COLS = sum(IN_SIZES)
IN_OFFSETS = tuple(int(o) for o in np.cumsum(IN_SIZES)[:-1])
D_FF = ((8 * D_MODEL // 3 + 255) // 256) * 256
FFN_CONV = 3
RMS_EPS = 1e-6
L2_EPS = 1e-6

kernel_name = "hybrid_gdn_stickbreaking_convffn"


def rms_norm(x, gain, eps=RMS_EPS):
    xf = x.astype(jnp.float32)
    y = xf * lax.rsqrt(jnp.mean(xf * xf, axis=-1, keepdims=True) + eps)
    return (y * gain.astype(jnp.float32)).astype(x.dtype)


def l2_norm(x):
    return x * lax.rsqrt(jnp.sum(x * x, axis=-1, keepdims=True) + L2_EPS)


def causal_dwconv(x, w):
    width, channels = w.shape
    return lax.conv_general_dilated(
        x, w[:, None, :].astype(x.dtype), window_strides=(1,), padding=[(width - 1, 0)],
        dimension_numbers=("NWC", "WIO", "NWC"), feature_group_count=channels)


def chunk_gated_delta_rule(q, k, v, g, beta):
    B, S, H, Dk = q.shape
    Dv = v.shape[-1]
    C = GDN_CHUNK
    N = S // C

    def to_chunks(t):
        return t.reshape(B, N, C, H, t.shape[-1]).transpose(1, 0, 3, 2, 4)

    q, k, v = to_chunks(q), to_chunks(k), to_chunks(v)
    g = g.reshape(B, N, C, H).transpose(1, 0, 3, 2)
    beta = beta.reshape(B, N, C, H).transpose(1, 0, 3, 2)
    g = jnp.cumsum(g, axis=-1)
    idx = jnp.arange(C)
    tril = idx[:, None] >= idx[None, :]
    strict = idx[:, None] > idx[None, :]
    decay = jnp.exp(jnp.where(tril, g[..., :, None] - g[..., None, :], -jnp.inf))
    k_beta = k * beta[..., None]
    v_beta = v * beta[..., None]
    lower = jnp.where(strict, jnp.einsum('nbhcd,nbhed->nbhce', k_beta, k) * decay, 0.0)
    eye = jnp.eye(C, dtype=jnp.float32)
    t_inv = lax.linalg.triangular_solve(eye + lower, jnp.broadcast_to(eye, lower.shape),
                                        left_side=True, lower=True, unit_diagonal=True)
    u = jnp.einsum('nbhce,nbhed->nbhcd', t_inv, v_beta)
    w = jnp.einsum('nbhce,nbhed->nbhcd', t_inv, k_beta * jnp.exp(g)[..., None])
    intra = jnp.where(tril, jnp.einsum('nbhcd,nbhed->nbhce', q, k) * decay, 0.0)

    def step(state, inp):
        q_i, k_i, u_i, w_i, g_i, a_i = inp
        v_new = u_i - jnp.einsum('bhcd,bhde->bhce', w_i, state)
        o_i = (jnp.einsum('bhcd,bhde->bhce', q_i * jnp.exp(g_i)[..., None], state)
               + jnp.einsum('bhce,bhed->bhcd', a_i, v_new))
        g_last = g_i[..., -1]
        k_dec = k_i * jnp.exp(g_last[..., None] - g_i)[..., None]
        state = state * jnp.exp(g_last)[..., None, None] + jnp.einsum('bhcd,bhce->bhde', k_dec, v_new)
        return state, o_i

    state0 = jnp.zeros((B, H, Dk, Dv), jnp.float32)
    _, o = lax.scan(step, state0, (q, k, u, w, g, intra))
    return o.transpose(1, 0, 3, 2, 4).reshape(B, S, H, Dv)


def gated_deltanet(q, k, v, z, b, a, conv_w, a_log, dt_bias, o_gain):
    B, S, _ = q.shape
    qkv = jax.nn.silu(causal_dwconv(jnp.concatenate([q, k, v], axis=-1), conv_w))
    qkv = qkv.astype(jnp.float32).reshape(B, S, 3, GDN_HEADS, GDN_HEAD_DIM)
    q = l2_norm(qkv[:, :, 0]) * (GDN_HEAD_DIM ** -0.5)
    k = l2_norm(qkv[:, :, 1])
    v = qkv[:, :, 2]
    beta = jax.nn.sigmoid(b.astype(jnp.float32))
    g = -jnp.exp(a_log.astype(jnp.float32)) * jax.nn.softplus(
        a.astype(jnp.float32) + dt_bias.astype(jnp.float32))
    o = chunk_gated_delta_rule(q, k, v, g, beta)
    gate = jax.nn.silu(z.astype(jnp.float32).reshape(B, S, GDN_HEADS, GDN_HEAD_DIM))
    o = rms_norm(o, o_gain) * gate
    return o.reshape(B, S, GDN_WIDTH)


def stick_breaking_attention(q, k, v):
    B, H, S, D = q.shape
    scale = D ** -0.5
    outs = []
    for blk in range(S // SB_BLOCK):
        start = blk * SB_BLOCK
        end = start + SB_BLOCK
        k_p, v_p = k[:, :, :end], v[:, :, :end]
        z = jnp.einsum('bhtd,bhsd->bhts', q[:, :, start:end], k_p) * scale
        past = jnp.arange(end)[None, :] < (start + jnp.arange(SB_BLOCK))[:, None]
        log_beta = jax.nn.log_sigmoid(z)
        log_keep = jnp.where(past, log_beta - z, 0.0)
        between = lax.cumsum(log_keep, axis=3, reverse=True) - log_keep
        weights = jnp.where(past, jnp.exp(log_beta + between), 0.0)
        outs.append(jnp.einsum('bhts,bhsd->bhtd', weights, v_p))
    return jnp.concatenate(outs, axis=2)


def setup_inputs(seed: int = 0) -> dict:
    key = jax.random.key(seed)
    ks = jax.random.split(key, 16)
    f32 = jnp.float32

    def gain(k, n):
        return 1.0 + 0.02 * jax.random.normal(k, (DEPTH, n), f32)

    x = jax.random.normal(ks[0], (BATCH, SEQ, D_MODEL), f32)
    attn_norm = gain(ks[1], D_MODEL)
    w_in = jax.random.normal(ks[2], (DEPTH, D_MODEL, IN_COLS), f32) * D_MODEL ** -0.5
    gdn_conv = jax.random.normal(ks[3], (DEPTH, GDN_CONV, 3 * GDN_WIDTH), f32) * GDN_CONV ** -0.5
    gdn_a_log = jnp.log(jax.random.uniform(ks[4], (DEPTH, GDN_HEADS), f32, 1.0, 16.0))
    dt = jnp.exp(jax.random.uniform(ks[5], (DEPTH, GDN_HEADS), f32, math.log(1e-3), math.log(1e-1)))
    gdn_dt_bias = dt + jnp.log(-jnp.expm1(-dt))
    gdn_o_norm = gain(ks[6], GDN_HEAD_DIM)
    sb_q_norm = gain(ks[7], SB_HEAD_DIM)
    sb_k_norm = gain(ks[8], SB_HEAD_DIM)
    sb_o_norm = gain(ks[9], SB_HEAD_DIM)
    w_out = jax.random.normal(ks[10], (DEPTH, MIX_WIDTH, D_MODEL), f32) * MIX_WIDTH ** -0.5
    ffn_norm = gain(ks[11], D_MODEL)
    w_up = jax.random.normal(ks[12], (DEPTH, D_MODEL, 2 * D_FF), f32) * D_MODEL ** -0.5
    ffn_conv = jax.random.normal(ks[13], (DEPTH, FFN_CONV, 2 * D_FF), f32) * FFN_CONV ** -0.5
    ffn_conv_bias = 0.02 * jax.random.normal(ks[14], (DEPTH, 2 * D_FF), f32)
    w_down = jax.random.normal(ks[15], (DEPTH, D_FF, D_MODEL), f32) * D_FF ** -0.5
    return {"x": x, "attn_norm": attn_norm, "w_in": w_in, "gdn_conv": gdn_conv,
            "gdn_a_log": gdn_a_log, "gdn_dt_bias": gdn_dt_bias, "gdn_o_norm": gdn_o_norm,
            "sb_q_norm": sb_q_norm, "sb_k_norm": sb_k_norm, "sb_o_norm": sb_o_norm,
            "w_out": w_out, "ffn_norm": ffn_norm, "w_up": w_up, "ffn_conv": ffn_conv,
            "ffn_conv_bias": ffn_conv_bias, "w_down": w_down}


def reference(x, attn_norm, w_in, gdn_conv, gdn_a_log, gdn_dt_bias, gdn_o_norm,
              sb_q_norm, sb_k_norm, sb_o_norm, w_out, ffn_norm, w_up, ffn_conv,
              ffn_conv_bias, w_down):
    B, S, _ = x.shape
    for l in range(DEPTH):
        h = rms_norm(x, attn_norm[l])
        proj = h @ w_in[l]
        gq, gk, gv, gz, gb, ga, sq, sk, sv = jnp.split(proj, IN_OFFSETS, axis=-1)
        y_gdn = gated_deltanet(gq, gk, gv, gz, gb, ga, gdn_conv[l], gdn_a_log[l],
                               gdn_dt_bias[l], gdn_o_norm[l])
        sq = rms_norm(sq.astype(jnp.float32).reshape(B, S, SB_HEADS, SB_HEAD_DIM), sb_q_norm[l])
        sk = rms_norm(sk.astype(jnp.float32).reshape(B, S, SB_HEADS, SB_HEAD_DIM), sb_k_norm[l])
        sv = sv.astype(jnp.float32).reshape(B, S, SB_HEADS, SB_HEAD_DIM)
        y_sb = stick_breaking_attention(sq.transpose(0, 2, 1, 3), sk.transpose(0, 2, 1, 3),
                                        sv.transpose(0, 2, 1, 3))
        y_sb = rms_norm(y_sb.transpose(0, 2, 1, 3), sb_o_norm[l]).reshape(B, S, SB_WIDTH)
        mixed = jnp.concatenate([y_gdn, y_sb], axis=-1).astype(x.dtype)
        x = x + mixed @ w_out[l]
        h = rms_norm(x, ffn_norm[l])
        up = causal_dwconv(h @ w_up[l], ffn_conv[l]) + ffn_conv_bias[l]
        gate, val = jnp.split(up, 2, axis=-1)
        x = x + (jax.nn.silu(gate) * val) @ w_down[l]
    return x
```

```python
import numpy as np
from contextlib import ExitStack
import concourse.bass as bass
import concourse.mybir as mybir
from concourse.bass_utils import run_bass_kernel_spmd

F32 = mybir.dt.float32
BF16 = mybir.dt.bfloat16
AF = mybir.ActivationFunctionType
ALU = mybir.AluOpType
AX = mybir.AxisListType

D = 2048
S = 2048
DEPTH = 2
NT = S // 128
KC = D // 128
H = 8
HD = 128
IN_COLS = 7184
D_FF = 5632
NFF = D_FF // 128
RMS_EPS = 1e-6
L2_EPS = 1e-6

ENGS = ["pe", "act", "dve", "pool", "sp"]
SCOPES = False


class Buf:
    __slots__ = ("name", "w", "r", "dsem", "dcount")

    def __init__(self, name):
        self.name = name
        self.w = None
        self.r = {}
        self.dsem = None
        self.dcount = 0


class Sched:
    def __init__(self, nc):
        self.nc = nc
        self.sem = {e: nc.alloc_semaphore("cnt_" + e) for e in ENGS}
        self.count = {e: 0 for e in ENGS}
        self.known = {e: {} for e in ENGS}
        self.ops = {e: [] for e in ENGS}
        self.dma_bufs = []
        self.nwaits = 0
        self.muted = False
        self.free_sems = []
        self.nsem = 0
        self.scope = "init"

    def _deps(self, eng, reads, writes, skip_sem=None):
        waits = {}
        known = self.known[eng]
        own = self.sem[eng]

        def need(sem, val):
            if eng == "pe" and sem is own:
                return
            if known.get(sem, 0) >= val:
                return
            if waits.get(sem, 0) < val:
                waits[sem] = val

        for b in reads:
            if b.w is not None:
                need(*b.w)
        for b in writes:
            if b.w is not None and b.w[0] is not skip_sem:
                need(*b.w)
            for s, v in b.r.items():
                need(s, v)
        for s, v in waits.items():
            known[s] = v
        self.nwaits += len(waits)
        return list(waits.items())

    def _commit(self, ev, reads, writes):
        for b in reads:
            if b.r.get(ev[0], 0) < ev[1]:
                b.r[ev[0]] = ev[1]
        for b in writes:
            b.w = ev
            b.r = {}

    def op(self, eng, fn, reads=(), writes=(), sig=True):
        if self.muted:
            return
        waits = self._deps(eng, reads, writes)
        if sig:
            self.count[eng] += 1
            ev = (self.sem[eng], self.count[eng])
            inc = (self.sem[eng], 1)
        else:
            assert eng == "pe"
            ev = (self.sem[eng], self.count[eng] + 1)
            inc = None
        self.ops[eng].append((waits, fn, inc, self.scope))
        self._commit(ev, reads, writes)

    def dma(self, q, out, in_, reads, writes, owner, grp=False):
        if self.muted:
            return
        if owner.dsem is None:
            if self.free_sems:
                owner.dsem, owner.dcount = self.free_sems.pop()
            else:
                owner.dsem = self.nc.alloc_semaphore(f"dsem{self.nsem}")
                owner.dcount = 0
                self.nsem += 1
            self.dma_bufs.append(owner)
        waits = self._deps(q, reads, writes, owner.dsem if grp else None)
        owner.dcount += 16
        ev = (owner.dsem, owner.dcount)
        self.ops[q].append((waits, (lambda e, o=out, i=in_: e.dma_start(out=o, in_=i)), (owner.dsem, 16), self.scope))
        self._commit(ev, reads, writes)

    def barrier(self):
        if self.muted:
            return
        evs = [(self.sem[e], self.count[e]) for e in ENGS if self.count[e] > 0]
        evs += [(b.dsem, b.dcount) for b in self.dma_bufs]
        for e in ENGS:
            waits = []
            for s, v in evs:
                if e == "pe" and s is self.sem["pe"]:
                    continue
                if s is self.sem[e]:
                    continue
                if self.known[e].get(s, 0) < v:
                    self.known[e][s] = v
                    waits.append((s, v))
            if waits:
                self.ops[e].append((waits, None, None, self.scope))
        for b in self.dma_bufs:
            self.free_sems.append((b.dsem, b.dcount))
            b.dsem = None
        self.dma_bufs = []

    def emit(self, ename, eng):
        cur, cm = None, None
        for waits, fn, inc, scope in self.ops[ename]:
            if SCOPES and scope != cur:
                if cm is not None:
                    cm.__exit__(None, None, None)
                cm = self.nc.named_scope(scope)
                cm.__enter__()
                cur = scope
            for s, v in waits:
                eng.wait_ge(s, v)
            if fn is None:
                continue
            ins = fn(eng)
            if inc is not None:
                ins.then_inc(inc[0], inc[1])
        if cm is not None:
            cm.__exit__(None, None, None)


class T:
    def __init__(self, h, name, nbufs=1):
        self.h = h
        self.b = Buf(name)
        self.bs = [Buf(f"{name}_{i}") for i in range(nbufs)] if nbufs > 1 else [self.b]

    def __getitem__(self, k):
        return self.h[k]


class Ctx:
    def __init__(self, nc, K, es):
        self.nc, self.K, self.es = nc, K, es
        self.uid = 0

    def sb(self, es, name, shape, dt, nbufs=1):
        self.uid += 1
        nm = f"{name}_{self.uid}"
        return T(es.enter_context(self.nc.sbuf_tensor(nm, list(shape), dt)), nm, nbufs)

    def ps(self, es, name, shape, dt=F32, nbufs=1):
        self.uid += 1
        nm = f"{name}_{self.uid}"
        return T(es.enter_context(self.nc.psum_tensor(nm, list(shape), dt)), nm, nbufs)


def phase_norm(C, x_ap, x_buf, gain_row_ap, hT, consts):
    K, nc = C.K, C.nc
    with ExitStack() as es:
        gbc = C.sb(es, "gbc", [128, D], F32)
        xt = [C.sb(es, f"xt{i}", [128, D], F32) for i in range(2)]
        hs = [C.sb(es, f"hs{i}", [128, D], BF16) for i in range(2)]
        junk = C.sb(es, "junk", [128, D], BF16)
        st = [C.sb(es, f"st{i}", [128, 4], F32) for i in range(2)]
        pt = [C.ps(es, f"pt{i}", [128, D], BF16) for i in range(2)]
        K.dma("sp", gbc[:], gain_row_ap.partition_broadcast(128), [], [gbc.b], gbc.b)
        for t in range(NT):
            i = t % 2
            x_, h_, s_, p_ = xt[i], hs[i], st[i], pt[i]
            K.dma("sp", x_[:], x_ap[t * 128:(t + 1) * 128, :], [x_buf], [x_.b], x_.b)
            K.op("act", lambda e, x_=x_, s_=s_: e.activation(out=junk[:], in_=x_[:], func=AF.Square,
                                                           accum_out=s_[:, 0:1]),
                 [x_.b], [s_.b])
            K.op("dve", lambda e, s_=s_: e.tensor_scalar(out=s_[:, 1:2], in0=s_[:, 0:1], scalar1=1.0 / D,
                                                       scalar2=RMS_EPS, op0=ALU.mult, op1=ALU.add),
                 [s_.b], [s_.b])
            K.op("pool", lambda e, s_=s_: e.tensor_tensor(out=s_[:, 2:3], in0=s_[:, 1:2],
                                                        in1=consts["neghalf"][:, 0:1], op=ALU.pow),
                 [s_.b, consts["neghalf"].b], [s_.b])
            K.op("dve", lambda e, x_=x_, h_=h_, s_=s_: e.scalar_tensor_tensor(
                out=h_[:], in0=x_[:], scalar=s_[:, 2:3], in1=gbc[:], op0=ALU.mult, op1=ALU.mult),
                 [x_.b, s_.b, gbc.b], [h_.b])
            for kc in range(KC):
                K.op("pe", lambda e, h_=h_, p_=p_, kc=kc: e.transpose(
                    out=p_[:, kc * 128:(kc + 1) * 128], in_=h_[:, kc * 128:(kc + 1) * 128],
                    identity=consts["ident"][:]),
                     [h_.b, consts["ident"].b], [p_.b], sig=(kc == KC - 1))
            K.op("act", lambda e, p_=p_, t=t: e.copy(
                out=hT[:, :, t * 128:(t + 1) * 128],
                in_=p_[:].rearrange("p (k n) -> p k n", n=128)),
                 [p_.b], [hT.bs[t // 4]])
        K.barrier()


def alloc_stage(C, es, n=3):
    C.stg = [C.sb(es, f"stg{i}", [128, 4, 512], F32) for i in range(n)]
    C.stg_i = 0


def load_w(C, slot, w_ap, c0, ncols, col_off=0, nk=KC, wbuf=None):
    K = C.K
    wbuf = wbuf if wbuf is not None else slot.b
    step = 4
    for k0 in range(0, nk, step):
        kk = min(step, nk - k0)
        st = C.stg[C.stg_i % len(C.stg)]
        C.stg_i += 1
        src = w_ap[k0 * 128:(k0 + kk) * 128, c0:c0 + ncols].rearrange("(kc p) n -> p kc n", p=128)
        K.dma("sp", st[:, 0:kk, 0:ncols], src, [], [st.b], st.b)
        K.op("pool", lambda e, st=st, k0=k0, kk=kk: e.tensor_copy(
            out=slot[:, k0:k0 + kk, col_off:col_off + ncols], in_=st[:, 0:kk, 0:ncols]), [st.b], [wbuf])


def phase_ffn(C, l, x_in, x_in_buf, x_out, x_out_buf, ins, consts, scratch):
    K, nc = C.K, C.nc
    a_scr, a_buf = scratch["a"], scratch["a_buf"]
    with ExitStack() as es:
        hT = C.sb(es, "hT", [128, KC, S], BF16, nbufs=4)
        wsl = [C.sb(es, f"wup{i}", [128, KC, 512], BF16) for i in range(2)]
        alloc_stage(C, es)
        for g in range(2):
            load_w(C, wsl[g], ins["w_up"][l], g * 256, 256, 0)
            load_w(C, wsl[g], ins["w_up"][l], D_FF + g * 256, 256, 256)
        phase_norm(C, x_in, x_in_buf, ins["ffn_norm"][l:l + 1, :], hT, consts)
        with ExitStack() as es2:
            cw = C.sb(es2, "cw", [128, 4, 2 * NFF], F32)
            raw = [C.sb(es2, f"raw{i}", [128, 2 + S], F32) for i in range(4)]
            acc = [C.sb(es2, f"acc{i}", [128, S], F32) for i in range(2)]
            sg = C.sb(es2, "sg", [128, S], F32)
            aT = [C.sb(es2, f"aT{i}", [128, NT, 2, 128], BF16) for i in range(2)]
            pp = [C.ps(es2, f"pp{i}", [128, 512], F32) for i in range(8)]
            cwr = C.sb(es2, "cwr", [2 * NFF, 4, 128], F32)
            for tap in range(3):
                K.dma("sp", cwr[:, tap, :], ins["ffn_conv"][l, tap, :].rearrange("(j p) -> j p", p=128),
                      [], [cwr.b], cwr.b)
            K.dma("sp", cwr[:, 3, :], ins["ffn_conv_bias"][l, :].rearrange("(j p) -> j p", p=128),
                  [], [cwr.b], cwr.b)
            for tap in range(4):
                K.op("pe", lambda e, tap=tap: e.transpose(out=pp[0][:, tap * 128:tap * 128 + 2 * NFF],
                                                          in_=cwr[:, tap, :], identity=consts["identf"][0:2 * NFF, 0:2 * NFF]),
                     [cwr.b, consts["identf"].b], [pp[0].b], sig=(tap == 3))
            K.op("act", lambda e: e.copy(out=cw[:], in_=pp[0][:].rearrange("p (t n) -> p t n", n=128)[:, :, 0:2 * NFF]),
                 [pp[0].b], [cw.b])
            for r in raw:
                K.op("pool", lambda e, r=r: e.memset(r[:, 0:2], 0.0), [], [r.b])
            ngrp = NFF // 2
            pidx = 0
            for g in range(ngrp):
                w_ = wsl[g % 2]
                if g >= 2:
                    load_w(C, w_, ins["w_up"][l], g * 256, 256, 0)
                    load_w(C, w_, ins["w_up"][l], D_FF + g * 256, 256, 256)
                for jj in range(2):
                    j = g * 2 + jj
                    rg, ru = raw[(j % 2) * 2], raw[(j % 2) * 2 + 1]
                    for which, r_ in ((0, rg), (1, ru)):
                        co = which * 256 + jj * 128
                        for tc in range(4):
                            p_ = pp[pidx % 8]
                            pidx += 1
                            for kc in range(KC):
                                K.op("pe", lambda e, p_=p_, w_=w_, kc=kc, co=co, tc=tc: e.matmul(
                                    p_[:], lhsT=w_[:, kc, co:co + 128], rhs=hT[:, kc, tc * 512:(tc + 1) * 512],
                                    start=(kc == 0), stop=(kc == KC - 1)),
                                     [w_.b, hT.bs[tc]], [p_.b], sig=(kc == KC - 1))
                            K.op("act", lambda e, p_=p_, r_=r_, tc=tc: e.copy(
                                out=r_[:, 2 + tc * 512:2 + (tc + 1) * 512], in_=p_[:]), [p_.b], [r_.b])
                    a_ = aT[(j // 2) % 2]
                    for which, r_, ac in ((0, rg, acc[0]), (1, ru, acc[1])):
                        ch = which * NFF + j
                        K.op("act", lambda e, r_=r_, ac=ac, ch=ch: e.activation(
                            out=ac[:], in_=r_[:, 2:2 + S], func=AF.Identity,
                            scale=cw[:, 2, ch:ch + 1], bias=cw[:, 3, ch:ch + 1]), [r_.b, cw.b], [ac.b])
                        K.op("dve", lambda e, r_=r_, ac=ac, ch=ch: e.scalar_tensor_tensor(
                            out=ac[:], in0=r_[:, 1:1 + S], scalar=cw[:, 1, ch:ch + 1], in1=ac[:],
                            op0=ALU.mult, op1=ALU.add), [r_.b, cw.b, ac.b], [ac.b])
                        K.op("dve", lambda e, r_=r_, ac=ac, ch=ch: e.scalar_tensor_tensor(
                            out=ac[:], in0=r_[:, 0:S], scalar=cw[:, 0, ch:ch + 1], in1=ac[:],
                            op0=ALU.mult, op1=ALU.add), [r_.b, cw.b, ac.b], [ac.b])
                    K.op("act", lambda e: e.activation(out=sg[:], in_=acc[0][:], func=AF.Silu),
                         [acc[0].b], [sg.b])
                    K.op("dve", lambda e, a_=a_, j=j: e.tensor_tensor(out=a_[:, :, j % 2, :], in0=sg[:].rearrange('p (t n) -> p t n', n=128), in1=acc[1][:].rearrange('p (t n) -> p t n', n=128), op=ALU.mult),
                         [sg.b, acc[1].b], [a_.b])
                    if j % 2 == 1:
                        K.dma("sp", a_scr[:, :, j - 1:j + 1, :].rearrange("t p j n -> p t (j n)"),
                              a_[:].rearrange("p t j n -> p t (j n)"), [a_.b], [a_buf], a_.b)
        K.barrier()
    with ExitStack() as es:
        wd = [C.sb(es, f"wd{i}", [128, NFF, 512], BF16) for i in range(2)]
        alloc_stage(C, es)
        at = [C.sb(es, f"at{i}", [128, NFF, 128], BF16) for i in range(3)]
        xr = [C.sb(es, f"xr{i}", [128, 512], F32) for i in range(3)]
        xo = [C.sb(es, f"xo{i}", [128, 512], F32) for i in range(3)]
        pp = [C.ps(es, f"pd{i}", [128, 512], F32) for i in range(4)]
        n = 0
        for q in range(4):
            w_ = wd[q % 2]
            load_w(C, w_, ins["w_down"][l], q * 512, 512, 0, nk=NFF)
            for t in range(NT):
                a_, xr_, xo_, p_ = at[n % 3], xr[n % 3], xo[n % 3], pp[n % 4]
                n += 1
                K.dma("sp", a_[:], a_scr[t], [a_buf], [a_.b], a_.b)
                K.dma("sp", xr_[:], x_in[t * 128:(t + 1) * 128, q * 512:(q + 1) * 512], [x_in_buf], [xr_.b], xr_.b)
                for j in range(NFF):
                    K.op("pe", lambda e, p_=p_, a_=a_, w_=w_, j=j: e.matmul(
                        p_[:], lhsT=a_[:, j, :], rhs=w_[:, j, :], start=(j == 0), stop=(j == NFF - 1)),
                         [a_.b, w_.b], [p_.b], sig=(j == NFF - 1))
                K.op("dve", lambda e, p_=p_, xr_=xr_, xo_=xo_: e.tensor_tensor(
                    out=xo_[:], in0=p_[:], in1=xr_[:], op=ALU.add), [p_.b, xr_.b], [xo_.b])
                K.dma("sp", x_out[t * 128:(t + 1) * 128, q * 512:(q + 1) * 512], xo_[:], [xo_.b], [x_out_buf], xo_.b)
        K.barrier()


def phase_inproj(C, l, x_in, x_in_buf, ins, consts, scr):
    K, nc = C.K, C.nc
    W = ins["w_in"][l]
    with ExitStack() as es:
        hT = C.sb(es, "hT", [128, KC, S], BF16, nbufs=4)
        wsl = [C.sb(es, f"win{i}", [128, KC, 512], BF16) for i in range(2)]
        wba = C.sb(es, "wba", [128, KC, 16], BF16)
        alloc_stage(C, es, 2)
        load_w(C, wsl[0], W, 0, 512, 0)
        load_w(C, wsl[1], W, 512, 512, 0)
        phase_norm(C, x_in, x_in_buf, ins["attn_norm"][l:l + 1, :], hT, consts)
        raw = [C.sb(es, f"raw{i}", [128, 3 + S], F32) for i in range(2)]
        acc = [C.sb(es, f"acc{i}", [128, S], F32) for i in range(2)]
        ysq = C.sb(es, "ysq", [128, S], BF16)
        lnv = C.sb(es, "lnv", [128, S], F32)
        ob = [C.sb(es, f"ob{i}", [128, S], BF16) for i in range(2)]
        svt = [C.sb(es, f"svt{i}", [128, 512], BF16) for i in range(2)]
        gcw = C.sb(es, "gcw", [128, 4, 24], F32)
        gcr = C.sb(es, "gcr", [24, 4, 128], F32)
        nrm = C.sb(es, "nrm", [128, 2], F32)
        hp = C.sb(es, "hp", [8, 4], F32)
        bg = C.sb(es, "bg", [8, 3, S], F32)
        rmask = C.sb(es, "rmask", [8, S], F32)
        pm = [C.ps(es, f"pm{i}", [128, 512], F32) for i in range(6)]
        po = [C.ps(es, f"po{i}", [128, 512], F32) for i in range(2)]
        ones_bf = consts["ones_bf"]
        for tap in range(4):
            K.dma("sp", gcr[:, tap, :], ins["gdn_conv"][l, tap, :].rearrange("(j p) -> j p", p=128), [], [gcr.b], gcr.b)
        for tap in range(4):
            K.op("pe", lambda e, tap=tap: e.transpose(out=pm[0][:, tap * 32:tap * 32 + 24], in_=gcr[:, tap, :],
                                                      identity=consts["identf"][0:24, 0:24]),
                 [gcr.b, consts["identf"].b], [pm[0].b], sig=(tap == 3))
        K.op("act", lambda e: e.copy(out=gcw[:], in_=pm[0][:, 0:128].rearrange("p (t n) -> p t n", n=32)[:, :, 0:24]),
             [pm[0].b], [gcw.b])
        K.dma("sp", nrm[:, 0:1], ins["sb_q_norm"][l, :].rearrange("(p o) -> p o", o=1), [], [nrm.b], nrm.b)
        K.dma("sp", nrm[:, 1:2], ins["sb_k_norm"][l, :].rearrange("(p o) -> p o", o=1), [], [nrm.b], nrm.b)
        K.dma("sp", hp[:, 0:1], ins["gdn_a_log"][l, :].rearrange("(p o) -> p o", o=1), [], [hp.b], hp.b)
        K.dma("sp", hp[:, 1:2], ins["gdn_dt_bias"][l, :].rearrange("(p o) -> p o", o=1), [], [hp.b], hp.b)
        K.op("act", lambda e: e.activation(out=hp[:, 2:3], in_=hp[:, 0:1], func=AF.Exp), [hp.b], [hp.b])
        K.op("dve", lambda e: e.tensor_scalar(out=hp[:, 2:3], in0=hp[:, 2:3], scalar1=-1.0, scalar2=None, op0=ALU.mult),
             [hp.b], [hp.b])
        K.op("pool", lambda e: e.memset(rmask[:], 1.0), [], [rmask.b])
        K.op("pool", lambda e: e.memset(rmask[:].rearrange("p (c n) -> p c n", n=128)[:, :, 0:1], 0.0), [rmask.b], [rmask.b])
        for r in raw:
            K.op("pool", lambda e, r=r: e.memset(r[:, 0:3], 0.0), [], [r.b])
        st = {"pi": 0, "ti": 0}

        def mm_tile(w_, co, M=128):
            ps = []
            for tc in range(4):
                p_ = pm[st["pi"] % 6]
                st["pi"] += 1
                for kc in range(KC):
                    K.op("pe", lambda e, p_=p_, w_=w_, kc=kc, co=co, tc=tc, M=M: e.matmul(
                        p_[0:M, :], lhsT=w_[:, kc, co:co + M], rhs=hT[:, kc, tc * 512:(tc + 1) * 512],
                        start=(kc == 0), stop=(kc == KC - 1)),
                         [w_.b, hT.bs[tc]], [p_.b], sig=(kc == KC - 1))
                ps.append(p_)
            return ps

        def l2_or_rms(src, out_, scale, bias, gain_ap):
            K.op("act", lambda e: e.activation(out=ysq[:], in_=src[:, 0:S], func=AF.Square), [src.b], [ysq.b])
            for tc in range(4):
                p_ = po[tc % 2]
                K.op("pe", lambda e, p_=p_, tc=tc: e.matmul(p_[:], lhsT=ones_bf[:], rhs=ysq[:, tc * 512:(tc + 1) * 512],
                                                            start=True, stop=True), [ones_bf.b, ysq.b], [p_.b])
                K.op("act", lambda e, p_=p_, tc=tc: e.activation(out=lnv[:, tc * 512:(tc + 1) * 512], in_=p_[:], func=AF.Ln,
                                                                 scale=scale, bias=consts["bias"][:, bias:bias + 1]),
                     [p_.b, consts["bias"].b], [lnv.b])
            K.op("act", lambda e: e.activation(out=lnv[:], in_=lnv[:], func=AF.Exp, scale=-0.5), [lnv.b], [lnv.b])
            if gain_ap is None:
                K.op("dve", lambda e: e.tensor_tensor(out=out_[:], in0=src[:, 0:S], in1=lnv[:], op=ALU.mult),
                     [src.b, lnv.b], [out_.b])
            else:
                K.op("dve", lambda e: e.scalar_tensor_tensor(out=out_[:], in0=src[:, 0:S], scalar=gain_ap, in1=lnv[:],
                                                             op0=ALU.mult, op1=ALU.mult), [src.b, lnv.b, nrm.b], [out_.b])

        groups = []
        for kind, c0 in [("gq", 0), ("gk", 1024), ("gv", 2048), ("gz", 3072), ("sq", 4112), ("sk", 5136), ("sv", 6160)]:
            groups += [(kind, c0, 0), (kind, c0 + 512, 1)]
        for gi, (kind, c0, half) in enumerate(groups):
            w_ = wsl[gi % 2]
            if gi >= 2:
                load_w(C, w_, W, c0, 512, 0)
            if kind == "gz" and half == 1:
                load_w(C, wba, W, 4096, 16, 0)
            if kind == "sv":
                for t in range(NT):
                    sv_ = svt[t % 2]
                    p_ = pm[st["pi"] % 6]
                    st["pi"] += 1
                    for kc in range(KC):
                        K.op("pe", lambda e, p_=p_, w_=w_, kc=kc, t=t: e.matmul(
                            p_[:], lhsT=hT[:, kc, t * 128:(t + 1) * 128], rhs=w_[:, kc, :],
                            start=(kc == 0), stop=(kc == KC - 1)),
                             [w_.b, hT.bs[t // 4]], [p_.b], sig=(kc == KC - 1))
                    K.op("act", lambda e, p_=p_, sv_=sv_: e.copy(out=sv_[:], in_=p_[:]), [p_.b], [sv_.b])
                    K.dma("sp", scr["sv"][t][:, half * 512:(half + 1) * 512], sv_[:], [sv_.b], [scr["sv_b"]], sv_.b)
                continue
            for hh in range(half * 4, half * 4 + 4):
                ti = st["ti"]
                st["ti"] += 1
                ps = mm_tile(w_, (hh % 4) * 128)
                r_, a_, o_ = raw[ti % 2], acc[ti % 2], ob[ti % 2]
                if kind in ("gq", "gk", "gv"):
                    ct = {"gq": 0, "gk": 8, "gv": 16}[kind] + hh
                    for tc in range(4):
                        K.op("act", lambda e, p_=ps[tc], r_=r_, tc=tc: e.copy(out=r_[:, 3 + tc * 512:3 + (tc + 1) * 512], in_=p_[:]),
                             [ps[tc].b], [r_.b])
                    K.op("act", lambda e, r_=r_, a_=a_, ct=ct: e.activation(out=a_[:], in_=r_[:, 3:3 + S], func=AF.Identity,
                                                                          scale=gcw[:, 3, ct:ct + 1]), [r_.b, gcw.b], [a_.b])
                    for tap in range(3):
                        K.op("dve", lambda e, r_=r_, a_=a_, ct=ct, tap=tap: e.scalar_tensor_tensor(
                            out=a_[:], in0=r_[:, tap:tap + S], scalar=gcw[:, tap, ct:ct + 1], in1=a_[:],
                            op0=ALU.mult, op1=ALU.add), [r_.b, gcw.b, a_.b], [a_.b])
                    if kind == "gv":
                        K.op("act", lambda e, a_=a_, o_=o_: e.activation(out=o_[:], in_=a_[:], func=AF.Silu), [a_.b], [o_.b])
                    else:
                        K.op("act", lambda e, a_=a_: e.activation(out=a_[:], in_=a_[:], func=AF.Silu), [a_.b], [a_.b])
                        if kind == "gq":
                            l2_or_rms(a_, o_, 128.0, 0, None)
                        else:
                            l2_or_rms(a_, o_, 1.0, 1, None)
                    K.dma("sp", scr[kind][hh], o_[:], [o_.b], [scr[kind + "_b"]], o_.b)
                elif kind == "gz":
                    for tc in range(4):
                        K.op("act", lambda e, p_=ps[tc], o_=o_, tc=tc: e.activation(
                            out=o_[:, tc * 512:(tc + 1) * 512], in_=p_[:], func=AF.Silu), [ps[tc].b], [o_.b])
                    K.dma("sp", scr["gz"][hh], o_[:], [o_.b], [scr["gz_b"]], o_.b)
                else:
                    for tc in range(4):
                        K.op("act", lambda e, p_=ps[tc], a_=a_, tc=tc: e.copy(out=a_[:, tc * 512:(tc + 1) * 512], in_=p_[:]),
                             [ps[tc].b], [a_.b])
                    if kind == "sq":
                        l2_or_rms(a_, o_, 1.0, 2, nrm[:, 0:1])
                    else:
                        l2_or_rms(a_, o_, 1.0 / 128.0, 3, nrm[:, 1:2])
                    K.dma("sp", scr[kind][hh], o_[:], [o_.b], [scr[kind + "_b"]], o_.b)
            if kind == "gz" and half == 1:
                for which in range(2):
                    ps = mm_tile(wba, which * 8, M=8)
                    for tc in range(4):
                        sl = slice(tc * 512, (tc + 1) * 512)
                        if which == 0:
                            K.op("act", lambda e, p_=ps[tc], sl=sl: e.activation(out=bg[:, 0, sl], in_=p_[0:8, :], func=AF.Sigmoid),
                                 [ps[tc].b], [bg.b])
                        else:
                            K.op("act", lambda e, p_=ps[tc], sl=sl: e.activation(out=bg[:, 1, sl], in_=p_[0:8, :], func=AF.Exp,
                                                                              bias=hp[:, 1:2]), [ps[tc].b, hp.b], [bg.b])
                K.op("act", lambda e: e.activation(out=bg[:, 1, :], in_=bg[:, 1, :], func=AF.Ln, bias=consts["bias"][0:8, 4:5]),
                     [bg.b, consts["bias"].b], [bg.b])
                K.op("dve", lambda e: e.tensor_scalar(out=bg[:, 1, :], in0=bg[:, 1, :], scalar1=hp[:, 2:3], scalar2=None,
                                                      op0=ALU.mult), [bg.b, hp.b], [bg.b])
                K.op("dve", lambda e: e.tensor_tensor_scan(out=bg[:, 2, :], data0=rmask[:], data1=bg[:, 1, :], initial=0.0,
                                                           op0=ALU.mult, op1=ALU.add), [bg.b, rmask.b], [bg.b])
                K.dma("sp", scr["beta"], bg[:, 0, :], [bg.b], [scr["beta_b"]], bg.b)
                K.dma("sp", scr["gcum"], bg[:, 2, :], [bg.b], [scr["gcum_b"]], bg.b)
        K.barrier()


GDN_STOP = 99


class _Stop(Exception):
    pass


_KREF = []


def _chk(k):
    if GDN_STOP <= k:
        _KREF[0].muted = True


def phase_gdn(C, l, ins, consts, scr):
    _KREF[:] = [C.K]
    _phase_gdn(C, l, ins, consts, scr)
    C.K.muted = False
    C.K.barrier()


def _phase_gdn(C, l, ins, consts, scr):
    K, nc = C.K, C.nc
    U32 = mybir.dt.uint32
    ident, onesf, identf, ones_bf = consts["ident"], consts["onesf"], consts["identf"], consts["ones_bf"]
    with ExitStack() as es:
        gT = C.sb(es, "gT", [8, S], F32)
        bT = C.sb(es, "bT", [8, S], F32)
        tk = C.sb(es, "tk", [128, 6, 128], F32)
        ogain = C.sb(es, "ogain", [128, 1], F32)
        masks = C.sb(es, "masks", [128, 14, 128], F32)
        masks8 = C.sb(es, "masks8", [128, 14, 512], mybir.dt.uint8)
        sel = C.sb(es, "sel", [128, 128], F32)
        zer = C.sb(es, "zer", [128, 128], F32)
        mneg = C.sb(es, "mneg", [128, 128], F32)
        B = [C.ps(es, f"B{i}", [128, 512], F32) for i in range(8)]
        K.dma("sp", gT[:], scr["gcum"], [scr["gcum_b"]], [gT.b], gT.b)
        K.dma("sp", bT[:], scr["beta"], [scr["beta_b"]], [bT.b], bT.b)
        K.dma("sp", ogain[:], ins["gdn_o_norm"][l, :].rearrange("(p o) -> p o", o=1), [], [ogain.b], ogain.b)
        K.op("pool", lambda e: e.memset(zer[:], 0.0), [], [zer.b])
        K.op("pool", lambda e: e.affine_select(out=mneg[:], in_=zer[:], pattern=[[1, 128]], compare_op=ALU.is_ge,
                                               fill=-30000.0, base=0, channel_multiplier=-1), [zer.b], [mneg.b])
        K.op("pool", lambda e: e.affine_select(out=sel[:], in_=onesf[:], pattern=[[0, 128]], compare_op=ALU.is_ge,
                                               fill=0.0, base=-127, channel_multiplier=1), [onesf.b], [sel.b])
        for lv in range(7):
            n = 1 << lv
            nb = 128 // (2 * n)
            specs = [
                (lv, [(-n, 1, [[-2 * n, nb], [0, 2], [0, n]]), (2 * n - 1, -1, [[2 * n, nb], [0, 2], [0, n]]),
                      (0, 0, [[0, nb], [-1, 2], [0, n]])]),
                (7 + lv, [(0, 1, [[-2 * n, nb], [0, 2], [0, n]]), (n - 1, -1, [[2 * n, nb], [0, 2], [0, n]]),
                          (-1, 0, [[0, nb], [1, 2], [0, n]])]),
            ]
            for mi, passes in specs:
                for pi, (base, cm, pat) in enumerate(passes):
                    src = onesf if pi == 0 else masks
                    K.op("pool", lambda e, mi=mi, base=base, cm=cm, pat=pat, pi=pi: e.affine_select(
                        out=masks[:, mi, :], in_=(onesf[:] if pi == 0 else masks[:, mi, :]), pattern=pat,
                        compare_op=ALU.is_ge, fill=0.0, base=base, channel_multiplier=cm),
                         [src.b], [masks.b])
        K.op("dve", lambda e: e.tensor_copy(out=masks8[:].rearrange("p m (r n) -> p m r n", n=128),
                                            in_=masks[:].unsqueeze(2).to_broadcast([128, 14, 4, 128])), [masks.b], [masks8.b])
        for which, srcT in ((0, gT), (1, bT)):
            for c in range(NT):
                K.op("pe", lambda e, which=which, srcT=srcT, c=c: e.transpose(
                    out=B[which][:, c * 8:c * 8 + 8], in_=srcT[:, c * 128:(c + 1) * 128],
                    identity=identf[0:8, 0:8]), [srcT.b, identf.b], [B[which].b], sig=(c == NT - 1))
            K.op("act", lambda e, which=which: e.copy(out=tk[:, which, :], in_=B[which][:, 0:128]),
                 [B[which].b], [tk.b])
        K.op("dve", lambda e: e.tensor_scalar(out=tk[:, 2, :], in0=tk[:, 1, :], scalar1=-1.0, scalar2=None, op0=ALU.mult),
             [tk.b], [tk.b])
        K.op("act", lambda e: e.activation(out=tk[:, 3, :], in_=tk[:, 0, :], func=AF.Exp), [tk.b], [tk.b])
        K.op("pe", lambda e: e.matmul(B[2][:, 0:128], lhsT=sel[:], rhs=tk[:, 0, :], start=True, stop=True),
             [sel.b, tk.b], [B[2].b])
        K.op("act", lambda e: e.activation(out=tk[:, 5, :], in_=B[2][:, 0:128], func=AF.Exp), [B[2].b], [tk.b])
        K.op("dve", lambda e: e.tensor_tensor(out=tk[:, 4, :], in0=B[2][:, 0:128], in1=tk[:, 0, :], op=ALU.subtract),
             [B[2].b, tk.b], [tk.b])
        K.op("act", lambda e: e.activation(out=tk[:, 4, :], in_=tk[:, 4, :], func=AF.Exp), [tk.b], [tk.b])
        _chk(1)

        def col(which, h):
            return tk[:, which, :].rearrange("p (c h) -> p c h", h=8)[:, :, h:h + 1]

        def v3(ap):
            return ap.rearrange("p (c n) -> p c n", n=128)

        with ExitStack() as eg:
            slots = []
            for hi in range(4):
                sl = {nm: C.sb(eg, f"{nm}{hi}", [128, S], BF16, nbufs=16) for nm in ("wT", "ub", "kd", "qg", "AT", "oT")}
                sl["S32"] = C.sb(eg, f"S32{hi}", [128, 128], F32)
                sl["Sbf"] = C.sb(eg, f"Sbf{hi}", [128, 128], BF16)
                sl["vn"] = [C.sb(eg, f"vn{hi}_{i}", [128, 128], BF16) for i in range(2)]
                slots.append(sl)
            for grp in range(2):
                heads = list(range(grp * 4, grp * 4 + 4))
                per = {h: slots[h % 4] for h in heads}
                if grp == 0:
                  qT = C.sb(eg, "qT", [128, S], BF16)
                  kT = C.sb(eg, "kT", [128, S], BF16)
                  vT = C.sb(eg, "vT", [128, S], BF16)
                  Rb = C.sb(eg, "Rb", [128, S], F32)
                  E = C.sb(eg, "E", [128, S], F32)
                  dU = C.sb(eg, "dU", [128, S], BF16)
                  LnT = C.sb(eg, "LnT", [128, S], BF16, nbufs=4)
                  Dm = C.sb(eg, "Dm", [128, S], BF16, nbufs=4)
                  DTm = C.sb(eg, "DTm", [128, S], BF16, nbufs=4)
                  Ysb = C.sb(eg, "Ysb", [128, S], BF16, nbufs=4)
                  kg = C.sb(eg, "kg", [128, S], BF16)
                  vtok = C.sb(eg, "vtok", [128, S], BF16)
                  zs = C.sb(eg, "zs", [128, S], BF16)
                for h in heads:
                    P = per[h]
                    K.dma("sp", qT[:], scr["gq"][h], [scr["gq_b"]], [qT.b], qT.b)
                    K.dma("sp", kT[:], scr["gk"][h], [scr["gk_b"]], [kT.b], kT.b)
                    K.dma("sp", vT[:], scr["gv"][h], [scr["gv_b"]], [vT.b], vT.b)
                    K.dma("sp", Rb[:], scr["gcum"][h, :].partition_broadcast(128), [scr["gcum_b"]], [Rb.b], Rb.b)
                    K.op("act", lambda e: e.activation(out=E[:], in_=Rb[:], func=AF.Exp), [Rb.b], [E.b])
                    K.op("dve", lambda e, P=P: e.tensor_tensor(out=P["qg"][:], in0=qT[:], in1=E[:], op=ALU.mult),
                         [qT.b, E.b], P["qg"].bs)
                    K.op("dve", lambda e, h=h: e.tensor_tensor(out=v3(E[:]), in0=v3(Rb[:]),
                                                               in1=col(0, h).to_broadcast([128, NT, 128]), op=ALU.subtract),
                         [Rb.b, tk.b], [E.b])
                    K.op("dve", lambda e: e.tensor_tensor(out=v3(E[:]), in0=v3(E[:]),
                                                          in1=mneg[:].unsqueeze(1).to_broadcast([128, NT, 128]), op=ALU.add),
                         [E.b, mneg.b], [E.b])
                    K.op("act", lambda e: e.activation(out=dU[:], in_=E[:], func=AF.Exp), [E.b], [dU.b])
                    for q in range(4):
                        for i in range(4):
                            cs = slice((q * 4 + i) * 128, (q * 4 + i + 1) * 128)
                            K.op("pe", lambda e, q=q, i=i, cs=cs: e.matmul(
                                B[q][:, i * 128:(i + 1) * 128], lhsT=kT[:, cs], rhs=kT[:, cs], start=True, stop=True),
                                 [kT.b], [B[q].b], sig=(i == 3))
                        for i in range(4):
                            c = q * 4 + i
                            cs = slice(c * 128, (c + 1) * 128)
                            K.op("dve", lambda e, q=q, i=i, cs=cs, c=c, h=h: e.scalar_tensor_tensor(
                                out=LnT[:, cs], in0=B[q][:, i * 128:(i + 1) * 128],
                                scalar=tk[:, 2, c * 8 + h:c * 8 + h + 1], in1=dU[:, cs], op0=ALU.mult, op1=ALU.mult),
                                 [B[q].b, tk.b, dU.b], [LnT.bs[q]])
                    for q in range(4):
                        for i in range(4):
                            cs = slice((q * 4 + i) * 128, (q * 4 + i + 1) * 128)
                            K.op("pe", lambda e, q=q, i=i, cs=cs: e.matmul(
                                B[4 + q][:, i * 128:(i + 1) * 128], lhsT=kT[:, cs], rhs=qT[:, cs], start=True, stop=True),
                                 [kT.b, qT.b], [B[4 + q].b], sig=(i == 3))
                        qs = slice(q * 512, (q + 1) * 512)
                        K.op("dve", lambda e, q=q, qs=qs, P=P: e.tensor_tensor(
                            out=P["AT"][:, qs], in0=B[4 + q][:], in1=dU[:, qs], op=ALU.mult),
                             [B[4 + q].b, dU.b], P["AT"].bs[q * 4:q * 4 + 4])
                    _chk(2)
                    for src, b0 in ((kT, 0), (vT, 2)):
                        for hb in range(2):
                            pv = B[b0 + hb][:].bitcast(BF16)
                            for i in range(8):
                                cs = slice((hb * 8 + i) * 128, (hb * 8 + i + 1) * 128)
                                K.op("pe", lambda e, pv=pv, i=i, cs=cs, src=src: e.transpose(
                                    out=pv[:, i * 128:(i + 1) * 128], in_=src[:, cs], identity=ident[:]),
                                     [src.b, ident.b], [B[b0 + hb].b], sig=(i == 7))
                    for hb in range(2):
                        hs_ = slice(hb * 1024, (hb + 1) * 1024)
                        pk = B[hb][:].bitcast(BF16)
                        pvv = B[2 + hb][:].bitcast(BF16)
                        K.op("dve", lambda e, h=h, hb=hb, hs_=hs_, pk=pk: e.tensor_tensor(
                            out=v3(kg[:, hs_]), in0=v3(pk), in1=col(3, h)[:, hb * 8:(hb + 1) * 8, :].to_broadcast([128, 8, 128]),
                            op=ALU.mult), [B[hb].b, tk.b], [kg.b])
                        K.op("dve", lambda e, h=h, hb=hb, hs_=hs_, pk=pk, P=P: e.tensor_tensor(
                            out=v3(P["kd"][:, hs_]), in0=v3(pk), in1=col(4, h)[:, hb * 8:(hb + 1) * 8, :].to_broadcast([128, 8, 128]),
                            op=ALU.mult), [B[hb].b, tk.b], P["kd"].bs[hb * 8:(hb + 1) * 8])
                        K.op("act", lambda e, hs_=hs_, pvv=pvv: e.copy(out=vtok[:, hs_], in_=pvv), [B[2 + hb].b], [vtok.b])
                    _chk(3)
                    K.op("dve", lambda e: e.tensor_copy(out=v3(Dm[:]), in_=ident[:].unsqueeze(1).to_broadcast([128, NT, 128])),
                         [ident.b], Dm.bs)
                    K.op("dve", lambda e: e.tensor_copy(out=v3(DTm[:]), in_=ident[:].unsqueeze(1).to_broadcast([128, NT, 128])),
                         [ident.b], DTm.bs)
                    for lv in range(7):
                        for q in range(4):
                            s_ = q % 2
                            qs = slice(q * 512, (q + 1) * 512)
                            Yp, Zp, ZTp = B[s_ * 3], B[s_ * 3 + 1], B[s_ * 3 + 2]
                            for i in range(4):
                                cs = slice(q * 512 + i * 128, q * 512 + (i + 1) * 128)
                                K.op("pe", lambda e, cs=cs, i=i, Yp=Yp: e.matmul(
                                    Yp[:, i * 128:(i + 1) * 128], lhsT=LnT[:, cs], rhs=Dm[:, cs],
                                    start=True, stop=True), [LnT.bs[q], Dm.bs[q]], [Yp.b], sig=(i == 3))
                            K.op("act", lambda e, qs=qs, Yp=Yp: e.copy(out=Ysb[:, qs], in_=Yp[:]), [Yp.b], [Ysb.bs[q]])
                            for i in range(4):
                                cs = slice(q * 512 + i * 128, q * 512 + (i + 1) * 128)
                                K.op("pe", lambda e, cs=cs, i=i, Zp=Zp: e.matmul(
                                    Zp[:, i * 128:(i + 1) * 128], lhsT=DTm[:, cs], rhs=Ysb[:, cs],
                                    start=True, stop=True), [DTm.bs[q], Ysb.bs[q]], [Zp.b], sig=(i == 3))
                            for i in range(4):
                                cs = slice(q * 512 + i * 128, q * 512 + (i + 1) * 128)
                                K.op("pe", lambda e, cs=cs, i=i, ZTp=ZTp: e.matmul(
                                    ZTp[:, i * 128:(i + 1) * 128], lhsT=Ysb[:, cs], rhs=DTm[:, cs],
                                    start=True, stop=True), [DTm.bs[q], Ysb.bs[q]], [ZTp.b], sig=(i == 3))
                            K.op("dve", lambda e, qs=qs, lv=lv, Zp=Zp: e.copy_predicated(
                                out=Dm[:, qs], mask=masks8[:, lv, :], data=Zp[:]), [Zp.b, masks8.b], [Dm.bs[q]])
                            K.op("dve", lambda e, qs=qs, lv=lv, ZTp=ZTp: e.copy_predicated(
                                out=DTm[:, qs], mask=masks8[:, 7 + lv, :], data=ZTp[:]), [ZTp.b, masks8.b], [DTm.bs[q]])
                    _chk(4)
                    for q in range(4):
                        qs = slice(q * 512, (q + 1) * 512)
                        for i in range(4):
                            cs = slice((q * 4 + i) * 128, (q * 4 + i + 1) * 128)
                            K.op("pe", lambda e, q=q, i=i, cs=cs: e.matmul(
                                B[q][:, i * 128:(i + 1) * 128], lhsT=DTm[:, cs], rhs=vtok[:, cs], start=True, stop=True),
                                 [DTm.bs[q], vtok.b], [B[q].b], sig=(i == 3))
                        K.op("dve", lambda e, q=q, qs=qs, P=P, h=h: e.tensor_tensor(
                            out=v3(P["ub"][:, qs]), in0=v3(B[q][:]),
                            in1=col(1, h)[:, q * 4:(q + 1) * 4, :].to_broadcast([128, 4, 128]), op=ALU.mult),
                             [B[q].b, tk.b], P["ub"].bs[q * 4:q * 4 + 4])
                    for q in range(4):
                        qs = slice(q * 512, (q + 1) * 512)
                        for i in range(4):
                            cs = slice((q * 4 + i) * 128, (q * 4 + i + 1) * 128)
                            K.op("pe", lambda e, q=q, i=i, cs=cs: e.matmul(
                                B[4 + q][:, i * 128:(i + 1) * 128], lhsT=kg[:, cs], rhs=DTm[:, cs], start=True, stop=True),
                                 [DTm.bs[q], kg.b], [B[4 + q].b], sig=(i == 3))
                        K.op("act", lambda e, q=q, qs=qs, P=P: e.copy(out=P["wT"][:, qs], in_=B[4 + q][:]),
                             [B[4 + q].b], P["wT"].bs[q * 4:q * 4 + 4])
                    K.op("pool", lambda e, P=P: e.memset(P["S32"][:], 0.0), [], [P["S32"].b])
                    K.op("pool", lambda e, P=P: e.memset(P["Sbf"][:], 0.0), [], [P["Sbf"].b])
                _chk(5)
                for c in range(NT):
                    cs = slice(c * 128, (c + 1) * 128)
                    par = (c % 2) * 3
                    Bw, Bo, Bs = B[par], B[par + 1], B[par + 2]
                    for hi, h in enumerate(heads):
                        P = per[h]
                        ss = slice(hi * 128, (hi + 1) * 128)
                        K.op("pe", lambda e, P=P, ss=ss, cs=cs, Bw=Bw: e.matmul(Bw[:, ss], lhsT=P["wT"][:, cs], rhs=P["Sbf"][:],
                                                                               start=True, stop=True),
                             [P["wT"].bs[c], P["Sbf"].b], [Bw.b], sig=(hi == 3))
                    for hi, h in enumerate(heads):
                        P = per[h]
                        ss = slice(hi * 128, (hi + 1) * 128)
                        idx = c * 8 + h
                        vn = P["vn"][c % 2]
                        K.op("dve", lambda e, P=P, ss=ss, cs=cs, idx=idx, vn=vn, Bw=Bw: e.scalar_tensor_tensor(
                            out=vn[:], in0=Bw[:, ss], scalar=tk[:, 2, idx:idx + 1], in1=P["ub"][:, cs],
                            op0=ALU.mult, op1=ALU.add), [Bw.b, tk.b, P["ub"].bs[c]], [vn.b])
                    for hi, h in enumerate(heads):
                        P = per[h]
                        ss = slice(hi * 128, (hi + 1) * 128)
                        vn = P["vn"][c % 2]
                        K.op("pe", lambda e, P=P, ss=ss, cs=cs, Bo=Bo: e.matmul(Bo[:, ss], lhsT=P["Sbf"][:], rhs=P["qg"][:, cs],
                                                                               start=True, stop=False),
                             [P["Sbf"].b, P["qg"].bs[c]], [Bo.b], sig=False)
                        K.op("pe", lambda e, P=P, ss=ss, cs=cs, vn=vn, Bo=Bo: e.matmul(Bo[:, ss], lhsT=vn[:], rhs=P["AT"][:, cs],
                                                                                      start=False, stop=True),
                             [vn.b, P["AT"].bs[c]], [Bo.b], sig=(hi == 3))
                    for hi, h in enumerate(heads):
                        P = per[h]
                        ss = slice(hi * 128, (hi + 1) * 128)
                        vn = P["vn"][c % 2]
                        K.op("pe", lambda e, P=P, ss=ss, cs=cs, vn=vn, Bs=Bs: e.matmul(Bs[:, ss], lhsT=P["kd"][:, cs], rhs=vn[:],
                                                                                      start=True, stop=True),
                             [vn.b, P["kd"].bs[c]], [Bs.b], sig=(hi == 3))
                    for hi, h in enumerate(heads):
                        P = per[h]
                        ss = slice(hi * 128, (hi + 1) * 128)
                        idx = c * 8 + h
                        K.op("dve", lambda e, P=P, ss=ss, idx=idx, Bs=Bs: e.scalar_tensor_tensor(
                            out=P["S32"][:], in0=P["S32"][:], scalar=tk[:, 5, idx:idx + 1], in1=Bs[:, ss],
                            op0=ALU.mult, op1=ALU.add), [P["S32"].b, tk.b, Bs.b], [P["S32"].b])
                        K.op("act", lambda e, P=P: e.copy(out=P["Sbf"][:], in_=P["S32"][:]), [P["S32"].b], [P["Sbf"].b])
                    for hi, h in enumerate(heads):
                        P = per[h]
                        ss = slice(hi * 128, (hi + 1) * 128)
                        K.op("act", lambda e, P=P, ss=ss, cs=cs, Bo=Bo: e.copy(out=P["oT"][:, cs], in_=Bo[:, ss]),
                             [Bo.b], [P["oT"].bs[c]])
                _chk(6)
                for h in heads:
                    P = per[h]
                    K.dma("sp", zs[:], scr["gz"][h], [scr["gz_b"]], [zs.b], zs.b)
                    K.op("act", lambda e, P=P: e.activation(out=dU[:], in_=P["oT"][:], func=AF.Square), P["oT"].bs, [dU.b])
                    for tc in range(4):
                        ts_ = slice(tc * 512, (tc + 1) * 512)
                        Bn = B[6 + tc % 2]
                        K.op("pe", lambda e, ts_=ts_, Bn=Bn: e.matmul(Bn[:], lhsT=ones_bf[:], rhs=dU[:, ts_], start=True, stop=True),
                             [ones_bf.b, dU.b], [Bn.b])
                        K.op("act", lambda e, ts_=ts_, Bn=Bn: e.activation(
                            out=E[:, ts_], in_=Bn[:], func=AF.Ln, scale=1.0 / 128.0,
                            bias=consts["bias"][:, 3:4]), [Bn.b, consts["bias"].b], [E.b])
                    K.op("act", lambda e: e.activation(out=E[:], in_=E[:], func=AF.Exp, scale=-0.5), [E.b], [E.b])
                    K.op("dve", lambda e, P=P: e.scalar_tensor_tensor(out=LnT[:], in0=P["oT"][:], scalar=ogain[:, 0:1], in1=E[:],
                                                                      op0=ALU.mult, op1=ALU.mult),
                         P["oT"].bs + [ogain.b, E.b], LnT.bs)
                    K.op("dve", lambda e: e.tensor_tensor(out=Dm[:], in0=LnT[:], in1=zs[:], op=ALU.mult),
                         LnT.bs + [zs.b], Dm.bs)
                    K.dma("sp", scr["mix"][h], Dm[:], Dm.bs, [scr["mix_b"]], Dm.bs[0])
                K.barrier()


def phase_sb(C, l, ins, consts, scr):
    K, nc = C.K, C.nc
    ones_bf = consts["ones_bf"]
    with ExitStack() as es:
        qT = [C.sb(es, f"sqT{i}", [128, S], BF16) for i in range(2)]
        kT = [C.sb(es, f"skT{i}", [128, S], BF16) for i in range(2)]
        vt = [C.sb(es, f"svt{i}", [128, NT, 128], BF16) for i in range(2)]
        e1 = [C.sb(es, f"e1_{i}", [128, 512], F32) for i in range(2)]
        sp = [C.sb(es, f"sp{i}", [128, 512], BF16) for i in range(2)]
        tmp = [C.sb(es, f"tmp{i}", [128, 512], F32) for i in range(2)]
        aa = [C.sb(es, f"aa{i}", [128, 512], F32) for i in range(2)]
        ww = [C.sb(es, f"ww{i}", [128, 512], BF16) for i in range(2)]
        Rp = C.sb(es, "Rp", [128, 512], F32)
        yraw = [C.sb(es, f"yraw{i}", [128, S], F32) for i in range(2)]
        ysq = C.sb(es, "ysq", [128, S], BF16)
        rn = C.sb(es, "rn", [128, S], F32)
        ymx = [C.sb(es, f"ymx{i}", [128, S], BF16) for i in range(2)]
        msk = C.sb(es, "msk", [128, 4, 512], BF16)
        uin = C.sb(es, "uin", [128, 128], BF16)
        og = C.sb(es, "og", [128, 1], F32)
        Bz = [C.ps(es, f"Bz{i}", [128, 512], F32) for i in range(2)]
        Bg = [C.ps(es, f"Bg{i}", [128, 512], F32) for i in range(2)]
        Bt = [C.ps(es, f"Bt{i}", [128, 512], F32) for i in range(2)]
        Bo = [C.ps(es, f"Bo{i}", [128, 512], F32) for i in range(2)]
        onesw = C.sb(es, "onesw", [128, 512], F32)
        K.op("pool", lambda e: e.memset(onesw[:], 1.0), [], [onesw.b])
        for k in range(4):
            K.op("pool", lambda e, k=k: e.affine_select(out=msk[:, k, :], in_=onesw[:], pattern=[[1, 512]],
                                                        compare_op=ALU.is_gt, fill=0.0, base=-128 * k, channel_multiplier=-1),
                 [onesw.b], [msk.b])
        K.op("pool", lambda e: e.affine_select(out=uin[:], in_=onesw[:, 0:128], pattern=[[-1, 128]], compare_op=ALU.is_ge,
                                               fill=0.0, base=0, channel_multiplier=1), [onesw.b], [uin.b])
        K.dma("sp", og[:], ins["sb_o_norm"][l, :].rearrange("(p o) -> p o", o=1), [], [og.b], og.b)

        blocks = []
        for h in range(H):
            for tc in range(4):
                nb = 4 * tc + 4
                for b in range(nb - 1, -1, -1):
                    blocks.append((h, tc, b, b == nb - 1, b == 0))

        def load(h):
            i = h % 2
            K.dma("sp", qT[i][:], scr["sq"][h], [scr["sq_b"]], [qT[i].b], qT[i].b)
            K.dma("sp", kT[i][:], scr["sk"][h], [scr["sk_b"]], [kT[i].b], kT[i].b)
            K.dma("sp", vt[i][:], scr["sv"][:, :, h * 128:(h + 1) * 128].rearrange("t p d -> p t d"),
                  [scr["sv_b"]], [vt[i].b], vt[i].b)

        def stageA(n):
            h, tc, b, first, last = blocks[n]
            i, j = h % 2, n % 2
            ts_ = slice(tc * 512, (tc + 1) * 512)
            K.op("pe", lambda e: e.matmul(Bz[j][:], lhsT=kT[i][:, b * 128:(b + 1) * 128], rhs=qT[i][:, ts_],
                                          start=True, stop=True), [kT[i].b, qT[i].b], [Bz[j].b])
            K.op("act", lambda e: e.activation(out=e1[j][:], in_=Bz[j][:], func=AF.Exp), [Bz[j].b], [e1[j].b])
            K.op("act", lambda e: e.activation(out=sp[j][:], in_=e1[j][:], func=AF.Ln, bias=consts["bias"][:, 4:5]),
                 [e1[j].b, consts["bias"].b], [sp[j].b])
            k = b - 4 * tc
            if k >= 0:
                K.op("pool", lambda e: e.tensor_tensor(out=sp[j][:], in0=sp[j][:], in1=msk[:, k, :], op=ALU.mult),
                     [sp[j].b, msk.b], [sp[j].b])
            K.op("pe", lambda e: e.matmul(Bg[j][:], lhsT=uin[:], rhs=sp[j][:], start=True, stop=True),
                 [uin.b, sp[j].b], [Bg[j].b])
            K.op("pe", lambda e: e.matmul(Bt[j][:], lhsT=ones_bf[:], rhs=sp[j][:], start=True, stop=True),
                 [ones_bf.b, sp[j].b], [Bt[j].b])

        def stageB(n):
            h, tc, b, first, last = blocks[n]
            i, j = h % 2, n % 2
            o_ = Bo[(h * 4 + tc) % 2]
            if first:
                K.op("pool", lambda e: e.memset(Rp[:], 0.0), [], [Rp.b])
            K.op("dve", lambda e: e.tensor_tensor(out=tmp[j][:], in0=Bg[j][:], in1=Rp[:], op=ALU.add),
                 [Bg[j].b, Rp.b], [tmp[j].b])
            K.op("dve", lambda e: e.tensor_tensor(out=aa[j][:], in0=Bz[j][:], in1=tmp[j][:], op=ALU.subtract),
                 [Bz[j].b, tmp[j].b], [aa[j].b])
            K.op("act", lambda e: e.activation(out=ww[j][:], in_=aa[j][:], func=AF.Exp), [aa[j].b], [ww[j].b])
            k = b - 4 * tc
            if k >= 0:
                K.op("pool", lambda e: e.tensor_tensor(out=ww[j][:], in0=ww[j][:], in1=msk[:, k, :], op=ALU.mult),
                     [ww[j].b, msk.b], [ww[j].b])
            if not last:
                K.op("dve", lambda e: e.tensor_tensor(out=Rp[:], in0=Rp[:], in1=Bt[j][:], op=ALU.add),
                     [Rp.b, Bt[j].b], [Rp.b])
            K.op("pe", lambda e: e.matmul(o_[:], lhsT=vt[i][:, b, :], rhs=ww[j][:], start=first, stop=last),
                 [vt[i].b, ww[j].b], [o_.b], sig=last)
            if last:
                y_ = yraw[h % 2]
                K.op("act", lambda e: e.copy(out=y_[:, tc * 512:(tc + 1) * 512], in_=o_[:]), [o_.b], [y_.b])
                if tc == 3:
                    finish(h)

        def finish(h):
            y_, m_ = yraw[h % 2], ymx[h % 2]
            K.op("act", lambda e: e.activation(out=ysq[:], in_=y_[:], func=AF.Square), [y_.b], [ysq.b])
            for tc in range(4):
                ts_ = slice(tc * 512, (tc + 1) * 512)
                p_ = Bo[tc % 2]
                K.op("pe", lambda e, p_=p_, ts_=ts_: e.matmul(p_[:], lhsT=ones_bf[:], rhs=ysq[:, ts_], start=True, stop=True),
                     [ones_bf.b, ysq.b], [p_.b])
                K.op("act", lambda e, p_=p_, ts_=ts_: e.activation(out=rn[:, ts_], in_=p_[:], func=AF.Ln, scale=1.0 / 128.0,
                                                                   bias=consts["bias"][:, 3:4]), [p_.b, consts["bias"].b], [rn.b])
            K.op("act", lambda e: e.activation(out=rn[:], in_=rn[:], func=AF.Exp, scale=-0.5), [rn.b], [rn.b])
            K.op("dve", lambda e: e.scalar_tensor_tensor(out=m_[:], in0=y_[:], scalar=og[:, 0:1], in1=rn[:],
                                                         op0=ALU.mult, op1=ALU.mult), [y_.b, og.b, rn.b], [m_.b])
            K.dma("sp", scr["mix"][8 + h], m_[:], [m_.b], [scr["mix_b"]], m_.b)

        load(0)
        for n in range(len(blocks)):
            h, tc, b, first, last = blocks[n]
            stageA(n)
            if n > 0:
                stageB(n - 1)
            if first and tc == 0 and h + 1 < H:
                load(h + 1)
        stageB(len(blocks) - 1)
        K.barrier()


def phase_outproj(C, l, x_in, x_in_buf, x_out, x_out_buf, ins, consts, scr):
    K, nc = C.K, C.nc
    with ExitStack() as es:
        mx = C.sb(es, "mx", [128, KC, S], BF16)
        wo = C.sb(es, "wo", [128, KC, D], BF16, nbufs=4)
        xt = [C.sb(es, f"xo_in{i}", [128, D], F32) for i in range(2)]
        xo = [C.sb(es, f"xo_out{i}", [128, D], F32) for i in range(2)]
        pp = [C.ps(es, f"po{i}", [128, 512], F32) for i in range(8)]
        for kc in range(KC):
            K.dma("sp", mx[:, kc, :], scr["mix"][kc], [scr["mix_b"]], [mx.b], mx.b, grp=True)
        alloc_stage(C, es)
        for cq in range(4):
            load_w(C, wo, ins["w_out"][l], cq * 512, 512, cq * 512, wbuf=wo.bs[cq])
        n = 0
        for t in range(NT):
            x_, o_ = xt[t % 2], xo[t % 2]
            K.dma("sp", x_[:], x_in[t * 128:(t + 1) * 128, :], [x_in_buf], [x_.b], x_.b)
            for cq in range(4):
                p_ = pp[n % 8]
                n += 1
                for kc in range(KC):
                    K.op("pe", lambda e, p_=p_, kc=kc, t=t, cq=cq: e.matmul(
                        p_[:], lhsT=mx[:, kc, t * 128:(t + 1) * 128], rhs=wo[:, kc, cq * 512:(cq + 1) * 512],
                        start=(kc == 0), stop=(kc == KC - 1)), [mx.b, wo.bs[cq]], [p_.b], sig=(kc == KC - 1))
                K.op("dve", lambda e, p_=p_, x_=x_, o_=o_, cq=cq: e.tensor_tensor(
                    out=o_[:, cq * 512:(cq + 1) * 512], in0=p_[:], in1=x_[:, cq * 512:(cq + 1) * 512], op=ALU.add),
                     [p_.b, x_.b], [o_.b])
            K.dma("sp", x_out[t * 128:(t + 1) * 128, :], o_[:], [o_.b], [x_out_buf], o_.b)
        K.barrier()


def make_consts(C, es):
    K, nc = C.K, C.nc
    c = {}
    c["neghalf"] = C.sb(es, "neghalf", [128, 1], F32)
    K.op("pool", lambda e: e.memset(c["neghalf"][:], -0.5), [], [c["neghalf"].b])
    onesf = C.sb(es, "onesf", [128, 128], F32)
    K.op("pool", lambda e: e.memset(onesf[:], 1.0), [], [onesf.b])
    identf = C.sb(es, "identf", [128, 128], F32)
    K.op("pool", lambda e: e.affine_select(out=identf[:], in_=onesf[:], pattern=[[1, 128]],
                                           compare_op=ALU.is_equal, fill=0.0, base=0, channel_multiplier=-1),
         [onesf.b], [identf.b])
    c["ident"] = C.sb(es, "ident", [128, 128], BF16)
    K.op("pool", lambda e: e.tensor_copy(out=c["ident"][:], in_=identf[:]), [identf.b], [c["ident"].b])
    c["onesf"] = onesf
    c["identf"] = identf
    c["ones_bf"] = C.sb(es, "ones_bf", [128, 128], BF16)
    K.op("pool", lambda e: e.memset(c["ones_bf"][:], 1.0), [], [c["ones_bf"].b])
    c["bias"] = C.sb(es, "biasc", [128, 8], F32)
    for i, v in enumerate([128.0 * L2_EPS, L2_EPS, 128.0 * RMS_EPS, RMS_EPS, 1.0]):
        K.op("pool", lambda e, i=i, v=v: e.memset(c["bias"][:, i:i + 1], v), [], [c["bias"].b])
    return c


PARAMS = [("attn_norm", [DEPTH, D]), ("w_in", [DEPTH, D, IN_COLS]), ("gdn_conv", [DEPTH, 4, 3072]),
          ("gdn_a_log", [DEPTH, H]), ("gdn_dt_bias", [DEPTH, H]), ("gdn_o_norm", [DEPTH, HD]),
          ("sb_q_norm", [DEPTH, HD]), ("sb_k_norm", [DEPTH, HD]), ("sb_o_norm", [DEPTH, HD]),
          ("w_out", [DEPTH, D, D]), ("ffn_norm", [DEPTH, D]), ("w_up", [DEPTH, D, 2 * D_FF]),
          ("ffn_conv", [DEPTH, 3, 2 * D_FF]), ("ffn_conv_bias", [DEPTH, 2 * D_FF]),
          ("w_down", [DEPTH, D_FF, D])]


def build(mode="full"):
    nc = bass.Bass("TRN2", target_bir_lowering=False)
    only = mode.startswith("only")
    small = {"attn_norm", "gdn_conv", "gdn_a_log", "gdn_dt_bias", "gdn_o_norm", "sb_q_norm", "sb_k_norm", "sb_o_norm",
             "ffn_norm", "ffn_conv", "ffn_conv_bias"}
    ins = {"x": nc.dram_tensor("x", [S, D], F32, kind="ExternalInput").ap()}
    for name, shape in PARAMS:
        if only and name not in small:
            continue
        ins[name] = nc.dram_tensor(name, shape, F32, kind="ExternalInput").ap()
    y = nc.dram_tensor("y", [S, D], F32, kind="ExternalOutput").ap()
    dbg = "ExternalOutput" if mode.startswith("dbg") else "Internal"
    sin = "ExternalInput" if only else dbg
    scratch = {}
    for nm in ("gq", "gk", "gv", "gz", "sq", "sk"):
        scratch[nm] = nc.dram_tensor(nm + "_scr", [H, 128, S], BF16, kind=sin).ap()
        scratch[nm + "_b"] = Buf(nm + "_scr")
    scratch["sv"] = nc.dram_tensor("sv_scr", [NT, 128, 1024], BF16, kind=sin).ap()
    scratch["sv_b"] = Buf("sv_scr")
    for nm in ("beta", "gcum"):
        scratch[nm] = nc.dram_tensor(nm + "_scr", [H, S], F32, kind=sin).ap()
        scratch[nm + "_b"] = Buf(nm + "_scr")
    scratch["mix"] = nc.dram_tensor("mix_scr", [16, 128, S], BF16, kind=("ExternalOutput" if only else dbg)).ap()
    scratch["mix_b"] = Buf("mix_scr")
    scratch.update({
        "a": nc.dram_tensor("a_scr", [NT, 128, NFF, 128], BF16).ap(),
        "a_buf": Buf("a_scr"),
        "x1": nc.dram_tensor("x1_scr", [S, D], F32).ap(),
        "x2": nc.dram_tensor("x2_scr", [S, D], F32).ap(),
    })
    K = Sched(nc)
    with ExitStack() as es:
        C = Ctx(nc, K, es)
        consts = make_consts(C, es)
        xb, yb = Buf("xin"), Buf("yout")
        if mode == "ffn0":
            phase_ffn(C, 0, ins["x"], xb, y, yb, ins, consts, scratch)
        if mode in ("dbg_inproj", "dbg_gdn"):
            phase_inproj(C, 0, ins["x"], xb, ins, consts, scratch)
        if mode in ("dbg_gdn", "only_gdn"):
            phase_gdn(C, 0, ins, consts, scratch)
        if mode == "only_sb":
            phase_sb(C, 0, ins, consts, scratch)
        if mode == "full":
            x1b, x2b = Buf("x1s"), Buf("x2s")
            cur, curb = ins["x"], xb
            for l in range(DEPTH):
                K.scope = f"inproj{l}"
                phase_inproj(C, l, cur, curb, ins, consts, scratch)
                K.scope = f"gdn{l}"
                phase_gdn(C, l, ins, consts, scratch)
                K.scope = f"sb{l}"
                phase_sb(C, l, ins, consts, scratch)
                K.scope = f"outproj{l}"
                phase_outproj(C, l, cur, curb, scratch["x1"], x1b, ins, consts, scratch)
                K.scope = f"ffn{l}"
                if l == DEPTH - 1:
                    phase_ffn(C, l, scratch["x1"], x1b, y, yb, ins, consts, scratch)
                else:
                    phase_ffn(C, l, scratch["x1"], x1b, scratch["x2"], x2b, ins, consts, scratch)
                    cur, curb = scratch["x2"], x2b
        K.barrier()
        with nc.Block() as block:
            @block.tensor
            def _(e):
                K.emit("pe", e)

            @block.scalar
            def _(e):
                K.emit("act", e)

            @block.vector
            def _(e):
                K.emit("dve", e)

            @block.gpsimd
            def _(e):
                K.emit("pool", e)

            @block.sync
            def _(e):
                K.emit("sp", e)
    return nc, K


def kernel(**inputs):
    nc, K = build("full")
    x = np.ascontiguousarray(inputs["x"], dtype=np.float32)
    in_maps = []
    for c in range(8):
        m = {"x": x[c]}
        for name, _ in PARAMS:
            m[name] = np.ascontiguousarray(inputs[name], dtype=np.float32)
        in_maps.append(m)
    res = run_bass_kernel_spmd(nc, in_maps, core_ids=list(range(8)))
    return np.stack([r["y"] for r in res.results], axis=0)
```

```python
import numpy as np
from contextlib import ExitStack
import concourse.bass as bass
import concourse.mybir as mybir
from concourse.bass_utils import run_bass_kernel_spmd

F32 = mybir.dt.float32
BF16 = mybir.dt.bfloat16
AF = mybir.ActivationFunctionType
ALU = mybir.AluOpType
AX = mybir.AxisListType

D = 2048
S = 2048
DEPTH = 2
NT = S // 128
KC = D // 128
H = 8
HD = 128
IN_COLS = 7184
D_FF = 5632
NFF = D_FF // 128
RMS_EPS = 1e-6
L2_EPS = 1e-6

ENGS = ["pe", "act", "dve", "pool", "sp"]
SCOPES = False


class Buf:
    __slots__ = ("name", "w", "r", "dsem", "dcount")

    def __init__(self, name):
        self.name = name
        self.w = None
        self.r = {}
        self.dsem = None
        self.dcount = 0


class Sched:
    def __init__(self, nc):
        self.nc = nc
        self.sem = {e: nc.alloc_semaphore("cnt_" + e) for e in ENGS}
        self.count = {e: 0 for e in ENGS}
        self.known = {e: {} for e in ENGS}
        self.ops = {e: [] for e in ENGS}
        self.dma_bufs = []
        self.nwaits = 0
        self.muted = False
        self.free_sems = []
        self.nsem = 0
        self.scope = "init"

    def _deps(self, eng, reads, writes, skip_sem=None):
        waits = {}
        known = self.known[eng]
        own = self.sem[eng]

        def need(sem, val):
            if eng == "pe" and sem is own:
                return
            if known.get(sem, 0) >= val:
                return
            if waits.get(sem, 0) < val:
                waits[sem] = val

        for b in reads:
            if b.w is not None:
                need(*b.w)
        for b in writes:
            if b.w is not None and b.w[0] is not skip_sem:
                need(*b.w)
            for s, v in b.r.items():
                need(s, v)
        for s, v in waits.items():
            known[s] = v
        self.nwaits += len(waits)
        return list(waits.items())

    def _commit(self, ev, reads, writes):
        for b in reads:
            if b.r.get(ev[0], 0) < ev[1]:
                b.r[ev[0]] = ev[1]
        for b in writes:
            b.w = ev
            b.r = {}

    def op(self, eng, fn, reads=(), writes=(), sig=True):
        if self.muted:
            return
        waits = self._deps(eng, reads, writes)
        if sig:
            self.count[eng] += 1
            ev = (self.sem[eng], self.count[eng])
            inc = (self.sem[eng], 1)
        else:
            assert eng == "pe"
            ev = (self.sem[eng], self.count[eng] + 1)
            inc = None
        self.ops[eng].append((waits, fn, inc, self.scope))
        self._commit(ev, reads, writes)

    def dma(self, q, out, in_, reads, writes, owner, grp=False):
        if self.muted:
            return
        if owner.dsem is None:
            if self.free_sems:
                owner.dsem, owner.dcount = self.free_sems.pop()
            else:
                owner.dsem = self.nc.alloc_semaphore(f"dsem{self.nsem}")
                owner.dcount = 0
                self.nsem += 1
            self.dma_bufs.append(owner)
        waits = self._deps(q, reads, writes, owner.dsem if grp else None)
        owner.dcount += 16
        ev = (owner.dsem, owner.dcount)
        self.ops[q].append((waits, (lambda e, o=out, i=in_: e.dma_start(out=o, in_=i)), (owner.dsem, 16), self.scope))
        self._commit(ev, reads, writes)

    def barrier(self):
        if self.muted:
            return
        evs = [(self.sem[e], self.count[e]) for e in ENGS if self.count[e] > 0]
        evs += [(b.dsem, b.dcount) for b in self.dma_bufs]
        for e in ENGS:
            waits = []
            for s, v in evs:
                if e == "pe" and s is self.sem["pe"]:
                    continue
                if s is self.sem[e]:
                    continue
                if self.known[e].get(s, 0) < v:
                    self.known[e][s] = v
                    waits.append((s, v))
            if waits:
                self.ops[e].append((waits, None, None, self.scope))
        for b in self.dma_bufs:
            self.free_sems.append((b.dsem, b.dcount))
            b.dsem = None
        self.dma_bufs = []

    def emit(self, ename, eng):
        cur, cm = None, None
        for waits, fn, inc, scope in self.ops[ename]:
            if SCOPES and scope != cur:
                if cm is not None:
                    cm.__exit__(None, None, None)
                cm = self.nc.named_scope(scope)
                cm.__enter__()
                cur = scope
            for s, v in waits:
                eng.wait_ge(s, v)
            if fn is None:
                continue
            ins = fn(eng)
            if inc is not None:
                ins.then_inc(inc[0], inc[1])
        if cm is not None:
            cm.__exit__(None, None, None)


class T:
    def __init__(self, h, name, nbufs=1):
        self.h = h
        self.b = Buf(name)
        self.bs = [Buf(f"{name}_{i}") for i in range(nbufs)] if nbufs > 1 else [self.b]

    def __getitem__(self, k):
        return self.h[k]


class Ctx:
    def __init__(self, nc, K, es):
        self.nc, self.K, self.es = nc, K, es
        self.uid = 0

    def sb(self, es, name, shape, dt, nbufs=1):
        self.uid += 1
        nm = f"{name}_{self.uid}"
        return T(es.enter_context(self.nc.sbuf_tensor(nm, list(shape), dt)), nm, nbufs)

    def ps(self, es, name, shape, dt=F32, nbufs=1):
        self.uid += 1
        nm = f"{name}_{self.uid}"
        return T(es.enter_context(self.nc.psum_tensor(nm, list(shape), dt)), nm, nbufs)


def phase_norm(C, x_ap, x_buf, gain_row_ap, hT, consts):
    K, nc = C.K, C.nc
    with ExitStack() as es:
        gbc = C.sb(es, "gbc", [128, D], F32)
        xt = [C.sb(es, f"xt{i}", [128, D], F32) for i in range(2)]
        hs = [C.sb(es, f"hs{i}", [128, D], BF16) for i in range(2)]
        junk = C.sb(es, "junk", [128, D], BF16)
        st = [C.sb(es, f"st{i}", [128, 4], F32) for i in range(2)]
        pt = [C.ps(es, f"pt{i}", [128, D], BF16) for i in range(2)]
        K.dma("sp", gbc[:], gain_row_ap.partition_broadcast(128), [], [gbc.b], gbc.b)
        for t in range(NT):
            i = t % 2
            x_, h_, s_, p_ = xt[i], hs[i], st[i], pt[i]
            K.dma("sp", x_[:], x_ap[t * 128:(t + 1) * 128, :], [x_buf], [x_.b], x_.b)
            K.op("act", lambda e, x_=x_, s_=s_: e.activation(out=junk[:], in_=x_[:], func=AF.Square,
                                                           accum_out=s_[:, 0:1]),
                 [x_.b], [s_.b])
            K.op("dve", lambda e, s_=s_: e.tensor_scalar(out=s_[:, 1:2], in0=s_[:, 0:1], scalar1=1.0 / D,
                                                       scalar2=RMS_EPS, op0=ALU.mult, op1=ALU.add),
                 [s_.b], [s_.b])
            K.op("pool", lambda e, s_=s_: e.tensor_tensor(out=s_[:, 2:3], in0=s_[:, 1:2],
                                                        in1=consts["neghalf"][:, 0:1], op=ALU.pow),
                 [s_.b, consts["neghalf"].b], [s_.b])
            K.op("dve", lambda e, x_=x_, h_=h_, s_=s_: e.scalar_tensor_tensor(
                out=h_[:], in0=x_[:], scalar=s_[:, 2:3], in1=gbc[:], op0=ALU.mult, op1=ALU.mult),
                 [x_.b, s_.b, gbc.b], [h_.b])
            for kc in range(KC):
                K.op("pe", lambda e, h_=h_, p_=p_, kc=kc: e.transpose(
                    out=p_[:, kc * 128:(kc + 1) * 128], in_=h_[:, kc * 128:(kc + 1) * 128],
                    identity=consts["ident"][:]),
                     [h_.b, consts["ident"].b], [p_.b], sig=(kc == KC - 1))
            K.op("act", lambda e, p_=p_, t=t: e.copy(
                out=hT[:, :, t * 128:(t + 1) * 128],
                in_=p_[:].rearrange("p (k n) -> p k n", n=128)),
                 [p_.b], [hT.bs[t // 4]])
        K.barrier()


def alloc_stage(C, es, n=3):
    C.stg = [C.sb(es, f"stg{i}", [128, 4, 512], F32) for i in range(n)]
    C.stg_i = 0


def load_w(C, slot, w_ap, c0, ncols, col_off=0, nk=KC, wbuf=None):
    K = C.K
    wbuf = wbuf if wbuf is not None else slot.b
    step = 4
    for k0 in range(0, nk, step):
        kk = min(step, nk - k0)
        st = C.stg[C.stg_i % len(C.stg)]
        C.stg_i += 1
        src = w_ap[k0 * 128:(k0 + kk) * 128, c0:c0 + ncols].rearrange("(kc p) n -> p kc n", p=128)
        K.dma("sp", st[:, 0:kk, 0:ncols], src, [], [st.b], st.b)
        K.op("dve", lambda e, st=st, k0=k0, kk=kk: e.tensor_copy(
            out=slot[:, k0:k0 + kk, col_off:col_off + ncols], in_=st[:, 0:kk, 0:ncols]), [st.b], [wbuf])


def phase_ffn(C, l, x_in, x_in_buf, x_out, x_out_buf, ins, consts, scratch):
    K, nc = C.K, C.nc
    a_scr, a_buf = scratch["a"], scratch["a_buf"]
    with ExitStack() as es:
        hT = C.sb(es, "hT", [128, KC, S], BF16, nbufs=4)
        wsl = [C.sb(es, f"wup{i}", [128, KC, 512], BF16) for i in range(3)]
        alloc_stage(C, es, 2)
        for g in range(2):
            load_w(C, wsl[g], ins["w_up"][l], g * 256, 256, 0)
            load_w(C, wsl[g], ins["w_up"][l], D_FF + g * 256, 256, 256)
        phase_norm(C, x_in, x_in_buf, ins["ffn_norm"][l:l + 1, :], hT, consts)
        with ExitStack() as es2:
            cw = C.sb(es2, "cw", [128, 4, 2 * NFF], F32)
            raw = [C.sb(es2, f"raw{i}", [128, 2 + S], F32) for i in range(4)]
            acc = [C.sb(es2, f"acc{i}", [128, S], F32) for i in range(2)]
            sg = C.sb(es2, "sg", [128, S], F32)
            aT = [C.sb(es2, f"aT{i}", [128, NT, 2, 128], BF16) for i in range(2)]
            pp = [C.ps(es2, f"pp{i}", [128, 512], F32) for i in range(8)]
            cwr = C.sb(es2, "cwr", [2 * NFF, 4, 128], F32)
            for tap in range(3):
                K.dma("sp", cwr[:, tap, :], ins["ffn_conv"][l, tap, :].rearrange("(j p) -> j p", p=128),
                      [], [cwr.b], cwr.b)
            K.dma("sp", cwr[:, 3, :], ins["ffn_conv_bias"][l, :].rearrange("(j p) -> j p", p=128),
                  [], [cwr.b], cwr.b)
            for tap in range(4):
                K.op("pe", lambda e, tap=tap: e.transpose(out=pp[0][:, tap * 128:tap * 128 + 2 * NFF],
                                                          in_=cwr[:, tap, :], identity=consts["identf"][0:2 * NFF, 0:2 * NFF]),
                     [cwr.b, consts["identf"].b], [pp[0].b], sig=(tap == 3))
            K.op("act", lambda e: e.copy(out=cw[:], in_=pp[0][:].rearrange("p (t n) -> p t n", n=128)[:, :, 0:2 * NFF]),
                 [pp[0].b], [cw.b])
            for r in raw:
                K.op("pool", lambda e, r=r: e.memset(r[:, 0:2], 0.0), [], [r.b])
            ngrp = NFF // 2
            pidx = 0
            for g in range(ngrp):
                w_ = wsl[g % 3]
                if g + 2 < ngrp:
                    load_w(C, wsl[(g + 2) % 3], ins["w_up"][l], (g + 2) * 256, 256, 0)
                    load_w(C, wsl[(g + 2) % 3], ins["w_up"][l], D_FF + (g + 2) * 256, 256, 256)
                for jj in range(2):
                    j = g * 2 + jj
                    rg, ru = raw[(j % 2) * 2], raw[(j % 2) * 2 + 1]
                    for which, r_ in ((0, rg), (1, ru)):
                        co = which * 256 + jj * 128
                        for tc in range(4):
                            p_ = pp[pidx % 8]
                            pidx += 1
                            for kc in range(KC):
                                K.op("pe", lambda e, p_=p_, w_=w_, kc=kc, co=co, tc=tc: e.matmul(
                                    p_[:], lhsT=w_[:, kc, co:co + 128], rhs=hT[:, kc, tc * 512:(tc + 1) * 512],
                                    start=(kc == 0), stop=(kc == KC - 1)),
                                     [w_.b, hT.bs[tc]], [p_.b], sig=(kc == KC - 1))
                            K.op("act", lambda e, p_=p_, r_=r_, tc=tc: e.copy(
                                out=r_[:, 2 + tc * 512:2 + (tc + 1) * 512], in_=p_[:]), [p_.b], [r_.b])
                    a_ = aT[(j // 2) % 2]
                    for which, r_, ac in ((0, rg, acc[0]), (1, ru, acc[1])):
                        ch = which * NFF + j
                        K.op("act", lambda e, r_=r_, ac=ac, ch=ch: e.activation(
                            out=ac[:], in_=r_[:, 2:2 + S], func=AF.Identity,
                            scale=cw[:, 2, ch:ch + 1], bias=cw[:, 3, ch:ch + 1]), [r_.b, cw.b], [ac.b])
                        K.op("dve", lambda e, r_=r_, ac=ac, ch=ch: e.scalar_tensor_tensor(
                            out=ac[:], in0=r_[:, 1:1 + S], scalar=cw[:, 1, ch:ch + 1], in1=ac[:],
                            op0=ALU.mult, op1=ALU.add), [r_.b, cw.b, ac.b], [ac.b])
                        K.op("dve", lambda e, r_=r_, ac=ac, ch=ch: e.scalar_tensor_tensor(
                            out=ac[:], in0=r_[:, 0:S], scalar=cw[:, 0, ch:ch + 1], in1=ac[:],
                            op0=ALU.mult, op1=ALU.add), [r_.b, cw.b, ac.b], [ac.b])
                    K.op("act", lambda e: e.activation(out=sg[:], in_=acc[0][:], func=AF.Silu),
                         [acc[0].b], [sg.b])
                    K.op("dve", lambda e, a_=a_, j=j: e.tensor_tensor(out=a_[:, :, j % 2, :], in0=sg[:].rearrange('p (t n) -> p t n', n=128), in1=acc[1][:].rearrange('p (t n) -> p t n', n=128), op=ALU.mult),
                         [sg.b, acc[1].b], [a_.b])
                    if j % 2 == 1:
                        K.dma("sp", a_scr[:, :, j - 1:j + 1, :].rearrange("t p j n -> p t (j n)"),
                              a_[:].rearrange("p t j n -> p t (j n)"), [a_.b], [a_buf], a_.b)
        K.barrier()
    with ExitStack() as es:
        wd = [C.sb(es, f"wd{i}", [128, NFF, 512], BF16) for i in range(2)]
        alloc_stage(C, es)
        at = [C.sb(es, f"at{i}", [128, NFF, 128], BF16) for i in range(3)]
        xr = [C.sb(es, f"xr{i}", [128, 512], F32) for i in range(3)]
        xo = [C.sb(es, f"xo{i}", [128, 512], F32) for i in range(3)]
        pp = [C.ps(es, f"pd{i}", [128, 512], F32) for i in range(4)]
        n = 0
        load_w(C, wd[0], ins["w_down"][l], 0, 512, 0, nk=NFF)
        for q in range(4):
            w_ = wd[q % 2]
            if q + 1 < 4:
                load_w(C, wd[(q + 1) % 2], ins["w_down"][l], (q + 1) * 512, 512, 0, nk=NFF)
            for t in range(NT):
                a_, xr_, xo_, p_ = at[n % 3], xr[n % 3], xo[n % 3], pp[n % 4]
                n += 1
                K.dma("sp", a_[:], a_scr[t], [a_buf], [a_.b], a_.b)
                K.dma("sp", xr_[:], x_in[t * 128:(t + 1) * 128, q * 512:(q + 1) * 512], [x_in_buf], [xr_.b], xr_.b)
                for j in range(NFF):
                    K.op("pe", lambda e, p_=p_, a_=a_, w_=w_, j=j: e.matmul(
                        p_[:], lhsT=a_[:, j, :], rhs=w_[:, j, :], start=(j == 0), stop=(j == NFF - 1)),
                         [a_.b, w_.b], [p_.b], sig=(j == NFF - 1))
                K.op("dve", lambda e, p_=p_, xr_=xr_, xo_=xo_: e.tensor_tensor(
                    out=xo_[:], in0=p_[:], in1=xr_[:], op=ALU.add), [p_.b, xr_.b], [xo_.b])
                K.dma("sp", x_out[t * 128:(t + 1) * 128, q * 512:(q + 1) * 512], xo_[:], [xo_.b], [x_out_buf], xo_.b)
        K.barrier()


def phase_inproj(C, l, x_in, x_in_buf, ins, consts, scr):
    K, nc = C.K, C.nc
    W = ins["w_in"][l]
    with ExitStack() as es:
        hT = C.sb(es, "hT", [128, KC, S], BF16, nbufs=4)
        wsl = [C.sb(es, f"win{i}", [128, KC, 512], BF16) for i in range(3)]
        wba = C.sb(es, "wba", [128, KC, 16], BF16)
        alloc_stage(C, es, 2)
        load_w(C, wsl[0], W, 0, 512, 0)
        load_w(C, wsl[1], W, 512, 512, 0)
        phase_norm(C, x_in, x_in_buf, ins["attn_norm"][l:l + 1, :], hT, consts)
        raw = [C.sb(es, f"raw{i}", [128, 3 + S], F32) for i in range(2)]
        acc = [C.sb(es, f"acc{i}", [128, S], F32) for i in range(2)]
        ysq = C.sb(es, "ysq", [128, S], BF16)
        lnv = C.sb(es, "lnv", [128, S], F32)
        ob = [C.sb(es, f"ob{i}", [128, S], BF16) for i in range(2)]
        svt = [C.sb(es, f"svt{i}", [128, 512], BF16) for i in range(2)]
        gcw = C.sb(es, "gcw", [128, 4, 24], F32)
        gcr = C.sb(es, "gcr", [24, 4, 128], F32)
        nrm = C.sb(es, "nrm", [128, 2], F32)
        hp = C.sb(es, "hp", [8, 4], F32)
        pm = [C.ps(es, f"pm{i}", [128, 512], F32) for i in range(6)]
        po = [C.ps(es, f"po{i}", [128, 512], F32) for i in range(2)]
        ones_bf = consts["ones_bf"]
        for tap in range(4):
            K.dma("sp", gcr[:, tap, :], ins["gdn_conv"][l, tap, :].rearrange("(j p) -> j p", p=128), [], [gcr.b], gcr.b)
        for tap in range(4):
            K.op("pe", lambda e, tap=tap: e.transpose(out=pm[0][:, tap * 32:tap * 32 + 24], in_=gcr[:, tap, :],
                                                      identity=consts["identf"][0:24, 0:24]),
                 [gcr.b, consts["identf"].b], [pm[0].b], sig=(tap == 3))
        K.op("act", lambda e: e.copy(out=gcw[:], in_=pm[0][:, 0:128].rearrange("p (t n) -> p t n", n=32)[:, :, 0:24]),
             [pm[0].b], [gcw.b])
        K.dma("sp", nrm[:, 0:1], ins["sb_q_norm"][l, :].rearrange("(p o) -> p o", o=1), [], [nrm.b], nrm.b)
        K.dma("sp", nrm[:, 1:2], ins["sb_k_norm"][l, :].rearrange("(p o) -> p o", o=1), [], [nrm.b], nrm.b)
        K.dma("sp", hp[:, 0:1], ins["gdn_a_log"][l, :].rearrange("(p o) -> p o", o=1), [], [hp.b], hp.b)
        K.dma("sp", hp[:, 1:2], ins["gdn_dt_bias"][l, :].rearrange("(p o) -> p o", o=1), [], [hp.b], hp.b)
        K.op("act", lambda e: e.activation(out=hp[:, 2:3], in_=hp[:, 0:1], func=AF.Exp), [hp.b], [hp.b])
        K.op("dve", lambda e: e.tensor_scalar(out=hp[:, 2:3], in0=hp[:, 2:3], scalar1=-1.0, scalar2=None, op0=ALU.mult),
             [hp.b], [hp.b])
        for r in raw:
            K.op("pool", lambda e, r=r: e.memset(r[:, 0:3], 0.0), [], [r.b])
        st = {"pi": 0, "ti": 0}

        def mm_tile(w_, co, M=128):
            ps = []
            for tc in range(4):
                p_ = pm[st["pi"] % 6]
                st["pi"] += 1
                for kc in range(KC):
                    K.op("pe", lambda e, p_=p_, w_=w_, kc=kc, co=co, tc=tc, M=M: e.matmul(
                        p_[0:M, :], lhsT=w_[:, kc, co:co + M], rhs=hT[:, kc, tc * 512:(tc + 1) * 512],
                        start=(kc == 0), stop=(kc == KC - 1)),
                         [w_.b, hT.bs[tc]], [p_.b], sig=(kc == KC - 1))
                ps.append(p_)
            return ps

        def l2_or_rms(src, out_, scale, bias, gain_ap):
            K.op("act", lambda e: e.activation(out=ysq[:], in_=src[:, 0:S], func=AF.Square), [src.b], [ysq.b])
            for tc in range(4):
                p_ = po[tc % 2]
                K.op("pe", lambda e, p_=p_, tc=tc: e.matmul(p_[:], lhsT=ones_bf[:], rhs=ysq[:, tc * 512:(tc + 1) * 512],
                                                            start=True, stop=True), [ones_bf.b, ysq.b], [p_.b])
                K.op("act", lambda e, p_=p_, tc=tc: e.activation(out=lnv[:, tc * 512:(tc + 1) * 512], in_=p_[:], func=AF.Ln,
                                                                 scale=scale, bias=consts["bias"][:, bias:bias + 1]),
                     [p_.b, consts["bias"].b], [lnv.b])
            K.op("act", lambda e: e.activation(out=lnv[:], in_=lnv[:], func=AF.Exp, scale=-0.5), [lnv.b], [lnv.b])
            if gain_ap is None:
                K.op("dve", lambda e: e.tensor_tensor(out=out_[:], in0=src[:, 0:S], in1=lnv[:], op=ALU.mult),
                     [src.b, lnv.b], [out_.b])
            else:
                K.op("dve", lambda e: e.scalar_tensor_tensor(out=out_[:], in0=src[:, 0:S], scalar=gain_ap, in1=lnv[:],
                                                             op0=ALU.mult, op1=ALU.mult), [src.b, lnv.b, nrm.b], [out_.b])

        groups = []
        for kind, c0 in [("gq", 0), ("gk", 1024), ("gv", 2048), ("gz", 3072), ("sq", 4112), ("sk", 5136), ("sv", 6160)]:
            groups += [(kind, c0, 0), (kind, c0 + 512, 1)]
        for gi, (kind, c0, half) in enumerate(groups):
            w_ = wsl[gi % 3]
            if gi + 2 < len(groups):
                load_w(C, wsl[(gi + 2) % 3], W, groups[gi + 2][1], 512, 0)
            if kind == "gv" and half == 0:
                load_w(C, wba, W, 4096, 16, 0)
            if kind == "sv":
                for t in range(NT):
                    sv_ = svt[t % 2]
                    p_ = pm[st["pi"] % 6]
                    st["pi"] += 1
                    for kc in range(KC):
                        K.op("pe", lambda e, p_=p_, w_=w_, kc=kc, t=t: e.matmul(
                            p_[:], lhsT=hT[:, kc, t * 128:(t + 1) * 128], rhs=w_[:, kc, :],
                            start=(kc == 0), stop=(kc == KC - 1)),
                             [w_.b, hT.bs[t // 4]], [p_.b], sig=(kc == KC - 1))
                    K.op("act", lambda e, p_=p_, sv_=sv_: e.copy(out=sv_[:], in_=p_[:]), [p_.b], [sv_.b])
                    K.dma("sp", scr["sv"][t][:, half * 512:(half + 1) * 512], sv_[:], [sv_.b], [scr["sv_b"]], sv_.b)
                continue
            for hh in range(half * 4, half * 4 + 4):
                ti = st["ti"]
                st["ti"] += 1
                ps = mm_tile(w_, (hh % 4) * 128)
                r_, a_, o_ = raw[ti % 2], acc[ti % 2], ob[ti % 2]
                if kind in ("gq", "gk", "gv"):
                    ct = {"gq": 0, "gk": 8, "gv": 16}[kind] + hh
                    for tc in range(4):
                        K.op("act", lambda e, p_=ps[tc], r_=r_, tc=tc: e.copy(out=r_[:, 3 + tc * 512:3 + (tc + 1) * 512], in_=p_[:]),
                             [ps[tc].b], [r_.b])
                    K.op("act", lambda e, r_=r_, a_=a_, ct=ct: e.activation(out=a_[:], in_=r_[:, 3:3 + S], func=AF.Identity,
                                                                          scale=gcw[:, 3, ct:ct + 1]), [r_.b, gcw.b], [a_.b])
                    for tap in range(3):
                        K.op("dve", lambda e, r_=r_, a_=a_, ct=ct, tap=tap: e.scalar_tensor_tensor(
                            out=a_[:], in0=r_[:, tap:tap + S], scalar=gcw[:, tap, ct:ct + 1], in1=a_[:],
                            op0=ALU.mult, op1=ALU.add), [r_.b, gcw.b, a_.b], [a_.b])
                    if kind == "gv":
                        K.op("act", lambda e, a_=a_, o_=o_: e.activation(out=o_[:], in_=a_[:], func=AF.Silu), [a_.b], [o_.b])
                    else:
                        K.op("act", lambda e, a_=a_: e.activation(out=a_[:], in_=a_[:], func=AF.Silu), [a_.b], [a_.b])
                        if kind == "gq":
                            l2_or_rms(a_, o_, 128.0, 0, None)
                        else:
                            l2_or_rms(a_, o_, 1.0, 1, None)
                    K.dma("sp", scr[kind][hh], o_[:], [o_.b], [scr[kind + "_b"]], o_.b)
                elif kind == "gz":
                    for tc in range(4):
                        K.op("act", lambda e, p_=ps[tc], o_=o_, tc=tc: e.activation(
                            out=o_[:, tc * 512:(tc + 1) * 512], in_=p_[:], func=AF.Silu), [ps[tc].b], [o_.b])
                    K.dma("sp", scr["gz"][hh], o_[:], [o_.b], [scr["gz_b"]], o_.b)
                else:
                    for tc in range(4):
                        K.op("act", lambda e, p_=ps[tc], a_=a_, tc=tc: e.copy(out=a_[:, tc * 512:(tc + 1) * 512], in_=p_[:]),
                             [ps[tc].b], [a_.b])
                    if kind == "sq":
                        l2_or_rms(a_, o_, 1.0, 2, nrm[:, 0:1])
                    else:
                        l2_or_rms(a_, o_, 1.0 / 128.0, 3, nrm[:, 1:2])
                    K.dma("sp", scr[kind][hh], o_[:], [o_.b], [scr[kind + "_b"]], o_.b)
            if kind == "gz" and half == 1:
                bb, ee, cc, rm = acc[0], acc[1], lnv, ysq
                for which in range(2):
                    ps = mm_tile(wba, which * 8, M=8)
                    for tc in range(4):
                        sl = slice(tc * 512, (tc + 1) * 512)
                        if which == 0:
                            K.op("act", lambda e, p_=ps[tc], sl=sl: e.activation(out=bb[0:8, sl], in_=p_[0:8, :], func=AF.Sigmoid),
                                 [ps[tc].b], [bb.b])
                        else:
                            K.op("act", lambda e, p_=ps[tc], sl=sl: e.activation(out=ee[0:8, sl], in_=p_[0:8, :], func=AF.Exp,
                                                                              bias=hp[:, 1:2]), [ps[tc].b, hp.b], [ee.b])
                K.op("act", lambda e: e.activation(out=ee[0:8, :], in_=ee[0:8, :], func=AF.Ln, bias=consts["bias"][0:8, 4:5]),
                     [ee.b, consts["bias"].b], [ee.b])
                K.op("dve", lambda e: e.tensor_scalar(out=ee[0:8, :], in0=ee[0:8, :], scalar1=hp[:, 2:3], scalar2=None,
                                                      op0=ALU.mult), [ee.b, hp.b], [ee.b])
                K.op("pool", lambda e: e.memset(rm[0:8, :], 1.0), [], [rm.b])
                K.op("pool", lambda e: e.memset(rm[0:8, :].rearrange("p (c n) -> p c n", n=128)[:, :, 0:1], 0.0), [rm.b], [rm.b])
                K.op("dve", lambda e: e.tensor_tensor_scan(out=cc[0:8, :], data0=rm[0:8, :], data1=ee[0:8, :], initial=0.0,
                                                           op0=ALU.mult, op1=ALU.add), [ee.b, rm.b], [cc.b])
                K.dma("sp", scr["beta"], bb[0:8, :], [bb.b], [scr["beta_b"]], bb.b)
                K.dma("sp", scr["gcum"], cc[0:8, :], [cc.b], [scr["gcum_b"]], cc.b)
        K.barrier()


GDN_STOP = 99


class _Stop(Exception):
    pass


_KREF = []


def _chk(k):
    if GDN_STOP <= k:
        _KREF[0].muted = True


def phase_gdn(C, l, ins, consts, scr):
    _KREF[:] = [C.K]
    _phase_gdn(C, l, ins, consts, scr)
    C.K.muted = False
    C.K.barrier()


def _phase_gdn(C, l, ins, consts, scr):
    K, nc = C.K, C.nc
    U32 = mybir.dt.uint32
    ident, onesf, identf, ones_bf = consts["ident"], consts["onesf"], consts["identf"], consts["ones_bf"]
    with ExitStack() as es:
        gT = C.sb(es, "gT", [8, S], F32)
        bT = C.sb(es, "bT", [8, S], F32)
        tk = C.sb(es, "tk", [128, 6, 128], F32)
        ogain = C.sb(es, "ogain", [128, 1], F32)
        masks = C.sb(es, "masks", [128, 14, 128], F32)
        masks8 = C.sb(es, "masks8", [128, 14, 512], mybir.dt.uint8)
        sel = C.sb(es, "sel", [128, 128], F32)
        zer = C.sb(es, "zer", [128, 128], F32)
        mneg = C.sb(es, "mneg", [128, 128], F32)
        B = [C.ps(es, f"B{i}", [128, 512], F32) for i in range(8)]
        K.dma("sp", gT[:], scr["gcum"], [scr["gcum_b"]], [gT.b], gT.b)
        K.dma("sp", bT[:], scr["beta"], [scr["beta_b"]], [bT.b], bT.b)
        K.dma("sp", ogain[:], ins["gdn_o_norm"][l, :].rearrange("(p o) -> p o", o=1), [], [ogain.b], ogain.b)
        K.op("pool", lambda e: e.memset(zer[:], 0.0), [], [zer.b])
        K.op("pool", lambda e: e.affine_select(out=mneg[:], in_=zer[:], pattern=[[1, 128]], compare_op=ALU.is_ge,
                                               fill=-30000.0, base=0, channel_multiplier=-1), [zer.b], [mneg.b])
        K.op("pool", lambda e: e.affine_select(out=sel[:], in_=onesf[:], pattern=[[0, 128]], compare_op=ALU.is_ge,
                                               fill=0.0, base=-127, channel_multiplier=1), [onesf.b], [sel.b])
        for lv in range(7):
            n = 1 << lv
            nb = 128 // (2 * n)
            specs = [
                (lv, [(-n, 1, [[-2 * n, nb], [0, 2], [0, n]]), (2 * n - 1, -1, [[2 * n, nb], [0, 2], [0, n]]),
                      (0, 0, [[0, nb], [-1, 2], [0, n]])]),
                (7 + lv, [(0, 1, [[-2 * n, nb], [0, 2], [0, n]]), (n - 1, -1, [[2 * n, nb], [0, 2], [0, n]]),
                          (-1, 0, [[0, nb], [1, 2], [0, n]])]),
            ]
            for mi, passes in specs:
                for pi, (base, cm, pat) in enumerate(passes):
                    src = onesf if pi == 0 else masks
                    K.op("pool", lambda e, mi=mi, base=base, cm=cm, pat=pat, pi=pi: e.affine_select(
                        out=masks[:, mi, :], in_=(onesf[:] if pi == 0 else masks[:, mi, :]), pattern=pat,
                        compare_op=ALU.is_ge, fill=0.0, base=base, channel_multiplier=cm),
                         [src.b], [masks.b])
        K.op("dve", lambda e: e.tensor_copy(out=masks8[:].rearrange("p m (r n) -> p m r n", n=128),
                                            in_=masks[:].unsqueeze(2).to_broadcast([128, 14, 4, 128])), [masks.b], [masks8.b])
        for which, srcT in ((0, gT), (1, bT)):
            for c in range(NT):
                K.op("pe", lambda e, which=which, srcT=srcT, c=c: e.transpose(
                    out=B[which][:, c * 8:c * 8 + 8], in_=srcT[:, c * 128:(c + 1) * 128],
                    identity=identf[0:8, 0:8]), [srcT.b, identf.b], [B[which].b], sig=(c == NT - 1))
            K.op("act", lambda e, which=which: e.copy(out=tk[:, which, :], in_=B[which][:, 0:128]),
                 [B[which].b], [tk.b])
        K.op("dve", lambda e: e.tensor_scalar(out=tk[:, 2, :], in0=tk[:, 1, :], scalar1=-1.0, scalar2=None, op0=ALU.mult),
             [tk.b], [tk.b])
        K.op("act", lambda e: e.activation(out=tk[:, 3, :], in_=tk[:, 0, :], func=AF.Exp), [tk.b], [tk.b])
        K.op("pe", lambda e: e.matmul(B[2][:, 0:128], lhsT=sel[:], rhs=tk[:, 0, :], start=True, stop=True),
             [sel.b, tk.b], [B[2].b])
        K.op("act", lambda e: e.activation(out=tk[:, 5, :], in_=B[2][:, 0:128], func=AF.Exp), [B[2].b], [tk.b])
        K.op("dve", lambda e: e.tensor_tensor(out=tk[:, 4, :], in0=B[2][:, 0:128], in1=tk[:, 0, :], op=ALU.subtract),
             [B[2].b, tk.b], [tk.b])
        K.op("act", lambda e: e.activation(out=tk[:, 4, :], in_=tk[:, 4, :], func=AF.Exp), [tk.b], [tk.b])
        _chk(1)

        def col(which, h):
            return tk[:, which, :].rearrange("p (c h) -> p c h", h=8)[:, :, h:h + 1]

        def v3(ap):
            return ap.rearrange("p (c n) -> p c n", n=128)

        with ExitStack() as eg:
            slots = []
            for hi in range(4):
                sl = {nm: C.sb(eg, f"{nm}{hi}", [128, S], BF16, nbufs=16) for nm in ("wT", "ub", "kd", "qg", "AT", "oT")}
                sl["S32"] = C.sb(eg, f"S32{hi}", [128, 128], F32)
                sl["Sbf"] = C.sb(eg, f"Sbf{hi}", [128, 128], BF16)
                sl["vn"] = [C.sb(eg, f"vn{hi}_{i}", [128, 128], BF16) for i in range(2)]
                slots.append(sl)
            for grp in range(2):
                heads = list(range(grp * 4, grp * 4 + 4))
                per = {h: slots[h % 4] for h in heads}
                if grp == 0:
                  qT = C.sb(eg, "qT", [128, S], BF16)
                  kT = C.sb(eg, "kT", [128, S], BF16)
                  vT = C.sb(eg, "vT", [128, S], BF16)
                  Rb = C.sb(eg, "Rb", [128, S], F32)
                  E = C.sb(eg, "E", [128, S], F32)
                  dU = C.sb(eg, "dU", [128, S], BF16)
                  LnT = C.sb(eg, "LnT", [128, S], BF16, nbufs=4)
                  Dm = C.sb(eg, "Dm", [128, S], BF16, nbufs=4)
                  DTm = C.sb(eg, "DTm", [128, S], BF16, nbufs=4)
                  Ysb = C.sb(eg, "Ysb", [128, S], BF16, nbufs=4)
                  kg = C.sb(eg, "kg", [128, S], BF16)
                  vtok = C.sb(eg, "vtok", [128, S], BF16)
                  zs = C.sb(eg, "zs", [128, S], BF16)
                for h in heads:
                    P = per[h]
                    K.dma("sp", qT[:], scr["gq"][h], [scr["gq_b"]], [qT.b], qT.b)
                    K.dma("sp", kT[:], scr["gk"][h], [scr["gk_b"]], [kT.b], kT.b)
                    K.dma("sp", vT[:], scr["gv"][h], [scr["gv_b"]], [vT.b], vT.b)
                    K.dma("sp", Rb[:], scr["gcum"][h, :].partition_broadcast(128), [scr["gcum_b"]], [Rb.b], Rb.b)
                    K.op("act", lambda e: e.activation(out=E[:], in_=Rb[:], func=AF.Exp), [Rb.b], [E.b])
                    K.op("dve", lambda e, P=P: e.tensor_tensor(out=P["qg"][:], in0=qT[:], in1=E[:], op=ALU.mult),
                         [qT.b, E.b], P["qg"].bs)
                    K.op("dve", lambda e, h=h: e.tensor_tensor(out=v3(E[:]), in0=v3(Rb[:]),
                                                               in1=col(0, h).to_broadcast([128, NT, 128]), op=ALU.subtract),
                         [Rb.b, tk.b], [E.b])
                    K.op("dve", lambda e: e.tensor_tensor(out=v3(E[:]), in0=v3(E[:]),
                                                          in1=mneg[:].unsqueeze(1).to_broadcast([128, NT, 128]), op=ALU.add),
                         [E.b, mneg.b], [E.b])
                    K.op("act", lambda e: e.activation(out=dU[:], in_=E[:], func=AF.Exp), [E.b], [dU.b])
                    for q in range(4):
                        for i in range(4):
                            cs = slice((q * 4 + i) * 128, (q * 4 + i + 1) * 128)
                            K.op("pe", lambda e, q=q, i=i, cs=cs: e.matmul(
                                B[q][:, i * 128:(i + 1) * 128], lhsT=kT[:, cs], rhs=kT[:, cs], start=True, stop=True),
                                 [kT.b], [B[q].b], sig=(i == 3))
                        for i in range(4):
                            c = q * 4 + i
                            cs = slice(c * 128, (c + 1) * 128)
                            K.op("dve", lambda e, q=q, i=i, cs=cs, c=c, h=h: e.scalar_tensor_tensor(
                                out=LnT[:, cs], in0=B[q][:, i * 128:(i + 1) * 128],
                                scalar=tk[:, 2, c * 8 + h:c * 8 + h + 1], in1=dU[:, cs], op0=ALU.mult, op1=ALU.mult),
                                 [B[q].b, tk.b, dU.b], [LnT.bs[q]])
                    for q in range(4):
                        for i in range(4):
                            cs = slice((q * 4 + i) * 128, (q * 4 + i + 1) * 128)
                            K.op("pe", lambda e, q=q, i=i, cs=cs: e.matmul(
                                B[4 + q][:, i * 128:(i + 1) * 128], lhsT=kT[:, cs], rhs=qT[:, cs], start=True, stop=True),
                                 [kT.b, qT.b], [B[4 + q].b], sig=(i == 3))
                        qs = slice(q * 512, (q + 1) * 512)
                        K.op("dve", lambda e, q=q, qs=qs, P=P: e.tensor_tensor(
                            out=P["AT"][:, qs], in0=B[4 + q][:], in1=dU[:, qs], op=ALU.mult),
                             [B[4 + q].b, dU.b], P["AT"].bs[q * 4:q * 4 + 4])
                    _chk(2)
                    for src, b0 in ((kT, 0), (vT, 2)):
                        for hb in range(2):
                            pv = B[b0 + hb][:].bitcast(BF16)
                            for i in range(8):
                                cs = slice((hb * 8 + i) * 128, (hb * 8 + i + 1) * 128)
                                K.op("pe", lambda e, pv=pv, i=i, cs=cs, src=src: e.transpose(
                                    out=pv[:, i * 128:(i + 1) * 128], in_=src[:, cs], identity=ident[:]),
                                     [src.b, ident.b], [B[b0 + hb].b], sig=(i == 7))
                    for hb in range(2):
                        hs_ = slice(hb * 1024, (hb + 1) * 1024)
                        pk = B[hb][:].bitcast(BF16)
                        pvv = B[2 + hb][:].bitcast(BF16)
                        K.op("dve", lambda e, h=h, hb=hb, hs_=hs_, pk=pk: e.tensor_tensor(
                            out=v3(kg[:, hs_]), in0=v3(pk), in1=col(3, h)[:, hb * 8:(hb + 1) * 8, :].to_broadcast([128, 8, 128]),
                            op=ALU.mult), [B[hb].b, tk.b], [kg.b])
                        K.op("dve", lambda e, h=h, hb=hb, hs_=hs_, pk=pk, P=P: e.tensor_tensor(
                            out=v3(P["kd"][:, hs_]), in0=v3(pk), in1=col(4, h)[:, hb * 8:(hb + 1) * 8, :].to_broadcast([128, 8, 128]),
                            op=ALU.mult), [B[hb].b, tk.b], P["kd"].bs[hb * 8:(hb + 1) * 8])
                        K.op("act", lambda e, hs_=hs_, pvv=pvv: e.copy(out=vtok[:, hs_], in_=pvv), [B[2 + hb].b], [vtok.b])
                    _chk(3)
                    K.op("dve", lambda e: e.tensor_copy(out=v3(Dm[:]), in_=ident[:].unsqueeze(1).to_broadcast([128, NT, 128])),
                         [ident.b], Dm.bs)
                    K.op("dve", lambda e: e.tensor_copy(out=v3(DTm[:]), in_=ident[:].unsqueeze(1).to_broadcast([128, NT, 128])),
                         [ident.b], DTm.bs)
                    for lv in range(7):
                        for q in range(4):
                            s_ = q % 2
                            qs = slice(q * 512, (q + 1) * 512)
                            Yp, Zp, ZTp = B[s_ * 3], B[s_ * 3 + 1], B[s_ * 3 + 2]
                            for i in range(4):
                                cs = slice(q * 512 + i * 128, q * 512 + (i + 1) * 128)
                                K.op("pe", lambda e, cs=cs, i=i, Yp=Yp: e.matmul(
                                    Yp[:, i * 128:(i + 1) * 128], lhsT=LnT[:, cs], rhs=Dm[:, cs],
                                    start=True, stop=True), [LnT.bs[q], Dm.bs[q]], [Yp.b], sig=(i == 3))
                            K.op("act", lambda e, qs=qs, Yp=Yp: e.copy(out=Ysb[:, qs], in_=Yp[:]), [Yp.b], [Ysb.bs[q]])
                            for i in range(4):
                                cs = slice(q * 512 + i * 128, q * 512 + (i + 1) * 128)
                                K.op("pe", lambda e, cs=cs, i=i, Zp=Zp: e.matmul(
                                    Zp[:, i * 128:(i + 1) * 128], lhsT=DTm[:, cs], rhs=Ysb[:, cs],
                                    start=True, stop=True), [DTm.bs[q], Ysb.bs[q]], [Zp.b], sig=(i == 3))
                            for i in range(4):
                                cs = slice(q * 512 + i * 128, q * 512 + (i + 1) * 128)
                                K.op("pe", lambda e, cs=cs, i=i, ZTp=ZTp: e.matmul(
                                    ZTp[:, i * 128:(i + 1) * 128], lhsT=Ysb[:, cs], rhs=DTm[:, cs],
                                    start=True, stop=True), [DTm.bs[q], Ysb.bs[q]], [ZTp.b], sig=(i == 3))
                            K.op("dve", lambda e, qs=qs, lv=lv, Zp=Zp: e.copy_predicated(
                                out=Dm[:, qs], mask=masks8[:, lv, :], data=Zp[:]), [Zp.b, masks8.b], [Dm.bs[q]])
                            K.op("dve", lambda e, qs=qs, lv=lv, ZTp=ZTp: e.copy_predicated(
                                out=DTm[:, qs], mask=masks8[:, 7 + lv, :], data=ZTp[:]), [ZTp.b, masks8.b], [DTm.bs[q]])
                    _chk(4)
                    for q in range(4):
                        qs = slice(q * 512, (q + 1) * 512)
                        for i in range(4):
                            cs = slice((q * 4 + i) * 128, (q * 4 + i + 1) * 128)
                            K.op("pe", lambda e, q=q, i=i, cs=cs: e.matmul(
                                B[q][:, i * 128:(i + 1) * 128], lhsT=DTm[:, cs], rhs=vtok[:, cs], start=True, stop=True),
                                 [DTm.bs[q], vtok.b], [B[q].b], sig=(i == 3))
                        K.op("dve", lambda e, q=q, qs=qs, P=P, h=h: e.tensor_tensor(
                            out=v3(P["ub"][:, qs]), in0=v3(B[q][:]),
                            in1=col(1, h)[:, q * 4:(q + 1) * 4, :].to_broadcast([128, 4, 128]), op=ALU.mult),
                             [B[q].b, tk.b], P["ub"].bs[q * 4:q * 4 + 4])
                    for q in range(4):
                        qs = slice(q * 512, (q + 1) * 512)
                        for i in range(4):
                            cs = slice((q * 4 + i) * 128, (q * 4 + i + 1) * 128)
                            K.op("pe", lambda e, q=q, i=i, cs=cs: e.matmul(
                                B[4 + q][:, i * 128:(i + 1) * 128], lhsT=kg[:, cs], rhs=DTm[:, cs], start=True, stop=True),
                                 [DTm.bs[q], kg.b], [B[4 + q].b], sig=(i == 3))
                        K.op("act", lambda e, q=q, qs=qs, P=P: e.copy(out=P["wT"][:, qs], in_=B[4 + q][:]),
                             [B[4 + q].b], P["wT"].bs[q * 4:q * 4 + 4])
                    K.op("pool", lambda e, P=P: e.memset(P["S32"][:], 0.0), [], [P["S32"].b])
                    K.op("pool", lambda e, P=P: e.memset(P["Sbf"][:], 0.0), [], [P["Sbf"].b])
                _chk(5)
                for c in range(NT):
                    cs = slice(c * 128, (c + 1) * 128)
                    par = (c % 2) * 3
                    Bw, Bo, Bs = B[par], B[par + 1], B[par + 2]
                    for hi, h in enumerate(heads):
                        P = per[h]
                        ss = slice(hi * 128, (hi + 1) * 128)
                        K.op("pe", lambda e, P=P, ss=ss, cs=cs, Bw=Bw: e.matmul(Bw[:, ss], lhsT=P["wT"][:, cs], rhs=P["Sbf"][:],
                                                                               start=True, stop=True),
                             [P["wT"].bs[c], P["Sbf"].b], [Bw.b], sig=(hi == 3))
                    for hi, h in enumerate(heads):
                        P = per[h]
                        ss = slice(hi * 128, (hi + 1) * 128)
                        idx = c * 8 + h
                        vn = P["vn"][c % 2]
                        K.op("dve", lambda e, P=P, ss=ss, cs=cs, idx=idx, vn=vn, Bw=Bw: e.scalar_tensor_tensor(
                            out=vn[:], in0=Bw[:, ss], scalar=tk[:, 2, idx:idx + 1], in1=P["ub"][:, cs],
                            op0=ALU.mult, op1=ALU.add), [Bw.b, tk.b, P["ub"].bs[c]], [vn.b])
                    for hi, h in enumerate(heads):
                        P = per[h]
                        ss = slice(hi * 128, (hi + 1) * 128)
                        vn = P["vn"][c % 2]
                        K.op("pe", lambda e, P=P, ss=ss, cs=cs, Bo=Bo: e.matmul(Bo[:, ss], lhsT=P["Sbf"][:], rhs=P["qg"][:, cs],
                                                                               start=True, stop=False),
                             [P["Sbf"].b, P["qg"].bs[c]], [Bo.b], sig=False)
                        K.op("pe", lambda e, P=P, ss=ss, cs=cs, vn=vn, Bo=Bo: e.matmul(Bo[:, ss], lhsT=vn[:], rhs=P["AT"][:, cs],
                                                                                      start=False, stop=True),
                             [vn.b, P["AT"].bs[c]], [Bo.b], sig=(hi == 3))
                    for hi, h in enumerate(heads):
                        P = per[h]
                        ss = slice(hi * 128, (hi + 1) * 128)
                        vn = P["vn"][c % 2]
                        K.op("pe", lambda e, P=P, ss=ss, cs=cs, vn=vn, Bs=Bs: e.matmul(Bs[:, ss], lhsT=P["kd"][:, cs], rhs=vn[:],
                                                                                      start=True, stop=True),
                             [vn.b, P["kd"].bs[c]], [Bs.b], sig=(hi == 3))
                    for hi, h in enumerate(heads):
                        P = per[h]
                        ss = slice(hi * 128, (hi + 1) * 128)
                        idx = c * 8 + h
                        K.op("dve", lambda e, P=P, ss=ss, idx=idx, Bs=Bs: e.scalar_tensor_tensor(
                            out=P["S32"][:], in0=P["S32"][:], scalar=tk[:, 5, idx:idx + 1], in1=Bs[:, ss],
                            op0=ALU.mult, op1=ALU.add), [P["S32"].b, tk.b, Bs.b], [P["S32"].b])
                        K.op("act", lambda e, P=P: e.copy(out=P["Sbf"][:], in_=P["S32"][:]), [P["S32"].b], [P["Sbf"].b])
                    for hi, h in enumerate(heads):
                        P = per[h]
                        ss = slice(hi * 128, (hi + 1) * 128)
                        K.op("act", lambda e, P=P, ss=ss, cs=cs, Bo=Bo: e.copy(out=P["oT"][:, cs], in_=Bo[:, ss]),
                             [Bo.b], [P["oT"].bs[c]])
                _chk(6)
                for h in heads:
                    P = per[h]
                    K.dma("sp", zs[:], scr["gz"][h], [scr["gz_b"]], [zs.b], zs.b)
                    K.op("act", lambda e, P=P: e.activation(out=dU[:], in_=P["oT"][:], func=AF.Square), P["oT"].bs, [dU.b])
                    for tc in range(4):
                        ts_ = slice(tc * 512, (tc + 1) * 512)
                        Bn = B[6 + tc % 2]
                        K.op("pe", lambda e, ts_=ts_, Bn=Bn: e.matmul(Bn[:], lhsT=ones_bf[:], rhs=dU[:, ts_], start=True, stop=True),
                             [ones_bf.b, dU.b], [Bn.b])
                        K.op("act", lambda e, ts_=ts_, Bn=Bn: e.activation(
                            out=E[:, ts_], in_=Bn[:], func=AF.Ln, scale=1.0 / 128.0,
                            bias=consts["bias"][:, 3:4]), [Bn.b, consts["bias"].b], [E.b])
                    K.op("act", lambda e: e.activation(out=E[:], in_=E[:], func=AF.Exp, scale=-0.5), [E.b], [E.b])
                    K.op("dve", lambda e, P=P: e.scalar_tensor_tensor(out=LnT[:], in0=P["oT"][:], scalar=ogain[:, 0:1], in1=E[:],
                                                                      op0=ALU.mult, op1=ALU.mult),
                         P["oT"].bs + [ogain.b, E.b], LnT.bs)
                    K.op("dve", lambda e: e.tensor_tensor(out=Dm[:], in0=LnT[:], in1=zs[:], op=ALU.mult),
                         LnT.bs + [zs.b], Dm.bs)
                    K.dma("sp", scr["mix"][h], Dm[:], Dm.bs, [scr["mix_b"]], Dm.bs[0])
                K.barrier()


def phase_sb(C, l, ins, consts, scr):
    K, nc = C.K, C.nc
    ones_bf = consts["ones_bf"]
    with ExitStack() as es:
        qT = [C.sb(es, f"sqT{i}", [128, S], BF16) for i in range(2)]
        kT = [C.sb(es, f"skT{i}", [128, S], BF16) for i in range(2)]
        vt = [C.sb(es, f"svt{i}", [128, NT, 128], BF16) for i in range(2)]
        e1 = [C.sb(es, f"e1_{i}", [128, 512], F32) for i in range(2)]
        sp = [C.sb(es, f"sp{i}", [128, 512], BF16) for i in range(2)]
        tmp = [C.sb(es, f"tmp{i}", [128, 512], F32) for i in range(2)]
        aa = [C.sb(es, f"aa{i}", [128, 512], F32) for i in range(2)]
        ww = [C.sb(es, f"ww{i}", [128, 512], BF16) for i in range(2)]
        Rp = C.sb(es, "Rp", [128, 512], F32)
        yraw = [C.sb(es, f"yraw{i}", [128, S], F32) for i in range(2)]
        ysq = C.sb(es, "ysq", [128, S], BF16)
        rn = C.sb(es, "rn", [128, S], F32)
        ymx = [C.sb(es, f"ymx{i}", [128, S], BF16) for i in range(2)]
        msk = C.sb(es, "msk", [128, 4, 512], BF16)
        uin = C.sb(es, "uin", [128, 128], BF16)
        og = C.sb(es, "og", [128, 1], F32)
        Bz = [C.ps(es, f"Bz{i}", [128, 512], F32) for i in range(2)]
        Bg = [C.ps(es, f"Bg{i}", [128, 512], F32) for i in range(2)]
        Bt = [C.ps(es, f"Bt{i}", [128, 512], F32) for i in range(2)]
        Bo = [C.ps(es, f"Bo{i}", [128, 512], F32) for i in range(2)]
        onesw = C.sb(es, "onesw", [128, 512], F32)
        K.op("pool", lambda e: e.memset(onesw[:], 1.0), [], [onesw.b])
        for k in range(4):
            K.op("pool", lambda e, k=k: e.affine_select(out=msk[:, k, :], in_=onesw[:], pattern=[[1, 512]],
                                                        compare_op=ALU.is_gt, fill=0.0, base=-128 * k, channel_multiplier=-1),
                 [onesw.b], [msk.b])
        K.op("pool", lambda e: e.affine_select(out=uin[:], in_=onesw[:, 0:128], pattern=[[-1, 128]], compare_op=ALU.is_ge,
                                               fill=0.0, base=0, channel_multiplier=1), [onesw.b], [uin.b])
        K.dma("sp", og[:], ins["sb_o_norm"][l, :].rearrange("(p o) -> p o", o=1), [], [og.b], og.b)

        blocks = []
        for h in range(H):
            for tc in range(4):
                nb = 4 * tc + 4
                for b in range(nb - 1, -1, -1):
                    blocks.append((h, tc, b, b == nb - 1, b == 0))

        def load(h):
            i = h % 2
            K.dma("sp", qT[i][:], scr["sq"][h], [scr["sq_b"]], [qT[i].b], qT[i].b)
            K.dma("sp", kT[i][:], scr["sk"][h], [scr["sk_b"]], [kT[i].b], kT[i].b)
            K.dma("sp", vt[i][:], scr["sv"][:, :, h * 128:(h + 1) * 128].rearrange("t p d -> p t d"),
                  [scr["sv_b"]], [vt[i].b], vt[i].b)

        def stageA(n):
            h, tc, b, first, last = blocks[n]
            i, j = h % 2, n % 2
            ts_ = slice(tc * 512, (tc + 1) * 512)
            K.op("pe", lambda e: e.matmul(Bz[j][:], lhsT=kT[i][:, b * 128:(b + 1) * 128], rhs=qT[i][:, ts_],
                                          start=True, stop=True), [kT[i].b, qT[i].b], [Bz[j].b])
            K.op("act", lambda e: e.activation(out=e1[j][:], in_=Bz[j][:], func=AF.Exp), [Bz[j].b], [e1[j].b])
            K.op("act", lambda e: e.activation(out=sp[j][:], in_=e1[j][:], func=AF.Ln, bias=consts["bias"][:, 4:5]),
                 [e1[j].b, consts["bias"].b], [sp[j].b])
            k = b - 4 * tc
            if k >= 0:
                K.op("pool", lambda e: e.tensor_tensor(out=sp[j][:], in0=sp[j][:], in1=msk[:, k, :], op=ALU.mult),
                     [sp[j].b, msk.b], [sp[j].b])
            K.op("pe", lambda e: e.matmul(Bg[j][:], lhsT=uin[:], rhs=sp[j][:], start=True, stop=True),
                 [uin.b, sp[j].b], [Bg[j].b])
            K.op("pe", lambda e: e.matmul(Bt[j][:], lhsT=ones_bf[:], rhs=sp[j][:], start=True, stop=True),
                 [ones_bf.b, sp[j].b], [Bt[j].b])

        def stageB(n):
            h, tc, b, first, last = blocks[n]
            i, j = h % 2, n % 2
            o_ = Bo[(h * 4 + tc) % 2]
            if first:
                K.op("pool", lambda e: e.memset(Rp[:], 0.0), [], [Rp.b])
            K.op("dve", lambda e: e.tensor_tensor(out=tmp[j][:], in0=Bg[j][:], in1=Rp[:], op=ALU.add),
                 [Bg[j].b, Rp.b], [tmp[j].b])
            K.op("dve", lambda e: e.tensor_tensor(out=aa[j][:], in0=Bz[j][:], in1=tmp[j][:], op=ALU.subtract),
                 [Bz[j].b, tmp[j].b], [aa[j].b])
            K.op("act", lambda e: e.activation(out=ww[j][:], in_=aa[j][:], func=AF.Exp), [aa[j].b], [ww[j].b])
            k = b - 4 * tc
            if k >= 0:
                K.op("pool", lambda e: e.tensor_tensor(out=ww[j][:], in0=ww[j][:], in1=msk[:, k, :], op=ALU.mult),
                     [ww[j].b, msk.b], [ww[j].b])
            if not last:
                K.op("dve", lambda e: e.tensor_tensor(out=Rp[:], in0=Rp[:], in1=Bt[j][:], op=ALU.add),
                     [Rp.b, Bt[j].b], [Rp.b])
            K.op("pe", lambda e: e.matmul(o_[:], lhsT=vt[i][:, b, :], rhs=ww[j][:], start=first, stop=last),
                 [vt[i].b, ww[j].b], [o_.b], sig=last)
            if last:
                y_ = yraw[h % 2]
                K.op("act", lambda e: e.copy(out=y_[:, tc * 512:(tc + 1) * 512], in_=o_[:]), [o_.b], [y_.b])
                if tc == 3:
                    finish(h)

        def finish(h):
            y_, m_ = yraw[h % 2], ymx[h % 2]
            K.op("act", lambda e: e.activation(out=ysq[:], in_=y_[:], func=AF.Square), [y_.b], [ysq.b])
            for tc in range(4):
                ts_ = slice(tc * 512, (tc + 1) * 512)
                p_ = Bo[tc % 2]
                K.op("pe", lambda e, p_=p_, ts_=ts_: e.matmul(p_[:], lhsT=ones_bf[:], rhs=ysq[:, ts_], start=True, stop=True),
                     [ones_bf.b, ysq.b], [p_.b])
                K.op("act", lambda e, p_=p_, ts_=ts_: e.activation(out=rn[:, ts_], in_=p_[:], func=AF.Ln, scale=1.0 / 128.0,
                                                                   bias=consts["bias"][:, 3:4]), [p_.b, consts["bias"].b], [rn.b])
            K.op("act", lambda e: e.activation(out=rn[:], in_=rn[:], func=AF.Exp, scale=-0.5), [rn.b], [rn.b])
            K.op("dve", lambda e: e.scalar_tensor_tensor(out=m_[:], in0=y_[:], scalar=og[:, 0:1], in1=rn[:],
                                                         op0=ALU.mult, op1=ALU.mult), [y_.b, og.b, rn.b], [m_.b])
            K.dma("sp", scr["mix"][8 + h], m_[:], [m_.b], [scr["mix_b"]], m_.b)

        load(0)
        for n in range(len(blocks)):
            h, tc, b, first, last = blocks[n]
            stageA(n)
            if n > 0:
                stageB(n - 1)
            if first and tc == 0 and h + 1 < H:
                load(h + 1)
        stageB(len(blocks) - 1)
        K.barrier()


def phase_outproj(C, l, x_in, x_in_buf, x_out, x_out_buf, ins, consts, scr):
    K, nc = C.K, C.nc
    with ExitStack() as es:
        mx = C.sb(es, "mx", [128, KC, S], BF16)
        wo = C.sb(es, "wo", [128, KC, D], BF16, nbufs=4)
        xt = [C.sb(es, f"xo_in{i}", [128, D], F32) for i in range(2)]
        xo = [C.sb(es, f"xo_out{i}", [128, D], F32) for i in range(2)]
        pp = [C.ps(es, f"po{i}", [128, 512], F32) for i in range(8)]
        for kc in range(KC):
            K.dma("sp", mx[:, kc, :], scr["mix"][kc], [scr["mix_b"]], [mx.b], mx.b, grp=True)
        alloc_stage(C, es)
        for cq in range(4):
            load_w(C, wo, ins["w_out"][l], cq * 512, 512, cq * 512, wbuf=wo.bs[cq])
        n = 0
        for t in range(NT):
            x_, o_ = xt[t % 2], xo[t % 2]
            K.dma("sp", x_[:], x_in[t * 128:(t + 1) * 128, :], [x_in_buf], [x_.b], x_.b)
            for cq in range(4):
                p_ = pp[n % 8]
                n += 1
                for kc in range(KC):
                    K.op("pe", lambda e, p_=p_, kc=kc, t=t, cq=cq: e.matmul(
                        p_[:], lhsT=mx[:, kc, t * 128:(t + 1) * 128], rhs=wo[:, kc, cq * 512:(cq + 1) * 512],
                        start=(kc == 0), stop=(kc == KC - 1)), [mx.b, wo.bs[cq]], [p_.b], sig=(kc == KC - 1))
                K.op("dve", lambda e, p_=p_, x_=x_, o_=o_, cq=cq: e.tensor_tensor(
                    out=o_[:, cq * 512:(cq + 1) * 512], in0=p_[:], in1=x_[:, cq * 512:(cq + 1) * 512], op=ALU.add),
                     [p_.b, x_.b], [o_.b])
            K.dma("sp", x_out[t * 128:(t + 1) * 128, :], o_[:], [o_.b], [x_out_buf], o_.b)
        K.barrier()


def make_consts(C, es):
    K, nc = C.K, C.nc
    c = {}
    c["neghalf"] = C.sb(es, "neghalf", [128, 1], F32)
    K.op("pool", lambda e: e.memset(c["neghalf"][:], -0.5), [], [c["neghalf"].b])
    onesf = C.sb(es, "onesf", [128, 128], F32)
    K.op("pool", lambda e: e.memset(onesf[:], 1.0), [], [onesf.b])
    identf = C.sb(es, "identf", [128, 128], F32)
    K.op("pool", lambda e: e.affine_select(out=identf[:], in_=onesf[:], pattern=[[1, 128]],
                                           compare_op=ALU.is_equal, fill=0.0, base=0, channel_multiplier=-1),
         [onesf.b], [identf.b])
    c["ident"] = C.sb(es, "ident", [128, 128], BF16)
    K.op("pool", lambda e: e.tensor_copy(out=c["ident"][:], in_=identf[:]), [identf.b], [c["ident"].b])
    c["onesf"] = onesf
    c["identf"] = identf
    c["ones_bf"] = C.sb(es, "ones_bf", [128, 128], BF16)
    K.op("pool", lambda e: e.memset(c["ones_bf"][:], 1.0), [], [c["ones_bf"].b])
    c["bias"] = C.sb(es, "biasc", [128, 8], F32)
    for i, v in enumerate([128.0 * L2_EPS, L2_EPS, 128.0 * RMS_EPS, RMS_EPS, 1.0]):
        K.op("pool", lambda e, i=i, v=v: e.memset(c["bias"][:, i:i + 1], v), [], [c["bias"].b])
    return c


PARAMS = [("attn_norm", [DEPTH, D]), ("w_in", [DEPTH, D, IN_COLS]), ("gdn_conv", [DEPTH, 4, 3072]),
          ("gdn_a_log", [DEPTH, H]), ("gdn_dt_bias", [DEPTH, H]), ("gdn_o_norm", [DEPTH, HD]),
          ("sb_q_norm", [DEPTH, HD]), ("sb_k_norm", [DEPTH, HD]), ("sb_o_norm", [DEPTH, HD]),
          ("w_out", [DEPTH, D, D]), ("ffn_norm", [DEPTH, D]), ("w_up", [DEPTH, D, 2 * D_FF]),
          ("ffn_conv", [DEPTH, 3, 2 * D_FF]), ("ffn_conv_bias", [DEPTH, 2 * D_FF]),
          ("w_down", [DEPTH, D_FF, D])]


def build(mode="full"):
    nc = bass.Bass("TRN2", target_bir_lowering=False)
    only = mode.startswith("only")
    small = {"attn_norm", "gdn_conv", "gdn_a_log", "gdn_dt_bias", "gdn_o_norm", "sb_q_norm", "sb_k_norm", "sb_o_norm",
             "ffn_norm", "ffn_conv", "ffn_conv_bias"}
    ins = {"x": nc.dram_tensor("x", [S, D], F32, kind="ExternalInput").ap()}
    for name, shape in PARAMS:
        if only and name not in small:
            continue
        ins[name] = nc.dram_tensor(name, shape, F32, kind="ExternalInput").ap()
    y = nc.dram_tensor("y", [S, D], F32, kind="ExternalOutput").ap()
    dbg = "ExternalOutput" if mode.startswith("dbg") else "Internal"
    sin = "ExternalInput" if only else dbg
    scratch = {}
    for nm in ("gq", "gk", "gv", "gz", "sq", "sk"):
        scratch[nm] = nc.dram_tensor(nm + "_scr", [H, 128, S], BF16, kind=sin).ap()
        scratch[nm + "_b"] = Buf(nm + "_scr")
    scratch["sv"] = nc.dram_tensor("sv_scr", [NT, 128, 1024], BF16, kind=sin).ap()
    scratch["sv_b"] = Buf("sv_scr")
    for nm in ("beta", "gcum"):
        scratch[nm] = nc.dram_tensor(nm + "_scr", [H, S], F32, kind=sin).ap()
        scratch[nm + "_b"] = Buf(nm + "_scr")
    scratch["mix"] = nc.dram_tensor("mix_scr", [16, 128, S], BF16, kind=("ExternalOutput" if only else dbg)).ap()
    scratch["mix_b"] = Buf("mix_scr")
    scratch.update({
        "a": nc.dram_tensor("a_scr", [NT, 128, NFF, 128], BF16).ap(),
        "a_buf": Buf("a_scr"),
        "x1": nc.dram_tensor("x1_scr", [S, D], F32).ap(),
        "x2": nc.dram_tensor("x2_scr", [S, D], F32).ap(),
    })
    K = Sched(nc)
    with ExitStack() as es:
        C = Ctx(nc, K, es)
        consts = make_consts(C, es)
        xb, yb = Buf("xin"), Buf("yout")
        if mode == "ffn0":
            phase_ffn(C, 0, ins["x"], xb, y, yb, ins, consts, scratch)
        if mode in ("dbg_inproj", "dbg_gdn"):
            phase_inproj(C, 0, ins["x"], xb, ins, consts, scratch)
        if mode in ("dbg_gdn", "only_gdn"):
            phase_gdn(C, 0, ins, consts, scratch)
        if mode == "only_sb":
            phase_sb(C, 0, ins, consts, scratch)
        if mode == "full":
            x1b, x2b = Buf("x1s"), Buf("x2s")
            cur, curb = ins["x"], xb
            for l in range(DEPTH):
                K.scope = f"inproj{l}"
                phase_inproj(C, l, cur, curb, ins, consts, scratch)
                K.scope = f"gdn{l}"
                phase_gdn(C, l, ins, consts, scratch)
                K.scope = f"sb{l}"
                phase_sb(C, l, ins, consts, scratch)
                K.scope = f"outproj{l}"
                phase_outproj(C, l, cur, curb, scratch["x1"], x1b, ins, consts, scratch)
                K.scope = f"ffn{l}"
                if l == DEPTH - 1:
                    phase_ffn(C, l, scratch["x1"], x1b, y, yb, ins, consts, scratch)
                else:
                    phase_ffn(C, l, scratch["x1"], x1b, scratch["x2"], x2b, ins, consts, scratch)
                    cur, curb = scratch["x2"], x2b
        K.barrier()
        with nc.Block() as block:
            @block.tensor
            def _(e):
                K.emit("pe", e)

            @block.scalar
            def _(e):
                K.emit("act", e)

            @block.vector
            def _(e):
                K.emit("dve", e)

            @block.gpsimd
            def _(e):
                K.emit("pool", e)

            @block.sync
            def _(e):
                K.emit("sp", e)
    return nc, K


def kernel(**inputs):
    nc, K = build("full")
    x = np.ascontiguousarray(inputs["x"], dtype=np.float32)
    in_maps = []
    for c in range(8):
        m = {"x": x[c]}
        for name, _ in PARAMS:
            m[name] = np.ascontiguousarray(inputs[name], dtype=np.float32)
        in_maps.append(m)
    res = run_bass_kernel_spmd(nc, in_maps, core_ids=list(range(8)))
    return np.stack([r["y"] for r in res.results], axis=0)
```

```python
import numpy as np
from contextlib import ExitStack
import concourse.bass as bass
import concourse.mybir as mybir
from concourse.bass_utils import run_bass_kernel_spmd

F32 = mybir.dt.float32
BF16 = mybir.dt.bfloat16
AF = mybir.ActivationFunctionType
ALU = mybir.AluOpType
AX = mybir.AxisListType

D = 2048
S = 2048
DEPTH = 2
NT = S // 128
KC = D // 128
H = 8
HD = 128
IN_COLS = 7184
D_FF = 5632
NFF = D_FF // 128
RMS_EPS = 1e-6
L2_EPS = 1e-6

ENGS = ["pe", "act", "dve", "pool", "sp"]
SCOPES = False


class Buf:
    __slots__ = ("name", "w", "r", "dsem", "dcount")

    def __init__(self, name):
        self.name = name
        self.w = None
        self.r = {}
        self.dsem = None
        self.dcount = 0


class Sched:
    def __init__(self, nc):
        self.nc = nc
        self.sem = {e: nc.alloc_semaphore("cnt_" + e) for e in ENGS}
        self.count = {e: 0 for e in ENGS}
        self.known = {e: {} for e in ENGS}
        self.ops = {e: [] for e in ENGS}
        self.dma_bufs = []
        self.nwaits = 0
        self.muted = False
        self.free_sems = []
        self.nsem = 0
        self.scope = "init"

    def _deps(self, eng, reads, writes, skip_sem=None):
        waits = {}
        known = self.known[eng]
        own = self.sem[eng]

        def need(sem, val):
            if eng == "pe" and sem is own:
                return
            if known.get(sem, 0) >= val:
                return
            if waits.get(sem, 0) < val:
                waits[sem] = val

        for b in reads:
            if b.w is not None:
                need(*b.w)
        for b in writes:
            if b.w is not None and b.w[0] is not skip_sem:
                need(*b.w)
            for s, v in b.r.items():
                need(s, v)
        for s, v in waits.items():
            known[s] = v
        self.nwaits += len(waits)
        return list(waits.items())

    def _commit(self, ev, reads, writes):
        for b in reads:
            if b.r.get(ev[0], 0) < ev[1]:
                b.r[ev[0]] = ev[1]
        for b in writes:
            b.w = ev
            b.r = {}

    def op(self, eng, fn, reads=(), writes=(), sig=True):
        if self.muted:
            return
        waits = self._deps(eng, reads, writes)
        if sig:
            self.count[eng] += 1
            ev = (self.sem[eng], self.count[eng])
            inc = (self.sem[eng], 1)
        else:
            assert eng == "pe"
            ev = (self.sem[eng], self.count[eng] + 1)
            inc = None
        self.ops[eng].append((waits, fn, inc, self.scope))
        self._commit(ev, reads, writes)

    def dma(self, q, out, in_, reads, writes, owner, grp=False):
        if self.muted:
            return
        if owner.dsem is None:
            if self.free_sems:
                owner.dsem, owner.dcount = self.free_sems.pop()
            else:
                owner.dsem = self.nc.alloc_semaphore(f"dsem{self.nsem}")
                owner.dcount = 0
                self.nsem += 1
            self.dma_bufs.append(owner)
        waits = self._deps(q, reads, writes, owner.dsem if grp else None)
        owner.dcount += 16
        ev = (owner.dsem, owner.dcount)
        self.ops[q].append((waits, (lambda e, o=out, i=in_: e.dma_start(out=o, in_=i)), (owner.dsem, 16), self.scope))
        self._commit(ev, reads, writes)

    def barrier(self):
        if self.muted:
            return
        evs = [(self.sem[e], self.count[e]) for e in ENGS if self.count[e] > 0]
        evs += [(b.dsem, b.dcount) for b in self.dma_bufs]
        for e in ENGS:
            waits = []
            for s, v in evs:
                if e == "pe" and s is self.sem["pe"]:
                    continue
                if s is self.sem[e]:
                    continue
                if self.known[e].get(s, 0) < v:
                    self.known[e][s] = v
                    waits.append((s, v))
            if waits:
                self.ops[e].append((waits, None, None, self.scope))
        for b in self.dma_bufs:
            self.free_sems.append((b.dsem, b.dcount))
            b.dsem = None
        self.dma_bufs = []

    def emit(self, ename, eng):
        cur, cm = None, None
        for waits, fn, inc, scope in self.ops[ename]:
            if SCOPES and scope != cur:
                if cm is not None:
                    cm.__exit__(None, None, None)
                cm = self.nc.named_scope(scope)
                cm.__enter__()
                cur = scope
            for s, v in waits:
                eng.wait_ge(s, v)
            if fn is None:
                continue
            ins = fn(eng)
            if inc is not None:
                ins.then_inc(inc[0], inc[1])
        if cm is not None:
            cm.__exit__(None, None, None)


class T:
    def __init__(self, h, name, nbufs=1):
        self.h = h
        self.b = Buf(name)
        self.bs = [Buf(f"{name}_{i}") for i in range(nbufs)] if nbufs > 1 else [self.b]

    def __getitem__(self, k):
        return self.h[k]


class Ctx:
    def __init__(self, nc, K, es):
        self.nc, self.K, self.es = nc, K, es
        self.uid = 0

    def sb(self, es, name, shape, dt, nbufs=1):
        self.uid += 1
        nm = f"{name}_{self.uid}"
        return T(es.enter_context(self.nc.sbuf_tensor(nm, list(shape), dt)), nm, nbufs)

    def ps(self, es, name, shape, dt=F32, nbufs=1):
        self.uid += 1
        nm = f"{name}_{self.uid}"
        return T(es.enter_context(self.nc.psum_tensor(nm, list(shape), dt)), nm, nbufs)


def phase_norm(C, x_ap, x_buf, gain_row_ap, hT, consts):
    K, nc = C.K, C.nc
    with ExitStack() as es:
        gbc = C.sb(es, "gbc", [128, D], F32)
        xt = [C.sb(es, f"xt{i}", [128, D], F32) for i in range(2)]
        hs = [C.sb(es, f"hs{i}", [128, D], BF16) for i in range(2)]
        junk = C.sb(es, "junk", [128, D], BF16)
        st = [C.sb(es, f"st{i}", [128, 4], F32) for i in range(2)]
        pt = [C.ps(es, f"pt{i}", [128, D], BF16) for i in range(2)]
        K.dma("sp", gbc[:], gain_row_ap.partition_broadcast(128), [], [gbc.b], gbc.b)
        for t in range(NT):
            i = t % 2
            x_, h_, s_, p_ = xt[i], hs[i], st[i], pt[i]
            K.dma("sp", x_[:], x_ap[t * 128:(t + 1) * 128, :], [x_buf], [x_.b], x_.b)
            K.op("act", lambda e, x_=x_, s_=s_: e.activation(out=junk[:], in_=x_[:], func=AF.Square,
                                                           accum_out=s_[:, 0:1]),
                 [x_.b], [s_.b])
            K.op("dve", lambda e, s_=s_: e.tensor_scalar(out=s_[:, 1:2], in0=s_[:, 0:1], scalar1=1.0 / D,
                                                       scalar2=RMS_EPS, op0=ALU.mult, op1=ALU.add),
                 [s_.b], [s_.b])
            K.op("pool", lambda e, s_=s_: e.tensor_tensor(out=s_[:, 2:3], in0=s_[:, 1:2],
                                                        in1=consts["neghalf"][:, 0:1], op=ALU.pow),
                 [s_.b, consts["neghalf"].b], [s_.b])
            K.op("dve", lambda e, x_=x_, h_=h_, s_=s_: e.scalar_tensor_tensor(
                out=h_[:], in0=x_[:], scalar=s_[:, 2:3], in1=gbc[:], op0=ALU.mult, op1=ALU.mult),
                 [x_.b, s_.b, gbc.b], [h_.b])
            for kc in range(KC):
                K.op("pe", lambda e, h_=h_, p_=p_, kc=kc: e.transpose(
                    out=p_[:, kc * 128:(kc + 1) * 128], in_=h_[:, kc * 128:(kc + 1) * 128],
                    identity=consts["ident"][:]),
                     [h_.b, consts["ident"].b], [p_.b], sig=(kc == KC - 1))
            K.op("act", lambda e, p_=p_, t=t: e.copy(
                out=hT[:, :, t * 128:(t + 1) * 128],
                in_=p_[:].rearrange("p (k n) -> p k n", n=128)),
                 [p_.b], [hT.bs[t // 4]])
        K.barrier()


def alloc_stage(C, es, n=3):
    C.stg = [C.sb(es, f"stg{i}", [128, 4, 512], F32) for i in range(n)]
    C.stg_i = 0


def load_w(C, slot, w_ap, c0, ncols, col_off=0, nk=KC, wbuf=None):
    K = C.K
    wbuf = wbuf if wbuf is not None else slot.b
    step = 4
    for k0 in range(0, nk, step):
        kk = min(step, nk - k0)
        st = C.stg[C.stg_i % len(C.stg)]
        C.stg_i += 1
        src = w_ap[k0 * 128:(k0 + kk) * 128, c0:c0 + ncols].rearrange("(kc p) n -> p kc n", p=128)
        K.dma("sp", st[:, 0:kk, 0:ncols], src, [], [st.b], st.b)
        K.op("dve", lambda e, st=st, k0=k0, kk=kk: e.tensor_copy(
            out=slot[:, k0:k0 + kk, col_off:col_off + ncols], in_=st[:, 0:kk, 0:ncols]), [st.b], [wbuf])


def phase_ffn(C, l, x_in, x_in_buf, x_out, x_out_buf, ins, consts, scratch):
    K, nc = C.K, C.nc
    a_scr, a_buf = scratch["a"], scratch["a_buf"]
    with ExitStack() as es:
        hT = C.sb(es, "hT", [128, KC, S], BF16, nbufs=4)
        wsl = [C.sb(es, f"wup{i}", [128, KC, 512], BF16) for i in range(3)]
        alloc_stage(C, es, 2)
        for g in range(2):
            load_w(C, wsl[g], ins["w_up"][l], g * 256, 256, 0)
            load_w(C, wsl[g], ins["w_up"][l], D_FF + g * 256, 256, 256)
        phase_norm(C, x_in, x_in_buf, ins["ffn_norm"][l:l + 1, :], hT, consts)
        with ExitStack() as es2:
            cw = C.sb(es2, "cw", [128, 4, 2 * NFF], F32)
            raw = [C.sb(es2, f"raw{i}", [128, 2 + S], F32) for i in range(4)]
            acc = [C.sb(es2, f"acc{i}", [128, S], F32) for i in range(2)]
            sg = C.sb(es2, "sg", [128, S], F32)
            aT = [C.sb(es2, f"aT{i}", [128, NT, 2, 128], BF16) for i in range(2)]
            pp = [C.ps(es2, f"pp{i}", [128, 512], F32) for i in range(8)]
            cwr = C.sb(es2, "cwr", [2 * NFF, 4, 128], F32)
            for tap in range(3):
                K.dma("sp", cwr[:, tap, :], ins["ffn_conv"][l, tap, :].rearrange("(j p) -> j p", p=128),
                      [], [cwr.b], cwr.b)
            K.dma("sp", cwr[:, 3, :], ins["ffn_conv_bias"][l, :].rearrange("(j p) -> j p", p=128),
                  [], [cwr.b], cwr.b)
            for tap in range(4):
                K.op("pe", lambda e, tap=tap: e.transpose(out=pp[0][:, tap * 128:tap * 128 + 2 * NFF],
                                                          in_=cwr[:, tap, :], identity=consts["identf"][0:2 * NFF, 0:2 * NFF]),
                     [cwr.b, consts["identf"].b], [pp[0].b], sig=(tap == 3))
            K.op("act", lambda e: e.copy(out=cw[:], in_=pp[0][:].rearrange("p (t n) -> p t n", n=128)[:, :, 0:2 * NFF]),
                 [pp[0].b], [cw.b])
            for r in raw:
                K.op("pool", lambda e, r=r: e.memset(r[:, 0:2], 0.0), [], [r.b])
            ngrp = NFF // 2
            pidx = 0
            for g in range(ngrp):
                w_ = wsl[g % 3]
                if g + 2 < ngrp:
                    load_w(C, wsl[(g + 2) % 3], ins["w_up"][l], (g + 2) * 256, 256, 0)
                    load_w(C, wsl[(g + 2) % 3], ins["w_up"][l], D_FF + (g + 2) * 256, 256, 256)
                for jj in range(2):
                    j = g * 2 + jj
                    rg, ru = raw[(j % 2) * 2], raw[(j % 2) * 2 + 1]
                    for which, r_ in ((0, rg), (1, ru)):
                        co = which * 256 + jj * 128
                        for tc in range(4):
                            p_ = pp[pidx % 8]
                            pidx += 1
                            for kc in range(KC):
                                K.op("pe", lambda e, p_=p_, w_=w_, kc=kc, co=co, tc=tc: e.matmul(
                                    p_[:], lhsT=w_[:, kc, co:co + 128], rhs=hT[:, kc, tc * 512:(tc + 1) * 512],
                                    start=(kc == 0), stop=(kc == KC - 1)),
                                     [w_.b, hT.bs[tc]], [p_.b], sig=(kc == KC - 1))
                            K.op("act", lambda e, p_=p_, r_=r_, tc=tc: e.copy(
                                out=r_[:, 2 + tc * 512:2 + (tc + 1) * 512], in_=p_[:]), [p_.b], [r_.b])
                    a_ = aT[(j // 2) % 2]
                    for which, r_, ac in ((0, rg, acc[0]), (1, ru, acc[1])):
                        ch = which * NFF + j
                        K.op("act", lambda e, r_=r_, ac=ac, ch=ch: e.activation(
                            out=ac[:], in_=r_[:, 2:2 + S], func=AF.Identity,
                            scale=cw[:, 2, ch:ch + 1], bias=cw[:, 3, ch:ch + 1]), [r_.b, cw.b], [ac.b])
                        K.op("dve", lambda e, r_=r_, ac=ac, ch=ch: e.scalar_tensor_tensor(
                            out=ac[:], in0=r_[:, 1:1 + S], scalar=cw[:, 1, ch:ch + 1], in1=ac[:],
                            op0=ALU.mult, op1=ALU.add), [r_.b, cw.b, ac.b], [ac.b])
                        K.op("dve", lambda e, r_=r_, ac=ac, ch=ch: e.scalar_tensor_tensor(
                            out=ac[:], in0=r_[:, 0:S], scalar=cw[:, 0, ch:ch + 1], in1=ac[:],
                            op0=ALU.mult, op1=ALU.add), [r_.b, cw.b, ac.b], [ac.b])
                    K.op("act", lambda e: e.activation(out=sg[:], in_=acc[0][:], func=AF.Silu),
                         [acc[0].b], [sg.b])
                    K.op("dve", lambda e, a_=a_, j=j: e.tensor_tensor(out=a_[:, :, j % 2, :], in0=sg[:].rearrange('p (t n) -> p t n', n=128), in1=acc[1][:].rearrange('p (t n) -> p t n', n=128), op=ALU.mult),
                         [sg.b, acc[1].b], [a_.b])
                    if j % 2 == 1:
                        K.dma("sp", a_scr[:, :, j - 1:j + 1, :].rearrange("t p j n -> p t (j n)"),
                              a_[:].rearrange("p t j n -> p t (j n)"), [a_.b], [a_buf], a_.b)
        K.barrier()
    with ExitStack() as es:
        wd = [C.sb(es, f"wd{i}", [128, NFF, 512], BF16) for i in range(2)]
        alloc_stage(C, es)
        at = [C.sb(es, f"at{i}", [128, NFF, 128], BF16) for i in range(3)]
        xr = [C.sb(es, f"xr{i}", [128, 512], F32) for i in range(3)]
        xo = [C.sb(es, f"xo{i}", [128, 512], F32) for i in range(3)]
        pp = [C.ps(es, f"pd{i}", [128, 512], F32) for i in range(4)]
        n = 0
        load_w(C, wd[0], ins["w_down"][l], 0, 512, 0, nk=NFF)
        for q in range(4):
            w_ = wd[q % 2]
            if q + 1 < 4:
                load_w(C, wd[(q + 1) % 2], ins["w_down"][l], (q + 1) * 512, 512, 0, nk=NFF)
            for t in range(NT):
                a_, xr_, xo_, p_ = at[n % 3], xr[n % 3], xo[n % 3], pp[n % 4]
                n += 1
                K.dma("sp", a_[:], a_scr[t], [a_buf], [a_.b], a_.b)
                K.dma("sp", xr_[:], x_in[t * 128:(t + 1) * 128, q * 512:(q + 1) * 512], [x_in_buf], [xr_.b], xr_.b)
                for j in range(NFF):
                    K.op("pe", lambda e, p_=p_, a_=a_, w_=w_, j=j: e.matmul(
                        p_[:], lhsT=a_[:, j, :], rhs=w_[:, j, :], start=(j == 0), stop=(j == NFF - 1)),
                         [a_.b, w_.b], [p_.b], sig=(j == NFF - 1))
                K.op("dve", lambda e, p_=p_, xr_=xr_, xo_=xo_: e.tensor_tensor(
                    out=xo_[:], in0=p_[:], in1=xr_[:], op=ALU.add), [p_.b, xr_.b], [xo_.b])
                K.dma("sp", x_out[t * 128:(t + 1) * 128, q * 512:(q + 1) * 512], xo_[:], [xo_.b], [x_out_buf], xo_.b)
        K.barrier()


def phase_inproj(C, l, x_in, x_in_buf, ins, consts, scr):
    K, nc = C.K, C.nc
    W = ins["w_in"][l]
    with ExitStack() as es:
        hT = C.sb(es, "hT", [128, KC, S], BF16, nbufs=4)
        wsl = [C.sb(es, f"win{i}", [128, KC, 512], BF16) for i in range(3)]
        wba = C.sb(es, "wba", [128, KC, 16], BF16)
        alloc_stage(C, es, 2)
        load_w(C, wsl[0], W, 0, 512, 0)
        load_w(C, wsl[1], W, 512, 512, 0)
        phase_norm(C, x_in, x_in_buf, ins["attn_norm"][l:l + 1, :], hT, consts)
        raw = [C.sb(es, f"raw{i}", [128, 3 + S], F32) for i in range(2)]
        acc = [C.sb(es, f"acc{i}", [128, S], F32) for i in range(2)]
        ysq = C.sb(es, "ysq", [128, S], BF16)
        lnv = C.sb(es, "lnv", [128, S], F32)
        ob = [C.sb(es, f"ob{i}", [128, S], BF16) for i in range(2)]
        svt = [C.sb(es, f"svt{i}", [128, 512], BF16) for i in range(2)]
        gcw = C.sb(es, "gcw", [128, 4, 24], F32)
        gcr = C.sb(es, "gcr", [24, 4, 128], F32)
        nrm = C.sb(es, "nrm", [128, 2], F32)
        hp = C.sb(es, "hp", [8, 4], F32)
        pm = [C.ps(es, f"pm{i}", [128, 512], F32) for i in range(6)]
        po = [C.ps(es, f"po{i}", [128, 512], F32) for i in range(2)]
        ones_bf = consts["ones_bf"]
        for tap in range(4):
            K.dma("sp", gcr[:, tap, :], ins["gdn_conv"][l, tap, :].rearrange("(j p) -> j p", p=128), [], [gcr.b], gcr.b)
        for tap in range(4):
            K.op("pe", lambda e, tap=tap: e.transpose(out=pm[0][:, tap * 32:tap * 32 + 24], in_=gcr[:, tap, :],
                                                      identity=consts["identf"][0:24, 0:24]),
                 [gcr.b, consts["identf"].b], [pm[0].b], sig=(tap == 3))
        K.op("act", lambda e: e.copy(out=gcw[:], in_=pm[0][:, 0:128].rearrange("p (t n) -> p t n", n=32)[:, :, 0:24]),
             [pm[0].b], [gcw.b])
        K.dma("sp", nrm[:, 0:1], ins["sb_q_norm"][l, :].rearrange("(p o) -> p o", o=1), [], [nrm.b], nrm.b)
        K.dma("sp", nrm[:, 1:2], ins["sb_k_norm"][l, :].rearrange("(p o) -> p o", o=1), [], [nrm.b], nrm.b)
        K.dma("sp", hp[:, 0:1], ins["gdn_a_log"][l, :].rearrange("(p o) -> p o", o=1), [], [hp.b], hp.b)
        K.dma("sp", hp[:, 1:2], ins["gdn_dt_bias"][l, :].rearrange("(p o) -> p o", o=1), [], [hp.b], hp.b)
        K.op("act", lambda e: e.activation(out=hp[:, 2:3], in_=hp[:, 0:1], func=AF.Exp), [hp.b], [hp.b])
        K.op("dve", lambda e: e.tensor_scalar(out=hp[:, 2:3], in0=hp[:, 2:3], scalar1=-1.0, scalar2=None, op0=ALU.mult),
             [hp.b], [hp.b])
        for r in raw:
            K.op("pool", lambda e, r=r: e.memset(r[:, 0:3], 0.0), [], [r.b])
        st = {"pi": 0, "ti": 0}

        def mm_tile(w_, co, M=128):
            ps = []
            for tc in range(4):
                p_ = pm[st["pi"] % 6]
                st["pi"] += 1
                for kc in range(KC):
                    K.op("pe", lambda e, p_=p_, w_=w_, kc=kc, co=co, tc=tc, M=M: e.matmul(
                        p_[0:M, :], lhsT=w_[:, kc, co:co + M], rhs=hT[:, kc, tc * 512:(tc + 1) * 512],
                        start=(kc == 0), stop=(kc == KC - 1)),
                         [w_.b, hT.bs[tc]], [p_.b], sig=(kc == KC - 1))
                ps.append(p_)
            return ps

        def l2_or_rms(src, out_, scale, bias, gain_ap):
            K.op("act", lambda e: e.activation(out=ysq[:], in_=src[:, 0:S], func=AF.Square), [src.b], [ysq.b])
            for tc in range(4):
                p_ = po[tc % 2]
                K.op("pe", lambda e, p_=p_, tc=tc: e.matmul(p_[:], lhsT=ones_bf[:], rhs=ysq[:, tc * 512:(tc + 1) * 512],
                                                            start=True, stop=True), [ones_bf.b, ysq.b], [p_.b])
                K.op("act", lambda e, p_=p_, tc=tc: e.activation(out=lnv[:, tc * 512:(tc + 1) * 512], in_=p_[:], func=AF.Ln,
                                                                 scale=scale, bias=consts["bias"][:, bias:bias + 1]),
                     [p_.b, consts["bias"].b], [lnv.b])
            K.op("act", lambda e: e.activation(out=lnv[:], in_=lnv[:], func=AF.Exp, scale=-0.5), [lnv.b], [lnv.b])
            if gain_ap is None:
                K.op("dve", lambda e: e.tensor_tensor(out=out_[:], in0=src[:, 0:S], in1=lnv[:], op=ALU.mult),
                     [src.b, lnv.b], [out_.b])
            else:
                K.op("dve", lambda e: e.scalar_tensor_tensor(out=out_[:], in0=src[:, 0:S], scalar=gain_ap, in1=lnv[:],
                                                             op0=ALU.mult, op1=ALU.mult), [src.b, lnv.b, nrm.b], [out_.b])

        groups = []
        for kind, c0 in [("gq", 0), ("gk", 1024), ("gv", 2048), ("gz", 3072), ("sq", 4112), ("sk", 5136), ("sv", 6160)]:
            groups += [(kind, c0, 0), (kind, c0 + 512, 1)]
        for gi, (kind, c0, half) in enumerate(groups):
            w_ = wsl[gi % 3]
            if gi + 2 < len(groups):
                load_w(C, wsl[(gi + 2) % 3], W, groups[gi + 2][1], 512, 0)
            if kind == "gv" and half == 0:
                load_w(C, wba, W, 4096, 16, 0)
            if kind == "sv":
                for t in range(NT):
                    sv_ = svt[t % 2]
                    p_ = pm[st["pi"] % 6]
                    st["pi"] += 1
                    for kc in range(KC):
                        K.op("pe", lambda e, p_=p_, w_=w_, kc=kc, t=t: e.matmul(
                            p_[:], lhsT=hT[:, kc, t * 128:(t + 1) * 128], rhs=w_[:, kc, :],
                            start=(kc == 0), stop=(kc == KC - 1)),
                             [w_.b, hT.bs[t // 4]], [p_.b], sig=(kc == KC - 1))
                    K.op("act", lambda e, p_=p_, sv_=sv_: e.copy(out=sv_[:], in_=p_[:]), [p_.b], [sv_.b])
                    K.dma("sp", scr["sv"][t][:, half * 512:(half + 1) * 512], sv_[:], [sv_.b], [scr["sv_b"]], sv_.b)
                continue
            for hh in range(half * 4, half * 4 + 4):
                ti = st["ti"]
                st["ti"] += 1
                ps = mm_tile(w_, (hh % 4) * 128)
                r_, a_, o_ = raw[ti % 2], acc[ti % 2], ob[ti % 2]
                if kind in ("gq", "gk", "gv"):
                    ct = {"gq": 0, "gk": 8, "gv": 16}[kind] + hh
                    for tc in range(4):
                        K.op("act", lambda e, p_=ps[tc], r_=r_, tc=tc: e.copy(out=r_[:, 3 + tc * 512:3 + (tc + 1) * 512], in_=p_[:]),
                             [ps[tc].b], [r_.b])
                    K.op("act", lambda e, r_=r_, a_=a_, ct=ct: e.activation(out=a_[:], in_=r_[:, 3:3 + S], func=AF.Identity,
                                                                          scale=gcw[:, 3, ct:ct + 1]), [r_.b, gcw.b], [a_.b])
                    for tap in range(3):
                        K.op("dve", lambda e, r_=r_, a_=a_, ct=ct, tap=tap: e.scalar_tensor_tensor(
                            out=a_[:], in0=r_[:, tap:tap + S], scalar=gcw[:, tap, ct:ct + 1], in1=a_[:],
                            op0=ALU.mult, op1=ALU.add), [r_.b, gcw.b, a_.b], [a_.b])
                    if kind == "gv":
                        K.op("act", lambda e, a_=a_, o_=o_: e.activation(out=o_[:], in_=a_[:], func=AF.Silu), [a_.b], [o_.b])
                    else:
                        K.op("act", lambda e, a_=a_: e.activation(out=a_[:], in_=a_[:], func=AF.Silu), [a_.b], [a_.b])
                        if kind == "gq":
                            l2_or_rms(a_, o_, 128.0, 0, None)
                        else:
                            l2_or_rms(a_, o_, 1.0, 1, None)
                    K.dma("sp", scr[kind][hh], o_[:], [o_.b], [scr[kind + "_b"]], o_.b)
                elif kind == "gz":
                    for tc in range(4):
                        K.op("act", lambda e, p_=ps[tc], o_=o_, tc=tc: e.activation(
                            out=o_[:, tc * 512:(tc + 1) * 512], in_=p_[:], func=AF.Silu), [ps[tc].b], [o_.b])
                    K.dma("sp", scr["gz"][hh], o_[:], [o_.b], [scr["gz_b"]], o_.b)
                else:
                    for tc in range(4):
                        K.op("act", lambda e, p_=ps[tc], a_=a_, tc=tc: e.copy(out=a_[:, tc * 512:(tc + 1) * 512], in_=p_[:]),
                             [ps[tc].b], [a_.b])
                    if kind == "sq":
                        l2_or_rms(a_, o_, 1.0, 2, nrm[:, 0:1])
                    else:
                        l2_or_rms(a_, o_, 1.0 / 128.0, 3, nrm[:, 1:2])
                    K.dma("sp", scr[kind][hh], o_[:], [o_.b], [scr[kind + "_b"]], o_.b)
            if kind == "gz" and half == 1:
                bb, ee, cc, rm = acc[0], acc[1], lnv, ysq
                for which in range(2):
                    ps = mm_tile(wba, which * 8, M=8)
                    for tc in range(4):
                        sl = slice(tc * 512, (tc + 1) * 512)
                        if which == 0:
                            K.op("act", lambda e, p_=ps[tc], sl=sl: e.activation(out=bb[0:8, sl], in_=p_[0:8, :], func=AF.Sigmoid),
                                 [ps[tc].b], [bb.b])
                        else:
                            K.op("act", lambda e, p_=ps[tc], sl=sl: e.activation(out=ee[0:8, sl], in_=p_[0:8, :], func=AF.Exp,
                                                                              bias=hp[:, 1:2]), [ps[tc].b, hp.b], [ee.b])
                K.op("act", lambda e: e.activation(out=ee[0:8, :], in_=ee[0:8, :], func=AF.Ln, bias=consts["bias"][0:8, 4:5]),
                     [ee.b, consts["bias"].b], [ee.b])
                K.op("dve", lambda e: e.tensor_scalar(out=ee[0:8, :], in0=ee[0:8, :], scalar1=hp[:, 2:3], scalar2=None,
                                                      op0=ALU.mult), [ee.b, hp.b], [ee.b])
                K.op("pool", lambda e: e.memset(rm[0:8, :], 1.0), [], [rm.b])
                K.op("pool", lambda e: e.memset(rm[0:8, :].rearrange("p (c n) -> p c n", n=128)[:, :, 0:1], 0.0), [rm.b], [rm.b])
                K.op("dve", lambda e: e.tensor_tensor_scan(out=cc[0:8, :], data0=rm[0:8, :], data1=ee[0:8, :], initial=0.0,
                                                           op0=ALU.mult, op1=ALU.add), [ee.b, rm.b], [cc.b])
                K.dma("sp", scr["beta"], bb[0:8, :], [bb.b], [scr["beta_b"]], bb.b)
                K.dma("sp", scr["gcum"], cc[0:8, :], [cc.b], [scr["gcum_b"]], cc.b)
        K.barrier()


GDN_STOP = 99


class _Stop(Exception):
    pass


_KREF = []


def _chk(k):
    if GDN_STOP <= k:
        _KREF[0].muted = True


def phase_gdn(C, l, ins, consts, scr):
    _KREF[:] = [C.K]
    _phase_gdn(C, l, ins, consts, scr)
    C.K.muted = False
    C.K.barrier()


def _phase_gdn(C, l, ins, consts, scr):
    K, nc = C.K, C.nc
    U32 = mybir.dt.uint32
    ident, onesf, identf, ones_bf = consts["ident"], consts["onesf"], consts["identf"], consts["ones_bf"]
    with ExitStack() as es:
        gT = C.sb(es, "gT", [8, S], F32)
        bT = C.sb(es, "bT", [8, S], F32)
        tk = C.sb(es, "tk", [128, 6, 128], F32)
        ogain = C.sb(es, "ogain", [128, 1], F32)
        masks = C.sb(es, "masks", [128, 14, 128], F32)
        masks8 = C.sb(es, "masks8", [128, 14, 512], mybir.dt.uint8)
        sel = C.sb(es, "sel", [128, 128], F32)
        zer = C.sb(es, "zer", [128, 128], F32)
        mneg = C.sb(es, "mneg", [128, 128], F32)
        B = [C.ps(es, f"B{i}", [128, 512], F32) for i in range(8)]
        K.dma("sp", gT[:], scr["gcum"], [scr["gcum_b"]], [gT.b], gT.b)
        K.dma("sp", bT[:], scr["beta"], [scr["beta_b"]], [bT.b], bT.b)
        K.dma("sp", ogain[:], ins["gdn_o_norm"][l, :].rearrange("(p o) -> p o", o=1), [], [ogain.b], ogain.b)
        K.op("pool", lambda e: e.memset(zer[:], 0.0), [], [zer.b])
        K.op("pool", lambda e: e.affine_select(out=mneg[:], in_=zer[:], pattern=[[1, 128]], compare_op=ALU.is_ge,
                                               fill=-30000.0, base=0, channel_multiplier=-1), [zer.b], [mneg.b])
        K.op("pool", lambda e: e.affine_select(out=sel[:], in_=onesf[:], pattern=[[0, 128]], compare_op=ALU.is_ge,
                                               fill=0.0, base=-127, channel_multiplier=1), [onesf.b], [sel.b])
        for lv in range(7):
            n = 1 << lv
            nb = 128 // (2 * n)
            specs = [
                (lv, [(-n, 1, [[-2 * n, nb], [0, 2], [0, n]]), (2 * n - 1, -1, [[2 * n, nb], [0, 2], [0, n]]),
                      (0, 0, [[0, nb], [-1, 2], [0, n]])]),
                (7 + lv, [(0, 1, [[-2 * n, nb], [0, 2], [0, n]]), (n - 1, -1, [[2 * n, nb], [0, 2], [0, n]]),
                          (-1, 0, [[0, nb], [1, 2], [0, n]])]),
            ]
            for mi, passes in specs:
                for pi, (base, cm, pat) in enumerate(passes):
                    src = onesf if pi == 0 else masks
                    K.op("pool", lambda e, mi=mi, base=base, cm=cm, pat=pat, pi=pi: e.affine_select(
                        out=masks[:, mi, :], in_=(onesf[:] if pi == 0 else masks[:, mi, :]), pattern=pat,
                        compare_op=ALU.is_ge, fill=0.0, base=base, channel_multiplier=cm),
                         [src.b], [masks.b])
        K.op("dve", lambda e: e.tensor_copy(out=masks8[:].rearrange("p m (r n) -> p m r n", n=128),
                                            in_=masks[:].unsqueeze(2).to_broadcast([128, 14, 4, 128])), [masks.b], [masks8.b])
        for which, srcT in ((0, gT), (1, bT)):
            for c in range(NT):
                K.op("pe", lambda e, which=which, srcT=srcT, c=c: e.transpose(
                    out=B[which][:, c * 8:c * 8 + 8], in_=srcT[:, c * 128:(c + 1) * 128],
                    identity=identf[0:8, 0:8]), [srcT.b, identf.b], [B[which].b], sig=(c == NT - 1))
            K.op("act", lambda e, which=which: e.copy(out=tk[:, which, :], in_=B[which][:, 0:128]),
                 [B[which].b], [tk.b])
        K.op("dve", lambda e: e.tensor_scalar(out=tk[:, 2, :], in0=tk[:, 1, :], scalar1=-1.0, scalar2=None, op0=ALU.mult),
             [tk.b], [tk.b])
        K.op("act", lambda e: e.activation(out=tk[:, 3, :], in_=tk[:, 0, :], func=AF.Exp), [tk.b], [tk.b])
        K.op("pe", lambda e: e.matmul(B[2][:, 0:128], lhsT=sel[:], rhs=tk[:, 0, :], start=True, stop=True),
             [sel.b, tk.b], [B[2].b])
        K.op("act", lambda e: e.activation(out=tk[:, 5, :], in_=B[2][:, 0:128], func=AF.Exp), [B[2].b], [tk.b])
        K.op("dve", lambda e: e.tensor_tensor(out=tk[:, 4, :], in0=B[2][:, 0:128], in1=tk[:, 0, :], op=ALU.subtract),
             [B[2].b, tk.b], [tk.b])
        K.op("act", lambda e: e.activation(out=tk[:, 4, :], in_=tk[:, 4, :], func=AF.Exp), [tk.b], [tk.b])
        _chk(1)

        def col(which, h):
            return tk[:, which, :].rearrange("p (c h) -> p c h", h=8)[:, :, h:h + 1]

        def v3(ap):
            return ap.rearrange("p (c n) -> p c n", n=128)

        with ExitStack() as eg:
            slots = []
            for hi in range(4):
                sl = {nm: C.sb(eg, f"{nm}{hi}", [128, S], BF16, nbufs=16) for nm in ("wT", "ub", "kd", "qg", "AT", "oT")}
                sl["S32"] = C.sb(eg, f"S32{hi}", [128, 128], F32)
                sl["Sbf"] = C.sb(eg, f"Sbf{hi}", [128, 128], BF16)
                sl["vn"] = [C.sb(eg, f"vn{hi}_{i}", [128, 128], BF16) for i in range(2)]
                slots.append(sl)
            for grp in range(2):
                heads = list(range(grp * 4, grp * 4 + 4))
                per = {h: slots[h % 4] for h in heads}
                if grp == 0:
                  qT = C.sb(eg, "qT", [128, S], BF16)
                  kT = C.sb(eg, "kT", [128, S], BF16)
                  vT = C.sb(eg, "vT", [128, S], BF16)
                  Rb = C.sb(eg, "Rb", [128, S], F32)
                  E = C.sb(eg, "E", [128, S], F32)
                  dU = C.sb(eg, "dU", [128, S], BF16)
                  LnT = C.sb(eg, "LnT", [128, S], BF16, nbufs=4)
                  Dm = C.sb(eg, "Dm", [128, S], BF16, nbufs=4)
                  DTm = C.sb(eg, "DTm", [128, S], BF16, nbufs=4)
                  Ysb = C.sb(eg, "Ysb", [128, S], BF16, nbufs=4)
                  kg = C.sb(eg, "kg", [128, S], BF16)
                  vtok = C.sb(eg, "vtok", [128, S], BF16)
                  zs = C.sb(eg, "zs", [128, S], BF16)
                for h in heads:
                    P = per[h]
                    K.dma("sp", qT[:], scr["gq"][h], [scr["gq_b"]], [qT.b], qT.b)
                    K.dma("sp", kT[:], scr["gk"][h], [scr["gk_b"]], [kT.b], kT.b)
                    K.dma("sp", vT[:], scr["gv"][h], [scr["gv_b"]], [vT.b], vT.b)
                    K.dma("sp", Rb[:], scr["gcum"][h, :].partition_broadcast(128), [scr["gcum_b"]], [Rb.b], Rb.b)
                    K.op("act", lambda e: e.activation(out=E[:], in_=Rb[:], func=AF.Exp), [Rb.b], [E.b])
                    K.op("dve", lambda e, P=P: e.tensor_tensor(out=P["qg"][:], in0=qT[:], in1=E[:], op=ALU.mult),
                         [qT.b, E.b], P["qg"].bs)
                    K.op("dve", lambda e, h=h: e.tensor_tensor(out=v3(E[:]), in0=v3(Rb[:]),
                                                               in1=col(0, h).to_broadcast([128, NT, 128]), op=ALU.subtract),
                         [Rb.b, tk.b], [E.b])
                    K.op("dve", lambda e: e.tensor_tensor(out=v3(E[:]), in0=v3(E[:]),
                                                          in1=mneg[:].unsqueeze(1).to_broadcast([128, NT, 128]), op=ALU.add),
                         [E.b, mneg.b], [E.b])
                    K.op("act", lambda e: e.activation(out=dU[:], in_=E[:], func=AF.Exp), [E.b], [dU.b])
                    for q in range(4):
                        for i in range(4):
                            cs = slice((q * 4 + i) * 128, (q * 4 + i + 1) * 128)
                            K.op("pe", lambda e, q=q, i=i, cs=cs: e.matmul(
                                B[q][:, i * 128:(i + 1) * 128], lhsT=kT[:, cs], rhs=kT[:, cs], start=True, stop=True),
                                 [kT.b], [B[q].b], sig=(i == 3))
                        for i in range(4):
                            c = q * 4 + i
                            cs = slice(c * 128, (c + 1) * 128)
                            K.op("dve", lambda e, q=q, i=i, cs=cs, c=c, h=h: e.scalar_tensor_tensor(
                                out=LnT[:, cs], in0=B[q][:, i * 128:(i + 1) * 128],
                                scalar=tk[:, 2, c * 8 + h:c * 8 + h + 1], in1=dU[:, cs], op0=ALU.mult, op1=ALU.mult),
                                 [B[q].b, tk.b, dU.b], [LnT.bs[q]])
                    for q in range(4):
                        for i in range(4):
                            cs = slice((q * 4 + i) * 128, (q * 4 + i + 1) * 128)
                            K.op("pe", lambda e, q=q, i=i, cs=cs: e.matmul(
                                B[4 + q][:, i * 128:(i + 1) * 128], lhsT=kT[:, cs], rhs=qT[:, cs], start=True, stop=True),
                                 [kT.b, qT.b], [B[4 + q].b], sig=(i == 3))
                        qs = slice(q * 512, (q + 1) * 512)
                        K.op("dve", lambda e, q=q, qs=qs, P=P: e.tensor_tensor(
                            out=P["AT"][:, qs], in0=B[4 + q][:], in1=dU[:, qs], op=ALU.mult),
                             [B[4 + q].b, dU.b], P["AT"].bs[q * 4:q * 4 + 4])
                    _chk(2)
                    for src, b0 in ((kT, 0), (vT, 2)):
                        for hb in range(2):
                            pv = B[b0 + hb][:].bitcast(BF16)
                            for i in range(8):
                                cs = slice((hb * 8 + i) * 128, (hb * 8 + i + 1) * 128)
                                K.op("pe", lambda e, pv=pv, i=i, cs=cs, src=src: e.transpose(
                                    out=pv[:, i * 128:(i + 1) * 128], in_=src[:, cs], identity=ident[:]),
                                     [src.b, ident.b], [B[b0 + hb].b], sig=(i == 7))
                    for hb in range(2):
                        hs_ = slice(hb * 1024, (hb + 1) * 1024)
                        pk = B[hb][:].bitcast(BF16)
                        pvv = B[2 + hb][:].bitcast(BF16)
                        K.op("dve", lambda e, h=h, hb=hb, hs_=hs_, pk=pk: e.tensor_tensor(
                            out=v3(kg[:, hs_]), in0=v3(pk), in1=col(3, h)[:, hb * 8:(hb + 1) * 8, :].to_broadcast([128, 8, 128]),
                            op=ALU.mult), [B[hb].b, tk.b], [kg.b])
                        K.op("dve", lambda e, h=h, hb=hb, hs_=hs_, pk=pk, P=P: e.tensor_tensor(
                            out=v3(P["kd"][:, hs_]), in0=v3(pk), in1=col(4, h)[:, hb * 8:(hb + 1) * 8, :].to_broadcast([128, 8, 128]),
                            op=ALU.mult), [B[hb].b, tk.b], P["kd"].bs[hb * 8:(hb + 1) * 8])
                        K.op("act", lambda e, hs_=hs_, pvv=pvv: e.copy(out=vtok[:, hs_], in_=pvv), [B[2 + hb].b], [vtok.b])
                    _chk(3)
                    K.op("dve", lambda e: e.tensor_copy(out=v3(Dm[:]), in_=ident[:].unsqueeze(1).to_broadcast([128, NT, 128])),
                         [ident.b], Dm.bs)
                    K.op("dve", lambda e: e.tensor_copy(out=v3(DTm[:]), in_=ident[:].unsqueeze(1).to_broadcast([128, NT, 128])),
                         [ident.b], DTm.bs)
                    for lv in range(7):
                        for q in range(4):
                            s_ = q % 2
                            qs = slice(q * 512, (q + 1) * 512)
                            Yp, Zp, ZTp = B[s_ * 3], B[s_ * 3 + 1], B[s_ * 3 + 2]
                            for i in range(4):
                                cs = slice(q * 512 + i * 128, q * 512 + (i + 1) * 128)
                                K.op("pe", lambda e, cs=cs, i=i, Yp=Yp: e.matmul(
                                    Yp[:, i * 128:(i + 1) * 128], lhsT=LnT[:, cs], rhs=Dm[:, cs],
                                    start=True, stop=True), [LnT.bs[q], Dm.bs[q]], [Yp.b], sig=(i == 3))
                            K.op("act", lambda e, qs=qs, Yp=Yp: e.copy(out=Ysb[:, qs], in_=Yp[:]), [Yp.b], [Ysb.bs[q]])
                            for i in range(4):
                                cs = slice(q * 512 + i * 128, q * 512 + (i + 1) * 128)
                                K.op("pe", lambda e, cs=cs, i=i, Zp=Zp: e.matmul(
                                    Zp[:, i * 128:(i + 1) * 128], lhsT=DTm[:, cs], rhs=Ysb[:, cs],
                                    start=True, stop=True), [DTm.bs[q], Ysb.bs[q]], [Zp.b], sig=(i == 3))
                            for i in range(4):
                                cs = slice(q * 512 + i * 128, q * 512 + (i + 1) * 128)
                                K.op("pe", lambda e, cs=cs, i=i, ZTp=ZTp: e.matmul(
                                    ZTp[:, i * 128:(i + 1) * 128], lhsT=Ysb[:, cs], rhs=DTm[:, cs],
                                    start=True, stop=True), [DTm.bs[q], Ysb.bs[q]], [ZTp.b], sig=(i == 3))
                            K.op("dve", lambda e, qs=qs, lv=lv, Zp=Zp: e.copy_predicated(
                                out=Dm[:, qs], mask=masks8[:, lv, :], data=Zp[:]), [Zp.b, masks8.b], [Dm.bs[q]])
                            K.op("dve", lambda e, qs=qs, lv=lv, ZTp=ZTp: e.copy_predicated(
                                out=DTm[:, qs], mask=masks8[:, 7 + lv, :], data=ZTp[:]), [ZTp.b, masks8.b], [DTm.bs[q]])
                    _chk(4)
                    for q in range(4):
                        qs = slice(q * 512, (q + 1) * 512)
                        for i in range(4):
                            cs = slice((q * 4 + i) * 128, (q * 4 + i + 1) * 128)
                            K.op("pe", lambda e, q=q, i=i, cs=cs: e.matmul(
                                B[q][:, i * 128:(i + 1) * 128], lhsT=DTm[:, cs], rhs=vtok[:, cs], start=True, stop=True),
                                 [DTm.bs[q], vtok.b], [B[q].b], sig=(i == 3))
                        K.op("dve", lambda e, q=q, qs=qs, P=P, h=h: e.tensor_tensor(
                            out=v3(P["ub"][:, qs]), in0=v3(B[q][:]),
                            in1=col(1, h)[:, q * 4:(q + 1) * 4, :].to_broadcast([128, 4, 128]), op=ALU.mult),
                             [B[q].b, tk.b], P["ub"].bs[q * 4:q * 4 + 4])
                    for q in range(4):
                        qs = slice(q * 512, (q + 1) * 512)
                        for i in range(4):
                            cs = slice((q * 4 + i) * 128, (q * 4 + i + 1) * 128)
                            K.op("pe", lambda e, q=q, i=i, cs=cs: e.matmul(
                                B[4 + q][:, i * 128:(i + 1) * 128], lhsT=kg[:, cs], rhs=DTm[:, cs], start=True, stop=True),
                                 [DTm.bs[q], kg.b], [B[4 + q].b], sig=(i == 3))
                        K.op("act", lambda e, q=q, qs=qs, P=P: e.copy(out=P["wT"][:, qs], in_=B[4 + q][:]),
                             [B[4 + q].b], P["wT"].bs[q * 4:q * 4 + 4])
                    K.op("pool", lambda e, P=P: e.memset(P["S32"][:], 0.0), [], [P["S32"].b])
                    K.op("pool", lambda e, P=P: e.memset(P["Sbf"][:], 0.0), [], [P["Sbf"].b])
                _chk(5)
                for c in range(NT):
                    cs = slice(c * 128, (c + 1) * 128)
                    par = (c % 2) * 3
                    Bw, Bo, Bs = B[par], B[par + 1], B[par + 2]
                    for hi, h in enumerate(heads):
                        P = per[h]
                        ss = slice(hi * 128, (hi + 1) * 128)
                        K.op("pe", lambda e, P=P, ss=ss, cs=cs, Bw=Bw: e.matmul(Bw[:, ss], lhsT=P["wT"][:, cs], rhs=P["Sbf"][:],
                                                                               start=True, stop=True),
                             [P["wT"].bs[c], P["Sbf"].b], [Bw.b], sig=(hi == 3))
                    for hi, h in enumerate(heads):
                        P = per[h]
                        ss = slice(hi * 128, (hi + 1) * 128)
                        idx = c * 8 + h
                        vn = P["vn"][c % 2]
                        K.op("dve", lambda e, P=P, ss=ss, cs=cs, idx=idx, vn=vn, Bw=Bw: e.scalar_tensor_tensor(
                            out=vn[:], in0=Bw[:, ss], scalar=tk[:, 2, idx:idx + 1], in1=P["ub"][:, cs],
                            op0=ALU.mult, op1=ALU.add), [Bw.b, tk.b, P["ub"].bs[c]], [vn.b])
                    for hi, h in enumerate(heads):
                        P = per[h]
                        ss = slice(hi * 128, (hi + 1) * 128)
                        vn = P["vn"][c % 2]
                        K.op("pe", lambda e, P=P, ss=ss, cs=cs, Bo=Bo: e.matmul(Bo[:, ss], lhsT=P["Sbf"][:], rhs=P["qg"][:, cs],
                                                                               start=True, stop=False),
                             [P["Sbf"].b, P["qg"].bs[c]], [Bo.b], sig=False)
                        K.op("pe", lambda e, P=P, ss=ss, cs=cs, vn=vn, Bo=Bo: e.matmul(Bo[:, ss], lhsT=vn[:], rhs=P["AT"][:, cs],
                                                                                      start=False, stop=True),
                             [vn.b, P["AT"].bs[c]], [Bo.b], sig=(hi == 3))
                    for hi, h in enumerate(heads):
                        P = per[h]
                        ss = slice(hi * 128, (hi + 1) * 128)
                        vn = P["vn"][c % 2]
                        K.op("pe", lambda e, P=P, ss=ss, cs=cs, vn=vn, Bs=Bs: e.matmul(Bs[:, ss], lhsT=P["kd"][:, cs], rhs=vn[:],
                                                                                      start=True, stop=True),
                             [vn.b, P["kd"].bs[c]], [Bs.b], sig=(hi == 3))
                    for hi, h in enumerate(heads):
                        P = per[h]
                        ss = slice(hi * 128, (hi + 1) * 128)
                        idx = c * 8 + h
                        K.op("dve", lambda e, P=P, ss=ss, idx=idx, Bs=Bs: e.scalar_tensor_tensor(
                            out=P["S32"][:], in0=P["S32"][:], scalar=tk[:, 5, idx:idx + 1], in1=Bs[:, ss],
                            op0=ALU.mult, op1=ALU.add), [P["S32"].b, tk.b, Bs.b], [P["S32"].b])
                        K.op("act", lambda e, P=P: e.copy(out=P["Sbf"][:], in_=P["S32"][:]), [P["S32"].b], [P["Sbf"].b])
                    for hi, h in enumerate(heads):
                        P = per[h]
                        ss = slice(hi * 128, (hi + 1) * 128)
                        K.op("act", lambda e, P=P, ss=ss, cs=cs, Bo=Bo: e.copy(out=P["oT"][:, cs], in_=Bo[:, ss]),
                             [Bo.b], [P["oT"].bs[c]])
                _chk(6)
                for h in heads:
                    P = per[h]
                    K.dma("sp", zs[:], scr["gz"][h], [scr["gz_b"]], [zs.b], zs.b)
                    K.op("act", lambda e, P=P: e.activation(out=dU[:], in_=P["oT"][:], func=AF.Square), P["oT"].bs, [dU.b])
                    for tc in range(4):
                        ts_ = slice(tc * 512, (tc + 1) * 512)
                        Bn = B[6 + tc % 2]
                        K.op("pe", lambda e, ts_=ts_, Bn=Bn: e.matmul(Bn[:], lhsT=ones_bf[:], rhs=dU[:, ts_], start=True, stop=True),
                             [ones_bf.b, dU.b], [Bn.b])
                        K.op("act", lambda e, ts_=ts_, Bn=Bn: e.activation(
                            out=E[:, ts_], in_=Bn[:], func=AF.Ln, scale=1.0 / 128.0,
                            bias=consts["bias"][:, 3:4]), [Bn.b, consts["bias"].b], [E.b])
                    K.op("act", lambda e: e.activation(out=E[:], in_=E[:], func=AF.Exp, scale=-0.5), [E.b], [E.b])
                    K.op("dve", lambda e, P=P: e.scalar_tensor_tensor(out=LnT[:], in0=P["oT"][:], scalar=ogain[:, 0:1], in1=E[:],
                                                                      op0=ALU.mult, op1=ALU.mult),
                         P["oT"].bs + [ogain.b, E.b], LnT.bs)
                    K.op("dve", lambda e: e.tensor_tensor(out=Dm[:], in0=LnT[:], in1=zs[:], op=ALU.mult),
                         LnT.bs + [zs.b], Dm.bs)
                    K.dma("sp", scr["mix"][h], Dm[:], Dm.bs, [scr["mix_b"]], Dm.bs[0])
                K.barrier()


def phase_sb(C, l, ins, consts, scr):
    K, nc = C.K, C.nc
    ones_bf, ident = consts["ones_bf"], consts["ident"]
    with ExitStack() as es:
        qT = [C.sb(es, f"sqT{i}", [128, S], BF16) for i in range(2)]
        kT = [C.sb(es, f"skT{i}", [128, S], BF16) for i in range(2)]
        nkT = [C.sb(es, f"snkT{i}", [128, S], BF16) for i in range(2)]
        vt = [C.sb(es, f"svt{i}", [128, NT, 128], BF16) for i in range(2)]
        e1 = [C.sb(es, f"e1_{i}", [128, 512], F32) for i in range(2)]
        sp = [C.sb(es, f"sp{i}", [128, 512], BF16) for i in range(2)]
        sps = [C.sb(es, f"sps{i}", [128, 512], BF16) for i in range(2)]
        ww = [C.sb(es, f"ww{i}", [128, 512], BF16) for i in range(2)]
        yraw = [C.sb(es, f"yraw{i}", [128, S], F32) for i in range(2)]
        ysq = C.sb(es, "ysq", [128, S], BF16)
        rn = C.sb(es, "rn", [128, S], F32)
        ymx = [C.sb(es, f"ymx{i}", [128, S], BF16) for i in range(2)]
        mneg = C.sb(es, "mnegs", [128, 4, 512], BF16)
        mpos = C.sb(es, "mposs", [128, 4, 512], BF16)
        uin = C.sb(es, "uin", [128, 128], BF16)
        og = C.sb(es, "og", [128, 1], F32)
        Bz = [C.ps(es, f"Bz{i}", [128, 512], F32) for i in range(2)]
        BA = [C.ps(es, f"BA{i}", [128, 512], F32) for i in range(2)]
        Bn = [C.ps(es, f"Bn{i}", [128, 512], F32) for i in range(2)]
        Bo = [C.ps(es, f"Bo{i}", [128, 512], F32) for i in range(2)]
        onesw = C.sb(es, "onesw", [128, 512], F32)
        zerw = C.sb(es, "zerw", [128, 512], F32)
        K.op("pool", lambda e: e.memset(onesw[:], 1.0), [], [onesw.b])
        K.op("pool", lambda e: e.memset(zerw[:], 0.0), [], [zerw.b])
        for k in range(4):
            K.op("pool", lambda e, k=k: e.affine_select(out=mneg[:, k, :], in_=zerw[:], pattern=[[1, 512]],
                                                        compare_op=ALU.is_gt, fill=-30000.0, base=-128 * k, channel_multiplier=-1),
                 [zerw.b], [mneg.b])
            K.op("pool", lambda e, k=k: e.affine_select(out=mpos[:, k, :], in_=zerw[:], pattern=[[1, 512]],
                                                        compare_op=ALU.is_gt, fill=30000.0, base=-128 * k, channel_multiplier=-1),
                 [zerw.b], [mpos.b])
        K.op("pool", lambda e: e.affine_select(out=uin[:], in_=onesw[:, 0:128], pattern=[[-1, 128]], compare_op=ALU.is_ge,
                                               fill=0.0, base=0, channel_multiplier=1), [onesw.b], [uin.b])
        K.dma("sp", og[:], ins["sb_o_norm"][l, :].rearrange("(p o) -> p o", o=1), [], [og.b], og.b)

        blocks = []
        for h in range(H):
            for tc in range(4):
                nb = 4 * tc + 4
                for b in range(nb - 1, -1, -1):
                    blocks.append((h, tc, b, b == nb - 1, b == 0))

        def load(h):
            i = h % 2
            K.dma("sp", qT[i][:], scr["sq"][h], [scr["sq_b"]], [qT[i].b], qT[i].b)
            K.dma("sp", kT[i][:], scr["sk"][h], [scr["sk_b"]], [kT[i].b], kT[i].b)
            K.dma("sp", vt[i][:], scr["sv"][:, :, h * 128:(h + 1) * 128].rearrange("t p d -> p t d"),
                  [scr["sv_b"]], [vt[i].b], vt[i].b)
            K.op("pool", lambda e: e.tensor_scalar(out=nkT[i][:], in0=kT[i][:], scalar1=-1.0, scalar2=None, op0=ALU.mult),
                 [kT[i].b], [nkT[i].b])

        def stageA(n):
            h, tc, b, first, last = blocks[n]
            i, j = h % 2, n % 2
            ts_ = slice(tc * 512, (tc + 1) * 512)
            k = b - 4 * tc
            K.op("pe", lambda e: e.matmul(Bz[j][:], lhsT=kT[i][:, b * 128:(b + 1) * 128], rhs=qT[i][:, ts_],
                                          start=True, stop=(k < 0)), [kT[i].b, qT[i].b], [Bz[j].b], sig=(k < 0))
            if k >= 0:
                K.op("pe", lambda e: e.matmul(Bz[j][:], lhsT=ident[:], rhs=mneg[:, k, :], start=False, stop=True),
                     [ident.b, mneg.b], [Bz[j].b])
            K.op("act", lambda e: e.activation(out=e1[j][:], in_=Bz[j][:], func=AF.Exp), [Bz[j].b], [e1[j].b])
            K.op("act", lambda e: e.activation(out=sp[j][:], in_=e1[j][:], func=AF.Ln, bias=consts["bias"][:, 4:5]),
                 [e1[j].b, consts["bias"].b], [sp[j].b])

        def stageB(n):
            h, tc, b, first, last = blocks[n]
            i, j = h % 2, n % 2
            ts_ = slice(tc * 512, (tc + 1) * 512)
            k = b - 4 * tc
            o_ = Bo[tc % 2]
            K.op("pe", lambda e: e.matmul(BA[j][:], lhsT=nkT[i][:, b * 128:(b + 1) * 128], rhs=qT[i][:, ts_],
                                          start=True, stop=False), [nkT[i].b, qT[i].b], [BA[j].b], sig=False)
            if k >= 0:
                K.op("pe", lambda e: e.matmul(BA[j][:], lhsT=ident[:], rhs=mpos[:, k, :], start=False, stop=False),
                     [ident.b, mpos.b], [BA[j].b], sig=False)
            K.op("pe", lambda e: e.matmul(BA[j][:], lhsT=uin[:], rhs=sp[j][:], start=False, stop=first),
                 [uin.b, sp[j].b], [BA[j].b], sig=first)
            if not first:
                K.op("pe", lambda e: e.matmul(BA[j][:], lhsT=ones_bf[:], rhs=sps[(n - 1) % 2][:], start=False, stop=True),
                     [ones_bf.b, sps[(n - 1) % 2].b], [BA[j].b])
            K.op("act", lambda e: e.activation(out=ww[j][:], in_=BA[j][:], func=AF.Exp, scale=-1.0), [BA[j].b], [ww[j].b])
            if not last:
                if first:
                    K.op("dve", lambda e: e.tensor_copy(out=sps[n % 2][:], in_=sp[j][:]), [sp[j].b], [sps[n % 2].b])
                else:
                    K.op("dve", lambda e: e.tensor_tensor(out=sps[n % 2][:], in0=sps[(n - 1) % 2][:], in1=sp[j][:], op=ALU.add),
                         [sps[(n - 1) % 2].b, sp[j].b], [sps[n % 2].b])
            K.op("pe", lambda e: e.matmul(o_[:], lhsT=vt[i][:, b, :], rhs=ww[j][:], start=first, stop=last),
                 [vt[i].b, ww[j].b], [o_.b], sig=last)
            if last:
                y_ = yraw[h % 2]
                K.op("act", lambda e: e.copy(out=y_[:, ts_], in_=o_[:]), [o_.b], [y_.b])
                if tc == 3:
                    finish(h)

        def finish(h):
            y_, m_ = yraw[h % 2], ymx[h % 2]
            K.op("act", lambda e: e.activation(out=ysq[:], in_=y_[:], func=AF.Square), [y_.b], [ysq.b])
            for tc in range(4):
                ts_ = slice(tc * 512, (tc + 1) * 512)
                p_ = Bn[tc % 2]
                K.op("pe", lambda e, p_=p_, ts_=ts_: e.matmul(p_[:], lhsT=ones_bf[:], rhs=ysq[:, ts_], start=True, stop=True),
                     [ones_bf.b, ysq.b], [p_.b])
                K.op("act", lambda e, p_=p_, ts_=ts_: e.activation(out=rn[:, ts_], in_=p_[:], func=AF.Ln, scale=1.0 / 128.0,
                                                                   bias=consts["bias"][:, 3:4]), [p_.b, consts["bias"].b], [rn.b])
            K.op("act", lambda e: e.activation(out=rn[:], in_=rn[:], func=AF.Exp, scale=-0.5), [rn.b], [rn.b])
            K.op("dve", lambda e: e.scalar_tensor_tensor(out=m_[:], in0=y_[:], scalar=og[:, 0:1], in1=rn[:],
                                                         op0=ALU.mult, op1=ALU.mult), [y_.b, og.b, rn.b], [m_.b])
            K.dma("sp", scr["mix"][8 + h], m_[:], [m_.b], [scr["mix_b"]], m_.b)

        load(0)
        for n in range(len(blocks)):
            h, tc, b, first, last = blocks[n]
            stageA(n)
            if n > 0:
                stageB(n - 1)
            if first and tc == 0 and h + 1 < H:
                load(h + 1)
        stageB(len(blocks) - 1)
        K.barrier()


def phase_outproj(C, l, x_in, x_in_buf, x_out, x_out_buf, ins, consts, scr):
    K, nc = C.K, C.nc
    with ExitStack() as es:
        mx = C.sb(es, "mx", [128, KC, S], BF16)
        wo = C.sb(es, "wo", [128, KC, D], BF16, nbufs=4)
        xt = [C.sb(es, f"xo_in{i}", [128, D], F32) for i in range(2)]
        xo = [C.sb(es, f"xo_out{i}", [128, D], F32) for i in range(2)]
        pp = [C.ps(es, f"po{i}", [128, 512], F32) for i in range(8)]
        for kc in range(KC):
            K.dma("sp", mx[:, kc, :], scr["mix"][kc], [scr["mix_b"]], [mx.b], mx.b, grp=True)
        alloc_stage(C, es)
        for cq in range(4):
            load_w(C, wo, ins["w_out"][l], cq * 512, 512, cq * 512, wbuf=wo.bs[cq])
        n = 0
        for t in range(NT):
            x_, o_ = xt[t % 2], xo[t % 2]
            K.dma("sp", x_[:], x_in[t * 128:(t + 1) * 128, :], [x_in_buf], [x_.b], x_.b)
            for cq in range(4):
                p_ = pp[n % 8]
                n += 1
                for kc in range(KC):
                    K.op("pe", lambda e, p_=p_, kc=kc, t=t, cq=cq: e.matmul(
                        p_[:], lhsT=mx[:, kc, t * 128:(t + 1) * 128], rhs=wo[:, kc, cq * 512:(cq + 1) * 512],
                        start=(kc == 0), stop=(kc == KC - 1)), [mx.b, wo.bs[cq]], [p_.b], sig=(kc == KC - 1))
                K.op("dve", lambda e, p_=p_, x_=x_, o_=o_, cq=cq: e.tensor_tensor(
                    out=o_[:, cq * 512:(cq + 1) * 512], in0=p_[:], in1=x_[:, cq * 512:(cq + 1) * 512], op=ALU.add),
                     [p_.b, x_.b], [o_.b])
            K.dma("sp", x_out[t * 128:(t + 1) * 128, :], o_[:], [o_.b], [x_out_buf], o_.b)
        K.barrier()


def make_consts(C, es):
    K, nc = C.K, C.nc
    c = {}
    c["neghalf"] = C.sb(es, "neghalf", [128, 1], F32)
    K.op("pool", lambda e: e.memset(c["neghalf"][:], -0.5), [], [c["neghalf"].b])
    onesf = C.sb(es, "onesf", [128, 128], F32)
    K.op("pool", lambda e: e.memset(onesf[:], 1.0), [], [onesf.b])
    identf = C.sb(es, "identf", [128, 128], F32)
    K.op("pool", lambda e: e.affine_select(out=identf[:], in_=onesf[:], pattern=[[1, 128]],
                                           compare_op=ALU.is_equal, fill=0.0, base=0, channel_multiplier=-1),
         [onesf.b], [identf.b])
    c["ident"] = C.sb(es, "ident", [128, 128], BF16)
    K.op("pool", lambda e: e.tensor_copy(out=c["ident"][:], in_=identf[:]), [identf.b], [c["ident"].b])
    c["onesf"] = onesf
    c["identf"] = identf
    c["ones_bf"] = C.sb(es, "ones_bf", [128, 128], BF16)
    K.op("pool", lambda e: e.memset(c["ones_bf"][:], 1.0), [], [c["ones_bf"].b])
    c["bias"] = C.sb(es, "biasc", [128, 8], F32)
    for i, v in enumerate([128.0 * L2_EPS, L2_EPS, 128.0 * RMS_EPS, RMS_EPS, 1.0]):
        K.op("pool", lambda e, i=i, v=v: e.memset(c["bias"][:, i:i + 1], v), [], [c["bias"].b])
    return c


PARAMS = [("attn_norm", [DEPTH, D]), ("w_in", [DEPTH, D, IN_COLS]), ("gdn_conv", [DEPTH, 4, 3072]),
          ("gdn_a_log", [DEPTH, H]), ("gdn_dt_bias", [DEPTH, H]), ("gdn_o_norm", [DEPTH, HD]),
          ("sb_q_norm", [DEPTH, HD]), ("sb_k_norm", [DEPTH, HD]), ("sb_o_norm", [DEPTH, HD]),
          ("w_out", [DEPTH, D, D]), ("ffn_norm", [DEPTH, D]), ("w_up", [DEPTH, D, 2 * D_FF]),
          ("ffn_conv", [DEPTH, 3, 2 * D_FF]), ("ffn_conv_bias", [DEPTH, 2 * D_FF]),
          ("w_down", [DEPTH, D_FF, D])]


def build(mode="full"):
    nc = bass.Bass("TRN2", target_bir_lowering=False)
    only = mode.startswith("only")
    small = {"attn_norm", "gdn_conv", "gdn_a_log", "gdn_dt_bias", "gdn_o_norm", "sb_q_norm", "sb_k_norm", "sb_o_norm",
             "ffn_norm", "ffn_conv", "ffn_conv_bias"}
    ins = {"x": nc.dram_tensor("x", [S, D], F32, kind="ExternalInput").ap()}
    for name, shape in PARAMS:
        if only and name not in small:
            continue
        ins[name] = nc.dram_tensor(name, shape, F32, kind="ExternalInput").ap()
    y = nc.dram_tensor("y", [S, D], F32, kind="ExternalOutput").ap()
    dbg = "ExternalOutput" if mode.startswith("dbg") else "Internal"
    sin = "ExternalInput" if only else dbg
    scratch = {}
    for nm in ("gq", "gk", "gv", "gz", "sq", "sk"):
        scratch[nm] = nc.dram_tensor(nm + "_scr", [H, 128, S], BF16, kind=sin).ap()
        scratch[nm + "_b"] = Buf(nm + "_scr")
    scratch["sv"] = nc.dram_tensor("sv_scr", [NT, 128, 1024], BF16, kind=sin).ap()
    scratch["sv_b"] = Buf("sv_scr")
    for nm in ("beta", "gcum"):
        scratch[nm] = nc.dram_tensor(nm + "_scr", [H, S], F32, kind=sin).ap()
        scratch[nm + "_b"] = Buf(nm + "_scr")
    scratch["mix"] = nc.dram_tensor("mix_scr", [16, 128, S], BF16, kind=("ExternalOutput" if only else dbg)).ap()
    scratch["mix_b"] = Buf("mix_scr")
    scratch.update({
        "a": nc.dram_tensor("a_scr", [NT, 128, NFF, 128], BF16).ap(),
        "a_buf": Buf("a_scr"),
        "x1": nc.dram_tensor("x1_scr", [S, D], F32).ap(),
        "x2": nc.dram_tensor("x2_scr", [S, D], F32).ap(),
    })
    K = Sched(nc)
    with ExitStack() as es:
        C = Ctx(nc, K, es)
        consts = make_consts(C, es)
        xb, yb = Buf("xin"), Buf("yout")
        if mode == "ffn0":
            phase_ffn(C, 0, ins["x"], xb, y, yb, ins, consts, scratch)
        if mode in ("dbg_inproj", "dbg_gdn"):
            phase_inproj(C, 0, ins["x"], xb, ins, consts, scratch)
        if mode in ("dbg_gdn", "only_gdn"):
            phase_gdn(C, 0, ins, consts, scratch)
        if mode == "only_sb":
            phase_sb(C, 0, ins, consts, scratch)
        if mode == "full":
            x1b, x2b = Buf("x1s"), Buf("x2s")
            cur, curb = ins["x"], xb
            for l in range(DEPTH):
                K.scope = f"inproj{l}"
                phase_inproj(C, l, cur, curb, ins, consts, scratch)
                K.scope = f"gdn{l}"
                phase_gdn(C, l, ins, consts, scratch)
                K.scope = f"sb{l}"
                phase_sb(C, l, ins, consts, scratch)
                K.scope = f"outproj{l}"
                phase_outproj(C, l, cur, curb, scratch["x1"], x1b, ins, consts, scratch)
                K.scope = f"ffn{l}"
                if l == DEPTH - 1:
                    phase_ffn(C, l, scratch["x1"], x1b, y, yb, ins, consts, scratch)
                else:
                    phase_ffn(C, l, scratch["x1"], x1b, scratch["x2"], x2b, ins, consts, scratch)
                    cur, curb = scratch["x2"], x2b
        K.barrier()
        with nc.Block() as block:
            @block.tensor
            def _(e):
                K.emit("pe", e)

            @block.scalar
            def _(e):
                K.emit("act", e)

            @block.vector
            def _(e):
                K.emit("dve", e)

            @block.gpsimd
            def _(e):
                K.emit("pool", e)

            @block.sync
            def _(e):
                K.emit("sp", e)
    return nc, K


def kernel(**inputs):
    nc, K = build("full")
    x = np.ascontiguousarray(inputs["x"], dtype=np.float32)
    in_maps = []
    for c in range(8):
        m = {"x": x[c]}
        for name, _ in PARAMS:
            m[name] = np.ascontiguousarray(inputs[name], dtype=np.float32)
        in_maps.append(m)
    res = run_bass_kernel_spmd(nc, in_maps, core_ids=list(range(8)))
    return np.stack([r["y"] for r in res.results], axis=0)
```

```python
import numpy as np
from contextlib import ExitStack
import concourse.bass as bass
import concourse.mybir as mybir
from concourse.bass_utils import run_bass_kernel_spmd

F32 = mybir.dt.float32
BF16 = mybir.dt.bfloat16
AF = mybir.ActivationFunctionType
ALU = mybir.AluOpType
AX = mybir.AxisListType

D = 2048
S = 2048
DEPTH = 2
NT = S // 128
KC = D // 128
H = 8
HD = 128
IN_COLS = 7184
D_FF = 5632
NFF = D_FF // 128
RMS_EPS = 1e-6
L2_EPS = 1e-6

ENGS = ["pe", "act", "dve", "pool", "sp"]
SCOPES = False


class Buf:
    __slots__ = ("name", "w", "r", "dsem", "dcount")

    def __init__(self, name):
        self.name = name
        self.w = None
        self.r = {}
        self.dsem = None
        self.dcount = 0


class Sched:
    def __init__(self, nc):
        self.nc = nc
        self.sem = {e: nc.alloc_semaphore("cnt_" + e) for e in ENGS}
        self.count = {e: 0 for e in ENGS}
        self.known = {e: {} for e in ENGS}
        self.ops = {e: [] for e in ENGS}
        self.dma_bufs = []
        self.nwaits = 0
        self.muted = False
        self.free_sems = []
        self.nsem = 0
        self.scope = "init"

    def _deps(self, eng, reads, writes, skip_sem=None):
        waits = {}
        known = self.known[eng]
        own = self.sem[eng]

        def need(sem, val):
            if eng == "pe" and sem is own:
                return
            if known.get(sem, 0) >= val:
                return
            if waits.get(sem, 0) < val:
                waits[sem] = val

        for b in reads:
            if b.w is not None:
                need(*b.w)
        for b in writes:
            if b.w is not None and b.w[0] is not skip_sem:
                need(*b.w)
            for s, v in b.r.items():
                need(s, v)
        for s, v in waits.items():
            known[s] = v
        self.nwaits += len(waits)
        return list(waits.items())

    def _commit(self, ev, reads, writes):
        for b in reads:
            if b.r.get(ev[0], 0) < ev[1]:
                b.r[ev[0]] = ev[1]
        for b in writes:
            b.w = ev
            b.r = {}

    def op(self, eng, fn, reads=(), writes=(), sig=True):
        if self.muted:
            return
        waits = self._deps(eng, reads, writes)
        if sig:
            self.count[eng] += 1
            ev = (self.sem[eng], self.count[eng])
            inc = (self.sem[eng], 1)
        else:
            assert eng == "pe"
            ev = (self.sem[eng], self.count[eng] + 1)
            inc = None
        self.ops[eng].append((waits, fn, inc, self.scope))
        self._commit(ev, reads, writes)

    def dma(self, q, out, in_, reads, writes, owner, grp=False):
        if self.muted:
            return
        if owner.dsem is None:
            if self.free_sems:
                owner.dsem, owner.dcount = self.free_sems.pop()
            else:
                owner.dsem = self.nc.alloc_semaphore(f"dsem{self.nsem}")
                owner.dcount = 0
                self.nsem += 1
            self.dma_bufs.append(owner)
        waits = self._deps(q, reads, writes, owner.dsem if grp else None)
        owner.dcount += 16
        ev = (owner.dsem, owner.dcount)
        self.ops[q].append((waits, (lambda e, o=out, i=in_: e.dma_start(out=o, in_=i)), (owner.dsem, 16), self.scope))
        self._commit(ev, reads, writes)

    def barrier(self):
        if self.muted:
            return
        evs = [(self.sem[e], self.count[e]) for e in ENGS if self.count[e] > 0]
        evs += [(b.dsem, b.dcount) for b in self.dma_bufs]
        for e in ENGS:
            waits = []
            for s, v in evs:
                if e == "pe" and s is self.sem["pe"]:
                    continue
                if s is self.sem[e]:
                    continue
                if self.known[e].get(s, 0) < v:
                    self.known[e][s] = v
                    waits.append((s, v))
            if waits:
                self.ops[e].append((waits, None, None, self.scope))
        for b in self.dma_bufs:
            self.free_sems.append((b.dsem, b.dcount))
            b.dsem = None
        self.dma_bufs = []

    def emit(self, ename, eng):
        cur, cm = None, None
        for waits, fn, inc, scope in self.ops[ename]:
            if SCOPES and scope != cur:
                if cm is not None:
                    cm.__exit__(None, None, None)
                cm = self.nc.named_scope(scope)
                cm.__enter__()
                cur = scope
            for s, v in waits:
                eng.wait_ge(s, v)
            if fn is None:
                continue
            ins = fn(eng)
            if inc is not None:
                ins.then_inc(inc[0], inc[1])
        if cm is not None:
            cm.__exit__(None, None, None)


class T:
    def __init__(self, h, name, nbufs=1):
        self.h = h
        self.b = Buf(name)
        self.bs = [Buf(f"{name}_{i}") for i in range(nbufs)] if nbufs > 1 else [self.b]

    def __getitem__(self, k):
        return self.h[k]


class Ctx:
    def __init__(self, nc, K, es):
        self.nc, self.K, self.es = nc, K, es
        self.uid = 0

    def sb(self, es, name, shape, dt, nbufs=1):
        self.uid += 1
        nm = f"{name}_{self.uid}"
        return T(es.enter_context(self.nc.sbuf_tensor(nm, list(shape), dt)), nm, nbufs)

    def ps(self, es, name, shape, dt=F32, nbufs=1):
        self.uid += 1
        nm = f"{name}_{self.uid}"
        return T(es.enter_context(self.nc.psum_tensor(nm, list(shape), dt)), nm, nbufs)


def phase_norm(C, x_ap, x_buf, gain_row_ap, hT, consts):
    K, nc = C.K, C.nc
    with ExitStack() as es:
        gbc = C.sb(es, "gbc", [128, D], F32)
        xt = [C.sb(es, f"xt{i}", [128, D], F32) for i in range(2)]
        hs = [C.sb(es, f"hs{i}", [128, D], BF16) for i in range(2)]
        junk = C.sb(es, "junk", [128, D], BF16)
        st = [C.sb(es, f"st{i}", [128, 4], F32) for i in range(2)]
        pt = [C.ps(es, f"pt{i}", [128, D], BF16) for i in range(2)]
        K.dma("sp", gbc[:], gain_row_ap.partition_broadcast(128), [], [gbc.b], gbc.b)
        for t in range(NT):
            i = t % 2
            x_, h_, s_, p_ = xt[i], hs[i], st[i], pt[i]
            K.dma("sp", x_[:], x_ap[t * 128:(t + 1) * 128, :], [x_buf], [x_.b], x_.b)
            K.op("act", lambda e, x_=x_, s_=s_: e.activation(out=junk[:], in_=x_[:], func=AF.Square,
                                                           accum_out=s_[:, 0:1]),
                 [x_.b], [s_.b])
            K.op("dve", lambda e, s_=s_: e.tensor_scalar(out=s_[:, 1:2], in0=s_[:, 0:1], scalar1=1.0 / D,
                                                       scalar2=RMS_EPS, op0=ALU.mult, op1=ALU.add),
                 [s_.b], [s_.b])
            K.op("pool", lambda e, s_=s_: e.tensor_tensor(out=s_[:, 2:3], in0=s_[:, 1:2],
                                                        in1=consts["neghalf"][:, 0:1], op=ALU.pow),
                 [s_.b, consts["neghalf"].b], [s_.b])
            K.op("dve", lambda e, x_=x_, h_=h_, s_=s_: e.scalar_tensor_tensor(
                out=h_[:], in0=x_[:], scalar=s_[:, 2:3], in1=gbc[:], op0=ALU.mult, op1=ALU.mult),
                 [x_.b, s_.b, gbc.b], [h_.b])
            for kc in range(KC):
                K.op("pe", lambda e, h_=h_, p_=p_, kc=kc: e.transpose(
                    out=p_[:, kc * 128:(kc + 1) * 128], in_=h_[:, kc * 128:(kc + 1) * 128],
                    identity=consts["ident"][:]),
                     [h_.b, consts["ident"].b], [p_.b], sig=(kc == KC - 1))
            K.op("act", lambda e, p_=p_, t=t: e.copy(
                out=hT[:, :, t * 128:(t + 1) * 128],
                in_=p_[:].rearrange("p (k n) -> p k n", n=128)),
                 [p_.b], [hT.bs[t // 4]])
        K.barrier()


def alloc_stage(C, es, n=3):
    C.stg = [C.sb(es, f"stg{i}", [128, 4, 512], F32) for i in range(n)]
    C.stg_i = 0


def load_w(C, slot, w_ap, c0, ncols, col_off=0, nk=KC, wbuf=None):
    K = C.K
    wbuf = wbuf if wbuf is not None else slot.b
    step = 4
    for k0 in range(0, nk, step):
        kk = min(step, nk - k0)
        st = C.stg[C.stg_i % len(C.stg)]
        C.stg_i += 1
        src = w_ap[k0 * 128:(k0 + kk) * 128, c0:c0 + ncols].rearrange("(kc p) n -> p kc n", p=128)
        K.dma("sp", st[:, 0:kk, 0:ncols], src, [], [st.b], st.b)
        K.op("dve", lambda e, st=st, k0=k0, kk=kk: e.tensor_copy(
            out=slot[:, k0:k0 + kk, col_off:col_off + ncols], in_=st[:, 0:kk, 0:ncols]), [st.b], [wbuf])


def phase_ffn(C, l, x_in, x_in_buf, x_out, x_out_buf, ins, consts, scratch):
    K, nc = C.K, C.nc
    a_scr, a_buf = scratch["a"], scratch["a_buf"]
    with ExitStack() as es:
        hT = C.sb(es, "hT", [128, KC, S], BF16, nbufs=4)
        wsl = [C.sb(es, f"wup{i}", [128, KC, 512], BF16) for i in range(3)]
        alloc_stage(C, es, 2)
        for g in range(2):
            load_w(C, wsl[g], ins["w_up"][l], g * 256, 256, 0)
            load_w(C, wsl[g], ins["w_up"][l], D_FF + g * 256, 256, 256)
        phase_norm(C, x_in, x_in_buf, ins["ffn_norm"][l:l + 1, :], hT, consts)
        with ExitStack() as es2:
            cw = C.sb(es2, "cw", [128, 4, 2 * NFF], F32)
            raw = [C.sb(es2, f"raw{i}", [128, 2 + S], F32) for i in range(4)]
            acc = [C.sb(es2, f"acc{i}", [128, S], F32) for i in range(2)]
            sg = C.sb(es2, "sg", [128, S], F32)
            aT = [C.sb(es2, f"aT{i}", [128, NT, 2, 128], BF16) for i in range(2)]
            pp = [C.ps(es2, f"pp{i}", [128, 512], F32) for i in range(8)]
            cwr = C.sb(es2, "cwr", [2 * NFF, 4, 128], F32)
            for tap in range(3):
                K.dma("sp", cwr[:, tap, :], ins["ffn_conv"][l, tap, :].rearrange("(j p) -> j p", p=128),
                      [], [cwr.b], cwr.b)
            K.dma("sp", cwr[:, 3, :], ins["ffn_conv_bias"][l, :].rearrange("(j p) -> j p", p=128),
                  [], [cwr.b], cwr.b)
            for tap in range(4):
                K.op("pe", lambda e, tap=tap: e.transpose(out=pp[0][:, tap * 128:tap * 128 + 2 * NFF],
                                                          in_=cwr[:, tap, :], identity=consts["identf"][0:2 * NFF, 0:2 * NFF]),
                     [cwr.b, consts["identf"].b], [pp[0].b], sig=(tap == 3))
            K.op("act", lambda e: e.copy(out=cw[:], in_=pp[0][:].rearrange("p (t n) -> p t n", n=128)[:, :, 0:2 * NFF]),
                 [pp[0].b], [cw.b])
            for r in raw:
                K.op("pool", lambda e, r=r: e.memset(r[:, 0:2], 0.0), [], [r.b])
            ngrp = NFF // 2
            pidx = 0
            for g in range(ngrp):
                w_ = wsl[g % 3]
                if g + 2 < ngrp:
                    load_w(C, wsl[(g + 2) % 3], ins["w_up"][l], (g + 2) * 256, 256, 0)
                    load_w(C, wsl[(g + 2) % 3], ins["w_up"][l], D_FF + (g + 2) * 256, 256, 256)
                for jj in range(2):
                    j = g * 2 + jj
                    rg, ru = raw[(j % 2) * 2], raw[(j % 2) * 2 + 1]
                    for which, r_ in ((0, rg), (1, ru)):
                        co = which * 256 + jj * 128
                        for tc in range(4):
                            p_ = pp[pidx % 8]
                            pidx += 1
                            for kc in range(KC):
                                K.op("pe", lambda e, p_=p_, w_=w_, kc=kc, co=co, tc=tc: e.matmul(
                                    p_[:], lhsT=w_[:, kc, co:co + 128], rhs=hT[:, kc, tc * 512:(tc + 1) * 512],
                                    start=(kc == 0), stop=(kc == KC - 1)),
                                     [w_.b, hT.bs[tc]], [p_.b], sig=(kc == KC - 1))
                            K.op("act", lambda e, p_=p_, r_=r_, tc=tc: e.copy(
                                out=r_[:, 2 + tc * 512:2 + (tc + 1) * 512], in_=p_[:]), [p_.b], [r_.b])
                    a_ = aT[(j // 2) % 2]
                    for which, r_, ac in ((0, rg, acc[0]), (1, ru, acc[1])):
                        ch = which * NFF + j
                        K.op("act", lambda e, r_=r_, ac=ac, ch=ch: e.activation(
                            out=ac[:], in_=r_[:, 2:2 + S], func=AF.Identity,
                            scale=cw[:, 2, ch:ch + 1], bias=cw[:, 3, ch:ch + 1]), [r_.b, cw.b], [ac.b])
                        K.op("dve", lambda e, r_=r_, ac=ac, ch=ch: e.scalar_tensor_tensor(
                            out=ac[:], in0=r_[:, 1:1 + S], scalar=cw[:, 1, ch:ch + 1], in1=ac[:],
                            op0=ALU.mult, op1=ALU.add), [r_.b, cw.b, ac.b], [ac.b])
                        K.op("dve", lambda e, r_=r_, ac=ac, ch=ch: e.scalar_tensor_tensor(
                            out=ac[:], in0=r_[:, 0:S], scalar=cw[:, 0, ch:ch + 1], in1=ac[:],
                            op0=ALU.mult, op1=ALU.add), [r_.b, cw.b, ac.b], [ac.b])
                    K.op("act", lambda e: e.activation(out=sg[:], in_=acc[0][:], func=AF.Silu),
                         [acc[0].b], [sg.b])
                    K.op("dve", lambda e, a_=a_, j=j: e.tensor_tensor(out=a_[:, :, j % 2, :], in0=sg[:].rearrange('p (t n) -> p t n', n=128), in1=acc[1][:].rearrange('p (t n) -> p t n', n=128), op=ALU.mult),
                         [sg.b, acc[1].b], [a_.b])
                    if j % 2 == 1:
                        K.dma("sp", a_scr[:, :, j - 1:j + 1, :].rearrange("t p j n -> p t (j n)"),
                              a_[:].rearrange("p t j n -> p t (j n)"), [a_.b], [a_buf], a_.b)
        K.barrier()
    with ExitStack() as es:
        wd = [C.sb(es, f"wd{i}", [128, NFF, 512], BF16) for i in range(2)]
        alloc_stage(C, es)
        at = [C.sb(es, f"at{i}", [128, NFF, 128], BF16) for i in range(3)]
        xr = [C.sb(es, f"xr{i}", [128, 512], F32) for i in range(3)]
        xo = [C.sb(es, f"xo{i}", [128, 512], F32) for i in range(3)]
        pp = [C.ps(es, f"pd{i}", [128, 512], F32) for i in range(4)]
        n = 0
        load_w(C, wd[0], ins["w_down"][l], 0, 512, 0, nk=NFF)
        for q in range(4):
            w_ = wd[q % 2]
            if q + 1 < 4:
                load_w(C, wd[(q + 1) % 2], ins["w_down"][l], (q + 1) * 512, 512, 0, nk=NFF)
            for t in range(NT):
                a_, xr_, xo_, p_ = at[n % 3], xr[n % 3], xo[n % 3], pp[n % 4]
                n += 1
                K.dma("sp", a_[:], a_scr[t], [a_buf], [a_.b], a_.b)
                K.dma("sp", xr_[:], x_in[t * 128:(t + 1) * 128, q * 512:(q + 1) * 512], [x_in_buf], [xr_.b], xr_.b)
                for j in range(NFF):
                    K.op("pe", lambda e, p_=p_, a_=a_, w_=w_, j=j: e.matmul(
                        p_[:], lhsT=a_[:, j, :], rhs=w_[:, j, :], start=(j == 0), stop=(j == NFF - 1)),
                         [a_.b, w_.b], [p_.b], sig=(j == NFF - 1))
                K.op("dve", lambda e, p_=p_, xr_=xr_, xo_=xo_: e.tensor_tensor(
                    out=xo_[:], in0=p_[:], in1=xr_[:], op=ALU.add), [p_.b, xr_.b], [xo_.b])
                K.dma("sp", x_out[t * 128:(t + 1) * 128, q * 512:(q + 1) * 512], xo_[:], [xo_.b], [x_out_buf], xo_.b)
        K.barrier()


def phase_inproj(C, l, x_in, x_in_buf, ins, consts, scr):
    K, nc = C.K, C.nc
    W = ins["w_in"][l]
    with ExitStack() as es:
        hT = C.sb(es, "hT", [128, KC, S], BF16, nbufs=4)
        wsl = [C.sb(es, f"win{i}", [128, KC, 512], BF16) for i in range(3)]
        wba = C.sb(es, "wba", [128, KC, 16], BF16)
        alloc_stage(C, es, 2)
        load_w(C, wsl[0], W, 0, 512, 0)
        load_w(C, wsl[1], W, 512, 512, 0)
        phase_norm(C, x_in, x_in_buf, ins["attn_norm"][l:l + 1, :], hT, consts)
        raw = [C.sb(es, f"raw{i}", [128, 3 + S], F32) for i in range(2)]
        acc = [C.sb(es, f"acc{i}", [128, S], F32) for i in range(2)]
        ysq = C.sb(es, "ysq", [128, S], BF16)
        lnv = C.sb(es, "lnv", [128, S], F32)
        ob = [C.sb(es, f"ob{i}", [128, S], BF16) for i in range(2)]
        svt = [C.sb(es, f"svt{i}", [128, 512], BF16) for i in range(2)]
        gcw = C.sb(es, "gcw", [128, 4, 24], F32)
        gcr = C.sb(es, "gcr", [24, 4, 128], F32)
        nrm = C.sb(es, "nrm", [128, 2], F32)
        hp = C.sb(es, "hp", [8, 4], F32)
        pm = [C.ps(es, f"pm{i}", [128, 512], F32) for i in range(6)]
        po = [C.ps(es, f"po{i}", [128, 512], F32) for i in range(2)]
        ones_bf = consts["ones_bf"]
        for tap in range(4):
            K.dma("sp", gcr[:, tap, :], ins["gdn_conv"][l, tap, :].rearrange("(j p) -> j p", p=128), [], [gcr.b], gcr.b)
        for tap in range(4):
            K.op("pe", lambda e, tap=tap: e.transpose(out=pm[0][:, tap * 32:tap * 32 + 24], in_=gcr[:, tap, :],
                                                      identity=consts["identf"][0:24, 0:24]),
                 [gcr.b, consts["identf"].b], [pm[0].b], sig=(tap == 3))
        K.op("act", lambda e: e.copy(out=gcw[:], in_=pm[0][:, 0:128].rearrange("p (t n) -> p t n", n=32)[:, :, 0:24]),
             [pm[0].b], [gcw.b])
        K.dma("sp", nrm[:, 0:1], ins["sb_q_norm"][l, :].rearrange("(p o) -> p o", o=1), [], [nrm.b], nrm.b)
        K.dma("sp", nrm[:, 1:2], ins["sb_k_norm"][l, :].rearrange("(p o) -> p o", o=1), [], [nrm.b], nrm.b)
        K.dma("sp", hp[:, 0:1], ins["gdn_a_log"][l, :].rearrange("(p o) -> p o", o=1), [], [hp.b], hp.b)
        K.dma("sp", hp[:, 1:2], ins["gdn_dt_bias"][l, :].rearrange("(p o) -> p o", o=1), [], [hp.b], hp.b)
        K.op("act", lambda e: e.activation(out=hp[:, 2:3], in_=hp[:, 0:1], func=AF.Exp), [hp.b], [hp.b])
        K.op("dve", lambda e: e.tensor_scalar(out=hp[:, 2:3], in0=hp[:, 2:3], scalar1=-1.0, scalar2=None, op0=ALU.mult),
             [hp.b], [hp.b])
        for r in raw:
            K.op("pool", lambda e, r=r: e.memset(r[:, 0:3], 0.0), [], [r.b])
        st = {"pi": 0, "ti": 0}

        def mm_tile(w_, co, M=128):
            ps = []
            for tc in range(4):
                p_ = pm[st["pi"] % 6]
                st["pi"] += 1
                for kc in range(KC):
                    K.op("pe", lambda e, p_=p_, w_=w_, kc=kc, co=co, tc=tc, M=M: e.matmul(
                        p_[0:M, :], lhsT=w_[:, kc, co:co + M], rhs=hT[:, kc, tc * 512:(tc + 1) * 512],
                        start=(kc == 0), stop=(kc == KC - 1)),
                         [w_.b, hT.bs[tc]], [p_.b], sig=(kc == KC - 1))
                ps.append(p_)
            return ps

        def l2_or_rms(src, out_, scale, bias, gain_ap):
            K.op("act", lambda e: e.activation(out=ysq[:], in_=src[:, 0:S], func=AF.Square), [src.b], [ysq.b])
            for tc in range(4):
                p_ = po[tc % 2]
                K.op("pe", lambda e, p_=p_, tc=tc: e.matmul(p_[:], lhsT=ones_bf[:], rhs=ysq[:, tc * 512:(tc + 1) * 512],
                                                            start=True, stop=True), [ones_bf.b, ysq.b], [p_.b])
                K.op("act", lambda e, p_=p_, tc=tc: e.activation(out=lnv[:, tc * 512:(tc + 1) * 512], in_=p_[:], func=AF.Ln,
                                                                 scale=scale, bias=consts["bias"][:, bias:bias + 1]),
                     [p_.b, consts["bias"].b], [lnv.b])
            K.op("act", lambda e: e.activation(out=lnv[:], in_=lnv[:], func=AF.Exp, scale=-0.5), [lnv.b], [lnv.b])
            if gain_ap is None:
                K.op("dve", lambda e: e.tensor_tensor(out=out_[:], in0=src[:, 0:S], in1=lnv[:], op=ALU.mult),
                     [src.b, lnv.b], [out_.b])
            else:
                K.op("dve", lambda e: e.scalar_tensor_tensor(out=out_[:], in0=src[:, 0:S], scalar=gain_ap, in1=lnv[:],
                                                             op0=ALU.mult, op1=ALU.mult), [src.b, lnv.b, nrm.b], [out_.b])

        groups = []
        for kind, c0 in [("gq", 0), ("gk", 1024), ("gv", 2048), ("gz", 3072), ("sq", 4112), ("sk", 5136), ("sv", 6160)]:
            groups += [(kind, c0, 0), (kind, c0 + 512, 1)]
        for gi, (kind, c0, half) in enumerate(groups):
            w_ = wsl[gi % 3]
            if gi + 2 < len(groups):
                load_w(C, wsl[(gi + 2) % 3], W, groups[gi + 2][1], 512, 0)
            if kind == "gv" and half == 0:
                load_w(C, wba, W, 4096, 16, 0)
            if kind == "sv":
                for t in range(NT):
                    sv_ = svt[t % 2]
                    p_ = pm[st["pi"] % 6]
                    st["pi"] += 1
                    for kc in range(KC):
                        K.op("pe", lambda e, p_=p_, w_=w_, kc=kc, t=t: e.matmul(
                            p_[:], lhsT=hT[:, kc, t * 128:(t + 1) * 128], rhs=w_[:, kc, :],
                            start=(kc == 0), stop=(kc == KC - 1)),
                             [w_.b, hT.bs[t // 4]], [p_.b], sig=(kc == KC - 1))
                    K.op("act", lambda e, p_=p_, sv_=sv_: e.copy(out=sv_[:], in_=p_[:]), [p_.b], [sv_.b])
                    K.dma("sp", scr["sv"][t][:, half * 512:(half + 1) * 512], sv_[:], [sv_.b], [scr["sv_b"]], sv_.b)
                continue
            for hh in range(half * 4, half * 4 + 4):
                ti = st["ti"]
                st["ti"] += 1
                ps = mm_tile(w_, (hh % 4) * 128)
                r_, a_, o_ = raw[ti % 2], acc[ti % 2], ob[ti % 2]
                if kind in ("gq", "gk", "gv"):
                    ct = {"gq": 0, "gk": 8, "gv": 16}[kind] + hh
                    for tc in range(4):
                        K.op("act", lambda e, p_=ps[tc], r_=r_, tc=tc: e.copy(out=r_[:, 3 + tc * 512:3 + (tc + 1) * 512], in_=p_[:]),
                             [ps[tc].b], [r_.b])
                    K.op("act", lambda e, r_=r_, a_=a_, ct=ct: e.activation(out=a_[:], in_=r_[:, 3:3 + S], func=AF.Identity,
                                                                          scale=gcw[:, 3, ct:ct + 1]), [r_.b, gcw.b], [a_.b])
                    for tap in range(3):
                        K.op("dve", lambda e, r_=r_, a_=a_, ct=ct, tap=tap: e.scalar_tensor_tensor(
                            out=a_[:], in0=r_[:, tap:tap + S], scalar=gcw[:, tap, ct:ct + 1], in1=a_[:],
                            op0=ALU.mult, op1=ALU.add), [r_.b, gcw.b, a_.b], [a_.b])
                    if kind == "gv":
                        K.op("act", lambda e, a_=a_, o_=o_: e.activation(out=o_[:], in_=a_[:], func=AF.Silu), [a_.b], [o_.b])
                    else:
                        K.op("act", lambda e, a_=a_: e.activation(out=a_[:], in_=a_[:], func=AF.Silu), [a_.b], [a_.b])
                        if kind == "gq":
                            l2_or_rms(a_, o_, 128.0, 0, None)
                        else:
                            l2_or_rms(a_, o_, 1.0, 1, None)
                    K.dma("sp", scr[kind][hh], o_[:], [o_.b], [scr[kind + "_b"]], o_.b)
                elif kind == "gz":
                    for tc in range(4):
                        K.op("act", lambda e, p_=ps[tc], o_=o_, tc=tc: e.activation(
                            out=o_[:, tc * 512:(tc + 1) * 512], in_=p_[:], func=AF.Silu), [ps[tc].b], [o_.b])
                    K.dma("sp", scr["gz"][hh], o_[:], [o_.b], [scr["gz_b"]], o_.b)
                else:
                    for tc in range(4):
                        K.op("act", lambda e, p_=ps[tc], a_=a_, tc=tc: e.copy(out=a_[:, tc * 512:(tc + 1) * 512], in_=p_[:]),
                             [ps[tc].b], [a_.b])
                    if kind == "sq":
                        l2_or_rms(a_, o_, 1.0, 2, nrm[:, 0:1])
                    else:
                        l2_or_rms(a_, o_, 1.0 / 128.0, 3, nrm[:, 1:2])
                    K.dma("sp", scr[kind][hh], o_[:], [o_.b], [scr[kind + "_b"]], o_.b)
            if kind == "gz" and half == 1:
                bb, ee, cc, rm = acc[0], acc[1], lnv, ysq
                for which in range(2):
                    ps = mm_tile(wba, which * 8, M=8)
                    for tc in range(4):
                        sl = slice(tc * 512, (tc + 1) * 512)
                        if which == 0:
                            K.op("act", lambda e, p_=ps[tc], sl=sl: e.activation(out=bb[0:8, sl], in_=p_[0:8, :], func=AF.Sigmoid),
                                 [ps[tc].b], [bb.b])
                        else:
                            K.op("act", lambda e, p_=ps[tc], sl=sl: e.activation(out=ee[0:8, sl], in_=p_[0:8, :], func=AF.Exp,
                                                                              bias=hp[:, 1:2]), [ps[tc].b, hp.b], [ee.b])
                K.op("act", lambda e: e.activation(out=ee[0:8, :], in_=ee[0:8, :], func=AF.Ln, bias=consts["bias"][0:8, 4:5]),
                     [ee.b, consts["bias"].b], [ee.b])
                K.op("dve", lambda e: e.tensor_scalar(out=ee[0:8, :], in0=ee[0:8, :], scalar1=hp[:, 2:3], scalar2=None,
                                                      op0=ALU.mult), [ee.b, hp.b], [ee.b])
                K.op("pool", lambda e: e.memset(rm[0:8, :], 1.0), [], [rm.b])
                K.op("pool", lambda e: e.memset(rm[0:8, :].rearrange("p (c n) -> p c n", n=128)[:, :, 0:1], 0.0), [rm.b], [rm.b])
                K.op("dve", lambda e: e.tensor_tensor_scan(out=cc[0:8, :], data0=rm[0:8, :], data1=ee[0:8, :], initial=0.0,
                                                           op0=ALU.mult, op1=ALU.add), [ee.b, rm.b], [cc.b])
                K.dma("sp", scr["beta"], bb[0:8, :], [bb.b], [scr["beta_b"]], bb.b)
                K.dma("sp", scr["gcum"], cc[0:8, :], [cc.b], [scr["gcum_b"]], cc.b)
        K.barrier()


GDN_STOP = 99


class _Stop(Exception):
    pass


_KREF = []


def _chk(k):
    if GDN_STOP <= k:
        _KREF[0].muted = True


def phase_gdn(C, l, ins, consts, scr):
    _KREF[:] = [C.K]
    _phase_gdn(C, l, ins, consts, scr)
    C.K.muted = False
    C.K.barrier()


def _phase_gdn(C, l, ins, consts, scr):
    K, nc = C.K, C.nc
    U32 = mybir.dt.uint32
    ident, onesf, identf, ones_bf = consts["ident"], consts["onesf"], consts["identf"], consts["ones_bf"]
    with ExitStack() as es:
        gT = C.sb(es, "gT", [8, S], F32)
        bT = C.sb(es, "bT", [8, S], F32)
        tk = C.sb(es, "tk", [128, 6, 128], F32)
        ogain = C.sb(es, "ogain", [128, 1], F32)
        masks = C.sb(es, "masks", [128, 14, 128], F32)
        masks8 = C.sb(es, "masks8", [128, 14, 512], mybir.dt.uint8)
        sel = C.sb(es, "sel", [128, 128], F32)
        zer = C.sb(es, "zer", [128, 128], F32)
        mneg = C.sb(es, "mneg", [128, 128], F32)
        B = [C.ps(es, f"B{i}", [128, 512], F32) for i in range(8)]
        K.dma("sp", gT[:], scr["gcum"], [scr["gcum_b"]], [gT.b], gT.b)
        K.dma("sp", bT[:], scr["beta"], [scr["beta_b"]], [bT.b], bT.b)
        K.dma("sp", ogain[:], ins["gdn_o_norm"][l, :].rearrange("(p o) -> p o", o=1), [], [ogain.b], ogain.b)
        K.op("pool", lambda e: e.memset(zer[:], 0.0), [], [zer.b])
        K.op("pool", lambda e: e.affine_select(out=mneg[:], in_=zer[:], pattern=[[1, 128]], compare_op=ALU.is_ge,
                                               fill=-30000.0, base=0, channel_multiplier=-1), [zer.b], [mneg.b])
        K.op("pool", lambda e: e.affine_select(out=sel[:], in_=onesf[:], pattern=[[0, 128]], compare_op=ALU.is_ge,
                                               fill=0.0, base=-127, channel_multiplier=1), [onesf.b], [sel.b])
        for lv in range(7):
            n = 1 << lv
            nb = 128 // (2 * n)
            specs = [
                (lv, [(-n, 1, [[-2 * n, nb], [0, 2], [0, n]]), (2 * n - 1, -1, [[2 * n, nb], [0, 2], [0, n]]),
                      (0, 0, [[0, nb], [-1, 2], [0, n]])]),
                (7 + lv, [(0, 1, [[-2 * n, nb], [0, 2], [0, n]]), (n - 1, -1, [[2 * n, nb], [0, 2], [0, n]]),
                          (-1, 0, [[0, nb], [1, 2], [0, n]])]),
            ]
            for mi, passes in specs:
                for pi, (base, cm, pat) in enumerate(passes):
                    src = onesf if pi == 0 else masks
                    K.op("pool", lambda e, mi=mi, base=base, cm=cm, pat=pat, pi=pi: e.affine_select(
                        out=masks[:, mi, :], in_=(onesf[:] if pi == 0 else masks[:, mi, :]), pattern=pat,
                        compare_op=ALU.is_ge, fill=0.0, base=base, channel_multiplier=cm),
                         [src.b], [masks.b])
        K.op("dve", lambda e: e.tensor_copy(out=masks8[:].rearrange("p m (r n) -> p m r n", n=128),
                                            in_=masks[:].unsqueeze(2).to_broadcast([128, 14, 4, 128])), [masks.b], [masks8.b])
        for which, srcT in ((0, gT), (1, bT)):
            for c in range(NT):
                K.op("pe", lambda e, which=which, srcT=srcT, c=c: e.transpose(
                    out=B[which][:, c * 8:c * 8 + 8], in_=srcT[:, c * 128:(c + 1) * 128],
                    identity=identf[0:8, 0:8]), [srcT.b, identf.b], [B[which].b], sig=(c == NT - 1))
            K.op("act", lambda e, which=which: e.copy(out=tk[:, which, :], in_=B[which][:, 0:128]),
                 [B[which].b], [tk.b])
        K.op("dve", lambda e: e.tensor_scalar(out=tk[:, 2, :], in0=tk[:, 1, :], scalar1=-1.0, scalar2=None, op0=ALU.mult),
             [tk.b], [tk.b])
        K.op("act", lambda e: e.activation(out=tk[:, 3, :], in_=tk[:, 0, :], func=AF.Exp), [tk.b], [tk.b])
        K.op("pe", lambda e: e.matmul(B[2][:, 0:128], lhsT=sel[:], rhs=tk[:, 0, :], start=True, stop=True),
             [sel.b, tk.b], [B[2].b])
        K.op("act", lambda e: e.activation(out=tk[:, 5, :], in_=B[2][:, 0:128], func=AF.Exp), [B[2].b], [tk.b])
        K.op("dve", lambda e: e.tensor_tensor(out=tk[:, 4, :], in0=B[2][:, 0:128], in1=tk[:, 0, :], op=ALU.subtract),
             [B[2].b, tk.b], [tk.b])
        K.op("act", lambda e: e.activation(out=tk[:, 4, :], in_=tk[:, 4, :], func=AF.Exp), [tk.b], [tk.b])
        _chk(1)

        def col(which, h):
            return tk[:, which, :].rearrange("p (c h) -> p c h", h=8)[:, :, h:h + 1]

        def v3(ap):
            return ap.rearrange("p (c n) -> p c n", n=128)

        with ExitStack() as eg:
            slots = []
            for hi in range(4):
                sl = {nm: C.sb(eg, f"{nm}{hi}", [128, S], BF16, nbufs=16) for nm in ("wT", "ub", "kd", "qg", "AT", "oT")}
                sl["S32"] = C.sb(eg, f"S32{hi}", [128, 128], F32)
                sl["Sbf"] = C.sb(eg, f"Sbf{hi}", [128, 128], BF16)
                sl["vn"] = [C.sb(eg, f"vn{hi}_{i}", [128, 128], BF16) for i in range(2)]
                slots.append(sl)
            for grp in range(2):
                heads = list(range(grp * 4, grp * 4 + 4))
                per = {h: slots[h % 4] for h in heads}
                if grp == 0:
                  qT = C.sb(eg, "qT", [128, S], BF16)
                  kT = C.sb(eg, "kT", [128, S], BF16)
                  vT = C.sb(eg, "vT", [128, S], BF16)
                  Rb = C.sb(eg, "Rb", [128, S], F32)
                  E = C.sb(eg, "E", [128, S], F32)
                  dU = C.sb(eg, "dU", [128, S], BF16)
                  LnT = C.sb(eg, "LnT", [128, S], BF16, nbufs=4)
                  Dm = C.sb(eg, "Dm", [128, S], BF16, nbufs=4)
                  DTm = C.sb(eg, "DTm", [128, S], BF16, nbufs=4)
                  Ysb = C.sb(eg, "Ysb", [128, S], BF16, nbufs=4)
                  kg = C.sb(eg, "kg", [128, S], BF16)
                  vtok = C.sb(eg, "vtok", [128, S], BF16)
                  zs = C.sb(eg, "zs", [128, S], BF16)
                for h in heads:
                    P = per[h]
                    K.dma("sp", qT[:], scr["gq"][h], [scr["gq_b"]], [qT.b], qT.b)
                    K.dma("sp", kT[:], scr["gk"][h], [scr["gk_b"]], [kT.b], kT.b)
                    K.dma("sp", vT[:], scr["gv"][h], [scr["gv_b"]], [vT.b], vT.b)
                    K.dma("sp", Rb[:], scr["gcum"][h, :].partition_broadcast(128), [scr["gcum_b"]], [Rb.b], Rb.b)
                    K.op("act", lambda e: e.activation(out=E[:], in_=Rb[:], func=AF.Exp), [Rb.b], [E.b])
                    K.op("dve", lambda e, P=P: e.tensor_tensor(out=P["qg"][:], in0=qT[:], in1=E[:], op=ALU.mult),
                         [qT.b, E.b], P["qg"].bs)
                    K.op("dve", lambda e, h=h: e.tensor_tensor(out=v3(E[:]), in0=v3(Rb[:]),
                                                               in1=col(0, h).to_broadcast([128, NT, 128]), op=ALU.subtract),
                         [Rb.b, tk.b], [E.b])
                    K.op("dve", lambda e: e.tensor_tensor(out=v3(E[:]), in0=v3(E[:]),
                                                          in1=mneg[:].unsqueeze(1).to_broadcast([128, NT, 128]), op=ALU.add),
                         [E.b, mneg.b], [E.b])
                    K.op("act", lambda e: e.activation(out=dU[:], in_=E[:], func=AF.Exp), [E.b], [dU.b])
                    for q in range(4):
                        for i in range(4):
                            cs = slice((q * 4 + i) * 128, (q * 4 + i + 1) * 128)
                            K.op("pe", lambda e, q=q, i=i, cs=cs: e.matmul(
                                B[q][:, i * 128:(i + 1) * 128], lhsT=kT[:, cs], rhs=kT[:, cs], start=True, stop=True),
                                 [kT.b], [B[q].b], sig=(i == 3))
                        for i in range(4):
                            c = q * 4 + i
                            cs = slice(c * 128, (c + 1) * 128)
                            K.op("dve", lambda e, q=q, i=i, cs=cs, c=c, h=h: e.scalar_tensor_tensor(
                                out=LnT[:, cs], in0=B[q][:, i * 128:(i + 1) * 128],
                                scalar=tk[:, 2, c * 8 + h:c * 8 + h + 1], in1=dU[:, cs], op0=ALU.mult, op1=ALU.mult),
                                 [B[q].b, tk.b, dU.b], [LnT.bs[q]])
                    for q in range(4):
                        for i in range(4):
                            cs = slice((q * 4 + i) * 128, (q * 4 + i + 1) * 128)
                            K.op("pe", lambda e, q=q, i=i, cs=cs: e.matmul(
                                B[4 + q][:, i * 128:(i + 1) * 128], lhsT=kT[:, cs], rhs=qT[:, cs], start=True, stop=True),
                                 [kT.b, qT.b], [B[4 + q].b], sig=(i == 3))
                        qs = slice(q * 512, (q + 1) * 512)
                        K.op("dve", lambda e, q=q, qs=qs, P=P: e.tensor_tensor(
                            out=P["AT"][:, qs], in0=B[4 + q][:], in1=dU[:, qs], op=ALU.mult),
                             [B[4 + q].b, dU.b], P["AT"].bs[q * 4:q * 4 + 4])
                    _chk(2)
                    for src, b0 in ((kT, 0), (vT, 2)):
                        for hb in range(2):
                            pv = B[b0 + hb][:].bitcast(BF16)
                            for i in range(8):
                                cs = slice((hb * 8 + i) * 128, (hb * 8 + i + 1) * 128)
                                K.op("pe", lambda e, pv=pv, i=i, cs=cs, src=src: e.transpose(
                                    out=pv[:, i * 128:(i + 1) * 128], in_=src[:, cs], identity=ident[:]),
                                     [src.b, ident.b], [B[b0 + hb].b], sig=(i == 7))
                    for hb in range(2):
                        hs_ = slice(hb * 1024, (hb + 1) * 1024)
                        pk = B[hb][:].bitcast(BF16)
                        pvv = B[2 + hb][:].bitcast(BF16)
                        K.op("dve", lambda e, h=h, hb=hb, hs_=hs_, pk=pk: e.tensor_tensor(
                            out=v3(kg[:, hs_]), in0=v3(pk), in1=col(3, h)[:, hb * 8:(hb + 1) * 8, :].to_broadcast([128, 8, 128]),
                            op=ALU.mult), [B[hb].b, tk.b], [kg.b])
                        K.op("dve", lambda e, h=h, hb=hb, hs_=hs_, pk=pk, P=P: e.tensor_tensor(
                            out=v3(P["kd"][:, hs_]), in0=v3(pk), in1=col(4, h)[:, hb * 8:(hb + 1) * 8, :].to_broadcast([128, 8, 128]),
                            op=ALU.mult), [B[hb].b, tk.b], P["kd"].bs[hb * 8:(hb + 1) * 8])
                        K.op("act", lambda e, hs_=hs_, pvv=pvv: e.copy(out=vtok[:, hs_], in_=pvv), [B[2 + hb].b], [vtok.b])
                    _chk(3)
                    K.op("dve", lambda e: e.tensor_copy(out=v3(Dm[:]), in_=ident[:].unsqueeze(1).to_broadcast([128, NT, 128])),
                         [ident.b], Dm.bs)
                    K.op("dve", lambda e: e.tensor_copy(out=v3(DTm[:]), in_=ident[:].unsqueeze(1).to_broadcast([128, NT, 128])),
                         [ident.b], DTm.bs)
                    for lv in range(7):
                        for q in range(4):
                            s_ = q % 2
                            qs = slice(q * 512, (q + 1) * 512)
                            Yp, Zp, ZTp = B[s_ * 3], B[s_ * 3 + 1], B[s_ * 3 + 2]
                            for i in range(4):
                                cs = slice(q * 512 + i * 128, q * 512 + (i + 1) * 128)
                                K.op("pe", lambda e, cs=cs, i=i, Yp=Yp: e.matmul(
                                    Yp[:, i * 128:(i + 1) * 128], lhsT=LnT[:, cs], rhs=Dm[:, cs],
                                    start=True, stop=True), [LnT.bs[q], Dm.bs[q]], [Yp.b], sig=(i == 3))
                            K.op("act", lambda e, qs=qs, Yp=Yp: e.copy(out=Ysb[:, qs], in_=Yp[:]), [Yp.b], [Ysb.bs[q]])
                            for i in range(4):
                                cs = slice(q * 512 + i * 128, q * 512 + (i + 1) * 128)
                                K.op("pe", lambda e, cs=cs, i=i, Zp=Zp: e.matmul(
                                    Zp[:, i * 128:(i + 1) * 128], lhsT=DTm[:, cs], rhs=Ysb[:, cs],
                                    start=True, stop=True), [DTm.bs[q], Ysb.bs[q]], [Zp.b], sig=(i == 3))
                            for i in range(4):
                                cs = slice(q * 512 + i * 128, q * 512 + (i + 1) * 128)
                                K.op("pe", lambda e, cs=cs, i=i, ZTp=ZTp: e.matmul(
                                    ZTp[:, i * 128:(i + 1) * 128], lhsT=Ysb[:, cs], rhs=DTm[:, cs],
                                    start=True, stop=True), [DTm.bs[q], Ysb.bs[q]], [ZTp.b], sig=(i == 3))
                            K.op("dve", lambda e, qs=qs, lv=lv, Zp=Zp: e.copy_predicated(
                                out=Dm[:, qs], mask=masks8[:, lv, :], data=Zp[:]), [Zp.b, masks8.b], [Dm.bs[q]])
                            K.op("dve", lambda e, qs=qs, lv=lv, ZTp=ZTp: e.copy_predicated(
                                out=DTm[:, qs], mask=masks8[:, 7 + lv, :], data=ZTp[:]), [ZTp.b, masks8.b], [DTm.bs[q]])
                    _chk(4)
                    for q in range(4):
                        qs = slice(q * 512, (q + 1) * 512)
                        for i in range(4):
                            cs = slice((q * 4 + i) * 128, (q * 4 + i + 1) * 128)
                            K.op("pe", lambda e, q=q, i=i, cs=cs: e.matmul(
                                B[q][:, i * 128:(i + 1) * 128], lhsT=DTm[:, cs], rhs=vtok[:, cs], start=True, stop=True),
                                 [DTm.bs[q], vtok.b], [B[q].b], sig=(i == 3))
                        K.op("dve", lambda e, q=q, qs=qs, P=P, h=h: e.tensor_tensor(
                            out=v3(P["ub"][:, qs]), in0=v3(B[q][:]),
                            in1=col(1, h)[:, q * 4:(q + 1) * 4, :].to_broadcast([128, 4, 128]), op=ALU.mult),
                             [B[q].b, tk.b], P["ub"].bs[q * 4:q * 4 + 4])
                    for q in range(4):
                        qs = slice(q * 512, (q + 1) * 512)
                        for i in range(4):
                            cs = slice((q * 4 + i) * 128, (q * 4 + i + 1) * 128)
                            K.op("pe", lambda e, q=q, i=i, cs=cs: e.matmul(
                                B[4 + q][:, i * 128:(i + 1) * 128], lhsT=kg[:, cs], rhs=DTm[:, cs], start=True, stop=True),
                                 [DTm.bs[q], kg.b], [B[4 + q].b], sig=(i == 3))
                        K.op("act", lambda e, q=q, qs=qs, P=P: e.copy(out=P["wT"][:, qs], in_=B[4 + q][:]),
                             [B[4 + q].b], P["wT"].bs[q * 4:q * 4 + 4])
                    K.op("pool", lambda e, P=P: e.memset(P["S32"][:], 0.0), [], [P["S32"].b])
                    K.op("pool", lambda e, P=P: e.memset(P["Sbf"][:], 0.0), [], [P["Sbf"].b])
                _chk(5)
                for c in range(NT):
                    cs = slice(c * 128, (c + 1) * 128)
                    par = (c % 2) * 3
                    Bw, Bo, Bs = B[par], B[par + 1], B[par + 2]
                    for hi, h in enumerate(heads):
                        P = per[h]
                        ss = slice(hi * 128, (hi + 1) * 128)
                        K.op("pe", lambda e, P=P, ss=ss, cs=cs, Bw=Bw: e.matmul(Bw[:, ss], lhsT=P["wT"][:, cs], rhs=P["Sbf"][:],
                                                                               start=True, stop=True),
                             [P["wT"].bs[c], P["Sbf"].b], [Bw.b], sig=(hi == 3))
                    for hi, h in enumerate(heads):
                        P = per[h]
                        ss = slice(hi * 128, (hi + 1) * 128)
                        idx = c * 8 + h
                        vn = P["vn"][c % 2]
                        K.op("dve", lambda e, P=P, ss=ss, cs=cs, idx=idx, vn=vn, Bw=Bw: e.scalar_tensor_tensor(
                            out=vn[:], in0=Bw[:, ss], scalar=tk[:, 2, idx:idx + 1], in1=P["ub"][:, cs],
                            op0=ALU.mult, op1=ALU.add), [Bw.b, tk.b, P["ub"].bs[c]], [vn.b])
                    for hi, h in enumerate(heads):
                        P = per[h]
                        ss = slice(hi * 128, (hi + 1) * 128)
                        vn = P["vn"][c % 2]
                        K.op("pe", lambda e, P=P, ss=ss, cs=cs, Bo=Bo: e.matmul(Bo[:, ss], lhsT=P["Sbf"][:], rhs=P["qg"][:, cs],
                                                                               start=True, stop=False),
                             [P["Sbf"].b, P["qg"].bs[c]], [Bo.b], sig=False)
                        K.op("pe", lambda e, P=P, ss=ss, cs=cs, vn=vn, Bo=Bo: e.matmul(Bo[:, ss], lhsT=vn[:], rhs=P["AT"][:, cs],
                                                                                      start=False, stop=True),
                             [vn.b, P["AT"].bs[c]], [Bo.b], sig=(hi == 3))
                    for hi, h in enumerate(heads):
                        P = per[h]
                        ss = slice(hi * 128, (hi + 1) * 128)
                        vn = P["vn"][c % 2]
                        K.op("pe", lambda e, P=P, ss=ss, cs=cs, vn=vn, Bs=Bs: e.matmul(Bs[:, ss], lhsT=P["kd"][:, cs], rhs=vn[:],
                                                                                      start=True, stop=True),
                             [vn.b, P["kd"].bs[c]], [Bs.b], sig=(hi == 3))
                    for hi, h in enumerate(heads):
                        P = per[h]
                        ss = slice(hi * 128, (hi + 1) * 128)
                        idx = c * 8 + h
                        K.op("dve", lambda e, P=P, ss=ss, idx=idx, Bs=Bs: e.scalar_tensor_tensor(
                            out=P["S32"][:], in0=P["S32"][:], scalar=tk[:, 5, idx:idx + 1], in1=Bs[:, ss],
                            op0=ALU.mult, op1=ALU.add), [P["S32"].b, tk.b, Bs.b], [P["S32"].b])
                        K.op("act", lambda e, P=P: e.copy(out=P["Sbf"][:], in_=P["S32"][:]), [P["S32"].b], [P["Sbf"].b])
                    for hi, h in enumerate(heads):
                        P = per[h]
                        ss = slice(hi * 128, (hi + 1) * 128)
                        K.op("act", lambda e, P=P, ss=ss, cs=cs, Bo=Bo: e.copy(out=P["oT"][:, cs], in_=Bo[:, ss]),
                             [Bo.b], [P["oT"].bs[c]])
                _chk(6)
                for h in heads:
                    P = per[h]
                    K.dma("sp", zs[:], scr["gz"][h], [scr["gz_b"]], [zs.b], zs.b)
                    K.op("act", lambda e, P=P: e.activation(out=dU[:], in_=P["oT"][:], func=AF.Square), P["oT"].bs, [dU.b])
                    for tc in range(4):
                        ts_ = slice(tc * 512, (tc + 1) * 512)
                        Bn = B[6 + tc % 2]
                        K.op("pe", lambda e, ts_=ts_, Bn=Bn: e.matmul(Bn[:], lhsT=ones_bf[:], rhs=dU[:, ts_], start=True, stop=True),
                             [ones_bf.b, dU.b], [Bn.b])
                        K.op("act", lambda e, ts_=ts_, Bn=Bn: e.activation(
                            out=E[:, ts_], in_=Bn[:], func=AF.Ln, scale=1.0 / 128.0,
                            bias=consts["bias"][:, 3:4]), [Bn.b, consts["bias"].b], [E.b])
                    K.op("act", lambda e: e.activation(out=E[:], in_=E[:], func=AF.Exp, scale=-0.5), [E.b], [E.b])
                    K.op("dve", lambda e, P=P: e.scalar_tensor_tensor(out=LnT[:], in0=P["oT"][:], scalar=ogain[:, 0:1], in1=E[:],
                                                                      op0=ALU.mult, op1=ALU.mult),
                         P["oT"].bs + [ogain.b, E.b], LnT.bs)
                    K.op("dve", lambda e: e.tensor_tensor(out=Dm[:], in0=LnT[:], in1=zs[:], op=ALU.mult),
                         LnT.bs + [zs.b], Dm.bs)
                    K.dma("sp", scr["mix"][h], Dm[:], Dm.bs, [scr["mix_b"]], Dm.bs[0])
                K.barrier()


def phase_sb(C, l, ins, consts, scr):
    K, nc = C.K, C.nc
    ones_bf, ident = consts["ones_bf"], consts["ident"]
    with ExitStack() as es:
        qT = [C.sb(es, f"sqT{i}", [128, S], BF16) for i in range(2)]
        kT = [C.sb(es, f"skT{i}", [128, S], BF16) for i in range(2)]
        nkT = [C.sb(es, f"snkT{i}", [128, S], BF16) for i in range(2)]
        vt = [C.sb(es, f"svt{i}", [128, NT, 128], BF16) for i in range(2)]
        e1 = [C.sb(es, f"e1_{i}", [128, 512], F32) for i in range(2)]
        sp = [C.sb(es, f"sp{i}", [128, 512], BF16) for i in range(2)]
        sps = [C.sb(es, f"sps{i}", [128, 512], BF16) for i in range(2)]
        ww = [C.sb(es, f"ww{i}", [128, 512], BF16) for i in range(2)]
        yraw = [C.sb(es, f"yraw{i}", [128, S], F32) for i in range(2)]
        ysq = C.sb(es, "ysq", [128, S], BF16)
        rn = C.sb(es, "rn", [128, S], F32)
        ymx = [C.sb(es, f"ymx{i}", [128, S], BF16) for i in range(2)]
        mneg = C.sb(es, "mnegs", [128, 4, 512], BF16)
        mpos = C.sb(es, "mposs", [128, 4, 512], BF16)
        uin = C.sb(es, "uin", [128, 128], BF16)
        og = C.sb(es, "og", [128, 1], F32)
        Bz = [C.ps(es, f"Bz{i}", [128, 512], F32) for i in range(4)]
        Bn = [C.ps(es, f"Bn{i}", [128, 512], F32) for i in range(2)]
        nuin = C.sb(es, "nuin", [128, 128], BF16)
        nones = C.sb(es, "nones", [128, 128], BF16)
        Bo = [C.ps(es, f"Bo{i}", [128, 512], F32) for i in range(2)]
        onesw = C.sb(es, "onesw", [128, 512], F32)
        zerw = C.sb(es, "zerw", [128, 512], F32)
        K.op("pool", lambda e: e.memset(onesw[:], 1.0), [], [onesw.b])
        K.op("pool", lambda e: e.memset(zerw[:], 0.0), [], [zerw.b])
        for k in range(4):
            K.op("pool", lambda e, k=k: e.affine_select(out=mneg[:, k, :], in_=zerw[:], pattern=[[1, 512]],
                                                        compare_op=ALU.is_gt, fill=-30000.0, base=-128 * k, channel_multiplier=-1),
                 [zerw.b], [mneg.b])
            K.op("pool", lambda e, k=k: e.affine_select(out=mpos[:, k, :], in_=zerw[:], pattern=[[1, 512]],
                                                        compare_op=ALU.is_gt, fill=30000.0, base=-128 * k, channel_multiplier=-1),
                 [zerw.b], [mpos.b])
        K.op("pool", lambda e: e.affine_select(out=uin[:], in_=onesw[:, 0:128], pattern=[[-1, 128]], compare_op=ALU.is_ge,
                                               fill=0.0, base=0, channel_multiplier=1), [onesw.b], [uin.b])
        K.dma("sp", og[:], ins["sb_o_norm"][l, :].rearrange("(p o) -> p o", o=1), [], [og.b], og.b)
        K.op("pool", lambda e: e.tensor_scalar(out=nuin[:], in0=uin[:], scalar1=-1.0, scalar2=None, op0=ALU.mult), [uin.b], [nuin.b])
        K.op("pool", lambda e: e.memset(nones[:], -1.0), [], [nones.b])

        blocks = []
        for h in range(H):
            for tc in range(4):
                nb = 4 * tc + 4
                for b in range(nb - 1, -1, -1):
                    blocks.append((h, tc, b, b == nb - 1, b == 0))

        def load(h):
            i = h % 2
            K.dma("sp", qT[i][:], scr["sq"][h], [scr["sq_b"]], [qT[i].b], qT[i].b)
            K.dma("sp", kT[i][:], scr["sk"][h], [scr["sk_b"]], [kT[i].b], kT[i].b)
            K.dma("sp", vt[i][:], scr["sv"][:, :, h * 128:(h + 1) * 128].rearrange("t p d -> p t d"),
                  [scr["sv_b"]], [vt[i].b], vt[i].b)
            K.op("pool", lambda e: e.tensor_scalar(out=nkT[i][:], in0=kT[i][:], scalar1=-1.0, scalar2=None, op0=ALU.mult),
                 [kT[i].b], [nkT[i].b])

        def info(n):
            h, tc, b, first, last = blocks[n]
            return h, tc, b, first, last, h % 2, n % 4, slice(tc * 512, (tc + 1) * 512), b - 4 * tc

        def a1(n):
            h, tc, b, first, last, i, j, ts_, k = info(n)
            K.op("pe", lambda e: e.matmul(Bz[j][:], lhsT=kT[i][:, b * 128:(b + 1) * 128], rhs=qT[i][:, ts_],
                                          start=True, stop=(k < 0)), [kT[i].b, qT[i].b], [Bz[j].b], sig=(k < 0))
            if k >= 0:
                K.op("pe", lambda e: e.matmul(Bz[j][:], lhsT=ident[:], rhs=mneg[:, k, :], start=False, stop=True),
                     [ident.b, mneg.b], [Bz[j].b])

        def a2(n):
            h, tc, b, first, last, i, j, ts_, k = info(n)
            K.op("act", lambda e: e.activation(out=e1[n % 2][:], in_=Bz[j][:], func=AF.Exp), [Bz[j].b], [e1[n % 2].b])
            K.op("act", lambda e: e.activation(out=sp[n % 2][:], in_=e1[n % 2][:], func=AF.Ln, bias=consts["bias"][:, 4:5]),
                 [e1[n % 2].b, consts["bias"].b], [sp[n % 2].b])

        def b1(n):
            h, tc, b, first, last, i, j, ts_, k = info(n)
            sp_ = sp[n % 2]
            K.op("pe", lambda e: e.matmul(Bz[j][:], lhsT=nuin[:], rhs=sp_[:], start=False, stop=first, skip_group_check=True),
                 [nuin.b, sp_.b], [Bz[j].b], sig=first)
            if not first:
                K.op("pe", lambda e: e.matmul(Bz[j][:], lhsT=nones[:], rhs=sps[(n - 1) % 2][:], start=False, stop=True,
                                              skip_group_check=True),
                     [nones.b, sps[(n - 1) % 2].b], [Bz[j].b])
            if not last:
                if first:
                    K.op("dve", lambda e: e.tensor_copy(out=sps[n % 2][:], in_=sp_[:]), [sp_.b], [sps[n % 2].b])
                else:
                    K.op("dve", lambda e: e.tensor_tensor(out=sps[n % 2][:], in0=sps[(n - 1) % 2][:], in1=sp_[:], op=ALU.add),
                         [sps[(n - 1) % 2].b, sp_.b], [sps[n % 2].b])

        def b2(n):
            h, tc, b, first, last, i, j, ts_, k = info(n)
            o_ = Bo[tc % 2]
            K.op("act", lambda e: e.activation(out=ww[n % 2][:], in_=Bz[j][:], func=AF.Exp), [Bz[j].b], [ww[n % 2].b])
            K.op("pe", lambda e: e.matmul(o_[:], lhsT=vt[i][:, b, :], rhs=ww[n % 2][:], start=first, stop=last),
                 [vt[i].b, ww[n % 2].b], [o_.b], sig=last)
            if last:
                y_ = yraw[h % 2]
                K.op("act", lambda e: e.copy(out=y_[:, ts_], in_=o_[:]), [o_.b], [y_.b])
                if tc == 3:
                    finish(h)

        def finish(h):
            y_, m_ = yraw[h % 2], ymx[h % 2]
            K.op("act", lambda e: e.activation(out=ysq[:], in_=y_[:], func=AF.Square), [y_.b], [ysq.b])
            for tc in range(4):
                ts_ = slice(tc * 512, (tc + 1) * 512)
                p_ = Bn[tc % 2]
                K.op("pe", lambda e, p_=p_, ts_=ts_: e.matmul(p_[:], lhsT=ones_bf[:], rhs=ysq[:, ts_], start=True, stop=True),
                     [ones_bf.b, ysq.b], [p_.b])
                K.op("act", lambda e, p_=p_, ts_=ts_: e.activation(out=rn[:, ts_], in_=p_[:], func=AF.Ln, scale=1.0 / 128.0,
                                                                   bias=consts["bias"][:, 3:4]), [p_.b, consts["bias"].b], [rn.b])
            K.op("act", lambda e: e.activation(out=rn[:], in_=rn[:], func=AF.Exp, scale=-0.5), [rn.b], [rn.b])
            K.op("dve", lambda e: e.scalar_tensor_tensor(out=m_[:], in0=y_[:], scalar=og[:, 0:1], in1=rn[:],
                                                         op0=ALU.mult, op1=ALU.mult), [y_.b, og.b, rn.b], [m_.b])
            K.dma("sp", scr["mix"][8 + h], m_[:], [m_.b], [scr["mix_b"]], m_.b)

        NB = len(blocks)
        load(0)
        load(1)
        a1(0)
        a1(1)
        for n in range(NB):
            h, tc, b, first, last = blocks[n]
            if n + 2 < NB:
                h2 = blocks[n + 2][0]
                if h2 != blocks[n + 1][0] and h2 + 1 < H:
                    pass
                a1(n + 2)
            a2(n)
            b1(n)
            if n > 0:
                b2(n - 1)
                if blocks[n - 1][4] and blocks[n - 1][1] == 3 and blocks[n - 1][0] + 2 < H:
                    load(blocks[n - 1][0] + 2)
        b2(NB - 1)
        K.barrier()


def phase_outproj(C, l, x_in, x_in_buf, x_out, x_out_buf, ins, consts, scr):
    K, nc = C.K, C.nc
    with ExitStack() as es:
        mx = C.sb(es, "mx", [128, KC, S], BF16)
        wo = C.sb(es, "wo", [128, KC, D], BF16, nbufs=4)
        xt = [C.sb(es, f"xo_in{i}", [128, D], F32) for i in range(2)]
        xo = [C.sb(es, f"xo_out{i}", [128, D], F32) for i in range(2)]
        pp = [C.ps(es, f"po{i}", [128, 512], F32) for i in range(8)]
        for kc in range(KC):
            K.dma("sp", mx[:, kc, :], scr["mix"][kc], [scr["mix_b"]], [mx.b], mx.b, grp=True)
        alloc_stage(C, es)
        for cq in range(4):
            load_w(C, wo, ins["w_out"][l], cq * 512, 512, cq * 512, wbuf=wo.bs[cq])
        n = 0
        for t in range(NT):
            x_, o_ = xt[t % 2], xo[t % 2]
            K.dma("sp", x_[:], x_in[t * 128:(t + 1) * 128, :], [x_in_buf], [x_.b], x_.b)
            for cq in range(4):
                p_ = pp[n % 8]
                n += 1
                for kc in range(KC):
                    K.op("pe", lambda e, p_=p_, kc=kc, t=t, cq=cq: e.matmul(
                        p_[:], lhsT=mx[:, kc, t * 128:(t + 1) * 128], rhs=wo[:, kc, cq * 512:(cq + 1) * 512],
                        start=(kc == 0), stop=(kc == KC - 1)), [mx.b, wo.bs[cq]], [p_.b], sig=(kc == KC - 1))
                K.op("dve", lambda e, p_=p_, x_=x_, o_=o_, cq=cq: e.tensor_tensor(
                    out=o_[:, cq * 512:(cq + 1) * 512], in0=p_[:], in1=x_[:, cq * 512:(cq + 1) * 512], op=ALU.add),
                     [p_.b, x_.b], [o_.b])
            K.dma("sp", x_out[t * 128:(t + 1) * 128, :], o_[:], [o_.b], [x_out_buf], o_.b)
        K.barrier()


def make_consts(C, es):
    K, nc = C.K, C.nc
    c = {}
    c["neghalf"] = C.sb(es, "neghalf", [128, 1], F32)
    K.op("pool", lambda e: e.memset(c["neghalf"][:], -0.5), [], [c["neghalf"].b])
    onesf = C.sb(es, "onesf", [128, 128], F32)
    K.op("pool", lambda e: e.memset(onesf[:], 1.0), [], [onesf.b])
    identf = C.sb(es, "identf", [128, 128], F32)
    K.op("pool", lambda e: e.affine_select(out=identf[:], in_=onesf[:], pattern=[[1, 128]],
                                           compare_op=ALU.is_equal, fill=0.0, base=0, channel_multiplier=-1),
         [onesf.b], [identf.b])
    c["ident"] = C.sb(es, "ident", [128, 128], BF16)
    K.op("pool", lambda e: e.tensor_copy(out=c["ident"][:], in_=identf[:]), [identf.b], [c["ident"].b])
    c["onesf"] = onesf
    c["identf"] = identf
    c["ones_bf"] = C.sb(es, "ones_bf", [128, 128], BF16)
    K.op("pool", lambda e: e.memset(c["ones_bf"][:], 1.0), [], [c["ones_bf"].b])
    c["bias"] = C.sb(es, "biasc", [128, 8], F32)
    for i, v in enumerate([128.0 * L2_EPS, L2_EPS, 128.0 * RMS_EPS, RMS_EPS, 1.0]):
        K.op("pool", lambda e, i=i, v=v: e.memset(c["bias"][:, i:i + 1], v), [], [c["bias"].b])
    return c


PARAMS = [("attn_norm", [DEPTH, D]), ("w_in", [DEPTH, D, IN_COLS]), ("gdn_conv", [DEPTH, 4, 3072]),
          ("gdn_a_log", [DEPTH, H]), ("gdn_dt_bias", [DEPTH, H]), ("gdn_o_norm", [DEPTH, HD]),
          ("sb_q_norm", [DEPTH, HD]), ("sb_k_norm", [DEPTH, HD]), ("sb_o_norm", [DEPTH, HD]),
          ("w_out", [DEPTH, D, D]), ("ffn_norm", [DEPTH, D]), ("w_up", [DEPTH, D, 2 * D_FF]),
          ("ffn_conv", [DEPTH, 3, 2 * D_FF]), ("ffn_conv_bias", [DEPTH, 2 * D_FF]),
          ("w_down", [DEPTH, D_FF, D])]


def build(mode="full"):
    nc = bass.Bass("TRN2", target_bir_lowering=False)
    only = mode.startswith("only")
    small = {"attn_norm", "gdn_conv", "gdn_a_log", "gdn_dt_bias", "gdn_o_norm", "sb_q_norm", "sb_k_norm", "sb_o_norm",
             "ffn_norm", "ffn_conv", "ffn_conv_bias"}
    ins = {"x": nc.dram_tensor("x", [S, D], F32, kind="ExternalInput").ap()}
    for name, shape in PARAMS:
        if only and name not in small:
            continue
        ins[name] = nc.dram_tensor(name, shape, F32, kind="ExternalInput").ap()
    y = nc.dram_tensor("y", [S, D], F32, kind="ExternalOutput").ap()
    dbg = "ExternalOutput" if mode.startswith("dbg") else "Internal"
    sin = "ExternalInput" if only else dbg
    scratch = {}
    for nm in ("gq", "gk", "gv", "gz", "sq", "sk"):
        scratch[nm] = nc.dram_tensor(nm + "_scr", [H, 128, S], BF16, kind=sin).ap()
        scratch[nm + "_b"] = Buf(nm + "_scr")
    scratch["sv"] = nc.dram_tensor("sv_scr", [NT, 128, 1024], BF16, kind=sin).ap()
    scratch["sv_b"] = Buf("sv_scr")
    for nm in ("beta", "gcum"):
        scratch[nm] = nc.dram_tensor(nm + "_scr", [H, S], F32, kind=sin).ap()
        scratch[nm + "_b"] = Buf(nm + "_scr")
    scratch["mix"] = nc.dram_tensor("mix_scr", [16, 128, S], BF16, kind=("ExternalOutput" if only else dbg)).ap()
    scratch["mix_b"] = Buf("mix_scr")
    scratch.update({
        "a": nc.dram_tensor("a_scr", [NT, 128, NFF, 128], BF16).ap(),
        "a_buf": Buf("a_scr"),
        "x1": nc.dram_tensor("x1_scr", [S, D], F32).ap(),
        "x2": nc.dram_tensor("x2_scr", [S, D], F32).ap(),
    })
    K = Sched(nc)
    with ExitStack() as es:
        C = Ctx(nc, K, es)
        consts = make_consts(C, es)
        xb, yb = Buf("xin"), Buf("yout")
        if mode == "ffn0":
            phase_ffn(C, 0, ins["x"], xb, y, yb, ins, consts, scratch)
        if mode in ("dbg_inproj", "dbg_gdn"):
            phase_inproj(C, 0, ins["x"], xb, ins, consts, scratch)
        if mode in ("dbg_gdn", "only_gdn"):
            phase_gdn(C, 0, ins, consts, scratch)
        if mode == "only_sb":
            phase_sb(C, 0, ins, consts, scratch)
        if mode == "full":
            x1b, x2b = Buf("x1s"), Buf("x2s")
            cur, curb = ins["x"], xb
            for l in range(DEPTH):
                K.scope = f"inproj{l}"
                phase_inproj(C, l, cur, curb, ins, consts, scratch)
                K.scope = f"gdn{l}"
                phase_gdn(C, l, ins, consts, scratch)
                K.scope = f"sb{l}"
                phase_sb(C, l, ins, consts, scratch)
                K.scope = f"outproj{l}"
                phase_outproj(C, l, cur, curb, scratch["x1"], x1b, ins, consts, scratch)
                K.scope = f"ffn{l}"
                if l == DEPTH - 1:
                    phase_ffn(C, l, scratch["x1"], x1b, y, yb, ins, consts, scratch)
                else:
                    phase_ffn(C, l, scratch["x1"], x1b, scratch["x2"], x2b, ins, consts, scratch)
                    cur, curb = scratch["x2"], x2b
        K.barrier()
        with nc.Block() as block:
            @block.tensor
            def _(e):
                K.emit("pe", e)

            @block.scalar
            def _(e):
                K.emit("act", e)

            @block.vector
            def _(e):
                K.emit("dve", e)

            @block.gpsimd
            def _(e):
                K.emit("pool", e)

            @block.sync
            def _(e):
                K.emit("sp", e)
    return nc, K


def kernel(**inputs):
    nc, K = build("full")
    x = np.ascontiguousarray(inputs["x"], dtype=np.float32)
    in_maps = []
    for c in range(8):
        m = {"x": x[c]}
        for name, _ in PARAMS:
            m[name] = np.ascontiguousarray(inputs[name], dtype=np.float32)
        in_maps.append(m)
    res = run_bass_kernel_spmd(nc, in_maps, core_ids=list(range(8)))
    return np.stack([r["y"] for r in res.results], axis=0)
```

```python
import numpy as np
from contextlib import ExitStack
import concourse.bass as bass
import concourse.mybir as mybir
from concourse.bass_utils import run_bass_kernel_spmd

F32 = mybir.dt.float32
BF16 = mybir.dt.bfloat16
AF = mybir.ActivationFunctionType
ALU = mybir.AluOpType
AX = mybir.AxisListType

D = 2048
S = 2048
DEPTH = 2
NT = S // 128
KC = D // 128
H = 8
HD = 128
IN_COLS = 7184
D_FF = 5632
NFF = D_FF // 128
RMS_EPS = 1e-6
L2_EPS = 1e-6

ENGS = ["pe", "act", "dve", "pool", "sp"]
SCOPES = False


class Buf:
    __slots__ = ("name", "w", "r", "dsem", "dcount", "dkind")

    def __init__(self, name):
        self.name = name
        self.w = None
        self.r = {}
        self.dsem = None
        self.dcount = 0
        self.dkind = None


class Sched:
    def __init__(self, nc):
        self.nc = nc
        self.sem = {e: nc.alloc_semaphore("cnt_" + e) for e in ENGS}
        self.count = {e: 0 for e in ENGS}
        self.known = {e: {} for e in ENGS}
        self.ops = {e: [] for e in ENGS}
        self.dma_bufs = []
        self.nwaits = 0
        self.muted = False
        self.free_sems = {}
        self.nsem = 0
        self.scope = "init"

    def _deps(self, eng, reads, writes, skip_sem=None):
        waits = {}
        known = self.known[eng]
        own = self.sem[eng]

        def need(sem, val):
            if eng == "pe" and sem is own:
                return
            if known.get(sem, 0) >= val:
                return
            if waits.get(sem, 0) < val:
                waits[sem] = val

        for b in reads:
            if b.w is not None:
                need(*b.w)
        for b in writes:
            if b.w is not None and b.w[0] is not skip_sem:
                need(*b.w)
            for s, v in b.r.items():
                need(s, v)
        for s, v in waits.items():
            known[s] = v
        self.nwaits += len(waits)
        return list(waits.items())

    def _commit(self, ev, reads, writes):
        for b in reads:
            if b.r.get(ev[0], 0) < ev[1]:
                b.r[ev[0]] = ev[1]
        for b in writes:
            b.w = ev
            b.r = {}

    def op(self, eng, fn, reads=(), writes=(), sig=True):
        if self.muted:
            return
        waits = self._deps(eng, reads, writes)
        if sig:
            self.count[eng] += 1
            ev = (self.sem[eng], self.count[eng])
            inc = (self.sem[eng], 1)
        else:
            assert eng == "pe"
            ev = (self.sem[eng], self.count[eng] + 1)
            inc = None
        self.ops[eng].append((waits, fn, inc, self.scope))
        self._commit(ev, reads, writes)

    def dma(self, q, out, in_, reads, writes, owner, grp=False):
        if self.muted:
            return
        kind = "sw" if q == "pool" else "hw"
        if owner.dsem is None:
            fl = self.free_sems.setdefault(kind, [])
            owner.dkind = kind
            if fl:
                owner.dsem, owner.dcount = fl.pop()
            else:
                owner.dsem = self.nc.alloc_semaphore(f"dsem{self.nsem}")
                owner.dcount = 0
                self.nsem += 1
            self.dma_bufs.append(owner)
        waits = self._deps(q, reads, writes, owner.dsem if grp else None)
        owner.dcount += 16
        ev = (owner.dsem, owner.dcount)
        self.ops[q].append((waits, (lambda e, o=out, i=in_: e.dma_start(out=o, in_=i)), (owner.dsem, 16), self.scope))
        self._commit(ev, reads, writes)

    def barrier(self):
        if self.muted:
            return
        evs = [(self.sem[e], self.count[e]) for e in ENGS if self.count[e] > 0]
        evs += [(b.dsem, b.dcount) for b in self.dma_bufs]
        for e in ENGS:
            waits = []
            for s, v in evs:
                if e == "pe" and s is self.sem["pe"]:
                    continue
                if s is self.sem[e]:
                    continue
                if self.known[e].get(s, 0) < v:
                    self.known[e][s] = v
                    waits.append((s, v))
            if waits:
                self.ops[e].append((waits, None, None, self.scope))
        for b in self.dma_bufs:
            self.free_sems.setdefault(b.dkind, []).append((b.dsem, b.dcount))
            b.dsem = None
        self.dma_bufs = []

    def emit(self, ename, eng):
        cur, cm = None, None
        for waits, fn, inc, scope in self.ops[ename]:
            if SCOPES and scope != cur:
                if cm is not None:
                    cm.__exit__(None, None, None)
                cm = self.nc.named_scope(scope)
                cm.__enter__()
                cur = scope
            for s, v in waits:
                eng.wait_ge(s, v)
            if fn is None:
                continue
            ins = fn(eng)
            if inc is not None:
                ins.then_inc(inc[0], inc[1])
        if cm is not None:
            cm.__exit__(None, None, None)


class T:
    def __init__(self, h, name, nbufs=1):
        self.h = h
        self.b = Buf(name)
        self.bs = [Buf(f"{name}_{i}") for i in range(nbufs)] if nbufs > 1 else [self.b]

    def __getitem__(self, k):
        return self.h[k]


class Ctx:
    def __init__(self, nc, K, es):
        self.nc, self.K, self.es = nc, K, es
        self.uid = 0

    def sb(self, es, name, shape, dt, nbufs=1):
        self.uid += 1
        nm = f"{name}_{self.uid}"
        return T(es.enter_context(self.nc.sbuf_tensor(nm, list(shape), dt)), nm, nbufs)

    def ps(self, es, name, shape, dt=F32, nbufs=1):
        self.uid += 1
        nm = f"{name}_{self.uid}"
        return T(es.enter_context(self.nc.psum_tensor(nm, list(shape), dt)), nm, nbufs)


def phase_norm(C, x_ap, x_buf, gain_row_ap, hT, consts):
    K, nc = C.K, C.nc
    with ExitStack() as es:
        gbc = C.sb(es, "gbc", [128, D], F32)
        xt = [C.sb(es, f"xt{i}", [128, D], F32) for i in range(2)]
        hs = [C.sb(es, f"hs{i}", [128, D], BF16) for i in range(2)]
        junk = C.sb(es, "junk", [128, D], BF16)
        st = [C.sb(es, f"st{i}", [128, 4], F32) for i in range(2)]
        pt = [C.ps(es, f"pt{i}", [128, D], BF16) for i in range(2)]
        K.dma("sp", gbc[:], gain_row_ap.partition_broadcast(128), [], [gbc.b], gbc.b)
        for t in range(NT):
            i = t % 2
            x_, h_, s_, p_ = xt[i], hs[i], st[i], pt[i]
            K.dma("sp", x_[:], x_ap[t * 128:(t + 1) * 128, :], [x_buf], [x_.b], x_.b)
            K.op("act", lambda e, x_=x_, s_=s_: e.activation(out=junk[:], in_=x_[:], func=AF.Square,
                                                           accum_out=s_[:, 0:1]),
                 [x_.b], [s_.b])
            K.op("dve", lambda e, s_=s_: e.tensor_scalar(out=s_[:, 1:2], in0=s_[:, 0:1], scalar1=1.0 / D,
                                                       scalar2=RMS_EPS, op0=ALU.mult, op1=ALU.add),
                 [s_.b], [s_.b])
            K.op("pool", lambda e, s_=s_: e.tensor_tensor(out=s_[:, 2:3], in0=s_[:, 1:2],
                                                        in1=consts["neghalf"][:, 0:1], op=ALU.pow),
                 [s_.b, consts["neghalf"].b], [s_.b])
            K.op("dve", lambda e, x_=x_, h_=h_, s_=s_: e.scalar_tensor_tensor(
                out=h_[:], in0=x_[:], scalar=s_[:, 2:3], in1=gbc[:], op0=ALU.mult, op1=ALU.mult),
                 [x_.b, s_.b, gbc.b], [h_.b])
            for kc in range(KC):
                K.op("pe", lambda e, h_=h_, p_=p_, kc=kc: e.transpose(
                    out=p_[:, kc * 128:(kc + 1) * 128], in_=h_[:, kc * 128:(kc + 1) * 128],
                    identity=consts["ident"][:]),
                     [h_.b, consts["ident"].b], [p_.b], sig=(kc == KC - 1))
            K.op("act", lambda e, p_=p_, t=t: e.copy(
                out=hT[:, :, t * 128:(t + 1) * 128],
                in_=p_[:].rearrange("p (k n) -> p k n", n=128)),
                 [p_.b], [hT.bs[t // 4]])
        K.barrier()


def alloc_stage(C, es, n=3):
    C.stg = [C.sb(es, f"stg{i}", [128, 4, 512], F32) for i in range(n)]
    C.stg_i = 0


def load_w(C, slot, w_ap, c0, ncols, col_off=0, nk=KC, wbuf=None, q="sp"):
    K = C.K
    wbuf = wbuf if wbuf is not None else slot.b
    step = 4
    for k0 in range(0, nk, step):
        kk = min(step, nk - k0)
        st = C.stg[C.stg_i % len(C.stg)]
        C.stg_i += 1
        src = w_ap[k0 * 128:(k0 + kk) * 128, c0:c0 + ncols].rearrange("(kc p) n -> p kc n", p=128)
        K.dma(q, st[:, 0:kk, 0:ncols], src, [], [st.b], st.b)
        K.op("dve", lambda e, st=st, k0=k0, kk=kk: e.tensor_copy(
            out=slot[:, k0:k0 + kk, col_off:col_off + ncols], in_=st[:, 0:kk, 0:ncols]), [st.b], [wbuf])


def phase_ffn(C, l, x_in, x_in_buf, x_out, x_out_buf, ins, consts, scratch):
    K, nc = C.K, C.nc
    a_scr, a_buf = scratch["a"], scratch["a_buf"]
    with ExitStack() as es:
        hT = C.sb(es, "hT", [128, KC, S], BF16, nbufs=4)
        wsl = [C.sb(es, f"wup{i}", [128, KC, 512], BF16) for i in range(3)]
        alloc_stage(C, es, 2)
        for g in range(2):
            load_w(C, wsl[g], ins["w_up"][l], g * 256, 256, 0)
            load_w(C, wsl[g], ins["w_up"][l], D_FF + g * 256, 256, 256)
        phase_norm(C, x_in, x_in_buf, ins["ffn_norm"][l:l + 1, :], hT, consts)
        with ExitStack() as es2:
            cw = C.sb(es2, "cw", [128, 4, 2 * NFF], F32)
            raw = [C.sb(es2, f"raw{i}", [128, 2 + S], F32) for i in range(4)]
            acc = [C.sb(es2, f"acc{i}", [128, S], F32) for i in range(2)]
            sg = C.sb(es2, "sg", [128, S], F32)
            aT = [C.sb(es2, f"aT{i}", [128, NT, 2, 128], BF16) for i in range(2)]
            pp = [C.ps(es2, f"pp{i}", [128, 512], F32) for i in range(8)]
            cwr = C.sb(es2, "cwr", [2 * NFF, 4, 128], F32)
            for tap in range(3):
                K.dma("sp", cwr[:, tap, :], ins["ffn_conv"][l, tap, :].rearrange("(j p) -> j p", p=128),
                      [], [cwr.b], cwr.b)
            K.dma("sp", cwr[:, 3, :], ins["ffn_conv_bias"][l, :].rearrange("(j p) -> j p", p=128),
                  [], [cwr.b], cwr.b)
            for tap in range(4):
                K.op("pe", lambda e, tap=tap: e.transpose(out=pp[0][:, tap * 128:tap * 128 + 2 * NFF],
                                                          in_=cwr[:, tap, :], identity=consts["identf"][0:2 * NFF, 0:2 * NFF]),
                     [cwr.b, consts["identf"].b], [pp[0].b], sig=(tap == 3))
            K.op("act", lambda e: e.copy(out=cw[:], in_=pp[0][:].rearrange("p (t n) -> p t n", n=128)[:, :, 0:2 * NFF]),
                 [pp[0].b], [cw.b])
            for r in raw:
                K.op("pool", lambda e, r=r: e.memset(r[:, 0:2], 0.0), [], [r.b])
            ngrp = NFF // 2
            pidx = 0
            for g in range(ngrp):
                w_ = wsl[g % 3]
                if g + 2 < ngrp:
                    load_w(C, wsl[(g + 2) % 3], ins["w_up"][l], (g + 2) * 256, 256, 0)
                    load_w(C, wsl[(g + 2) % 3], ins["w_up"][l], D_FF + (g + 2) * 256, 256, 256)
                for jj in range(2):
                    j = g * 2 + jj
                    rg, ru = raw[(j % 2) * 2], raw[(j % 2) * 2 + 1]
                    for which, r_ in ((0, rg), (1, ru)):
                        co = which * 256 + jj * 128
                        for tc in range(4):
                            p_ = pp[pidx % 8]
                            pidx += 1
                            for kc in range(KC):
                                K.op("pe", lambda e, p_=p_, w_=w_, kc=kc, co=co, tc=tc: e.matmul(
                                    p_[:], lhsT=w_[:, kc, co:co + 128], rhs=hT[:, kc, tc * 512:(tc + 1) * 512],
                                    start=(kc == 0), stop=(kc == KC - 1)),
                                     [w_.b, hT.bs[tc]], [p_.b], sig=(kc == KC - 1))
                            K.op("act", lambda e, p_=p_, r_=r_, tc=tc: e.copy(
                                out=r_[:, 2 + tc * 512:2 + (tc + 1) * 512], in_=p_[:]), [p_.b], [r_.b])
                    a_ = aT[(j // 2) % 2]
                    for which, r_, ac in ((0, rg, acc[0]), (1, ru, acc[1])):
                        ch = which * NFF + j
                        K.op("act", lambda e, r_=r_, ac=ac, ch=ch: e.activation(
                            out=ac[:], in_=r_[:, 2:2 + S], func=AF.Identity,
                            scale=cw[:, 2, ch:ch + 1], bias=cw[:, 3, ch:ch + 1]), [r_.b, cw.b], [ac.b])
                        K.op("dve", lambda e, r_=r_, ac=ac, ch=ch: e.scalar_tensor_tensor(
                            out=ac[:], in0=r_[:, 1:1 + S], scalar=cw[:, 1, ch:ch + 1], in1=ac[:],
                            op0=ALU.mult, op1=ALU.add), [r_.b, cw.b, ac.b], [ac.b])
                        K.op("dve", lambda e, r_=r_, ac=ac, ch=ch: e.scalar_tensor_tensor(
                            out=ac[:], in0=r_[:, 0:S], scalar=cw[:, 0, ch:ch + 1], in1=ac[:],
                            op0=ALU.mult, op1=ALU.add), [r_.b, cw.b, ac.b], [ac.b])
                    K.op("act", lambda e: e.activation(out=sg[:], in_=acc[0][:], func=AF.Silu),
                         [acc[0].b], [sg.b])
                    K.op("dve", lambda e, a_=a_, j=j: e.tensor_tensor(out=a_[:, :, j % 2, :], in0=sg[:].rearrange('p (t n) -> p t n', n=128), in1=acc[1][:].rearrange('p (t n) -> p t n', n=128), op=ALU.mult),
                         [sg.b, acc[1].b], [a_.b])
                    if j % 2 == 1:
                        K.dma("sp", a_scr[:, :, j - 1:j + 1, :].rearrange("t p j n -> p t (j n)"),
                              a_[:].rearrange("p t j n -> p t (j n)"), [a_.b], [a_buf], a_.b)
        K.barrier()
    with ExitStack() as es:
        wd = [C.sb(es, f"wd{i}", [128, NFF, 512], BF16) for i in range(2)]
        alloc_stage(C, es)
        at = [C.sb(es, f"at{i}", [128, NFF, 128], BF16) for i in range(3)]
        xr = [C.sb(es, f"xr{i}", [128, 512], F32) for i in range(3)]
        xo = [C.sb(es, f"xo{i}", [128, 512], F32) for i in range(3)]
        pp = [C.ps(es, f"pd{i}", [128, 512], F32) for i in range(4)]
        n = 0
        load_w(C, wd[0], ins["w_down"][l], 0, 512, 0, nk=NFF, q="act")
        for q in range(4):
            w_ = wd[q % 2]
            if q + 1 < 4:
                load_w(C, wd[(q + 1) % 2], ins["w_down"][l], (q + 1) * 512, 512, 0, nk=NFF, q="act")
            for t in range(NT):
                a_, xr_, xo_, p_ = at[n % 3], xr[n % 3], xo[n % 3], pp[n % 4]
                n += 1
                K.dma("sp", a_[:], a_scr[t], [a_buf], [a_.b], a_.b)
                K.dma("pool", xr_[:], x_in[t * 128:(t + 1) * 128, q * 512:(q + 1) * 512], [x_in_buf], [xr_.b], xr_.b)
                for j in range(NFF):
                    K.op("pe", lambda e, p_=p_, a_=a_, w_=w_, j=j: e.matmul(
                        p_[:], lhsT=a_[:, j, :], rhs=w_[:, j, :], start=(j == 0), stop=(j == NFF - 1)),
                         [a_.b, w_.b], [p_.b], sig=(j == NFF - 1))
                K.op("dve", lambda e, p_=p_, xr_=xr_, xo_=xo_: e.tensor_tensor(
                    out=xo_[:], in0=p_[:], in1=xr_[:], op=ALU.add), [p_.b, xr_.b], [xo_.b])
                K.dma("pool", x_out[t * 128:(t + 1) * 128, q * 512:(q + 1) * 512], xo_[:], [xo_.b], [x_out_buf], xo_.b)
        K.barrier()


def phase_inproj(C, l, x_in, x_in_buf, ins, consts, scr):
    K, nc = C.K, C.nc
    W = ins["w_in"][l]
    with ExitStack() as es:
        hT = C.sb(es, "hT", [128, KC, S], BF16, nbufs=4)
        wsl = [C.sb(es, f"win{i}", [128, KC, 512], BF16) for i in range(3)]
        wba = C.sb(es, "wba", [128, KC, 16], BF16)
        alloc_stage(C, es, 2)
        load_w(C, wsl[0], W, 0, 512, 0)
        load_w(C, wsl[1], W, 512, 512, 0)
        phase_norm(C, x_in, x_in_buf, ins["attn_norm"][l:l + 1, :], hT, consts)
        raw = [C.sb(es, f"raw{i}", [128, 3 + S], F32) for i in range(2)]
        acc = [C.sb(es, f"acc{i}", [128, S], F32) for i in range(2)]
        ysq = C.sb(es, "ysq", [128, S], BF16)
        lnv = C.sb(es, "lnv", [128, S], F32)
        ob = [C.sb(es, f"ob{i}", [128, S], BF16) for i in range(2)]
        svt = [C.sb(es, f"svt{i}", [128, 512], BF16) for i in range(2)]
        gcw = C.sb(es, "gcw", [128, 4, 24], F32)
        gcr = C.sb(es, "gcr", [24, 4, 128], F32)
        nrm = C.sb(es, "nrm", [128, 2], F32)
        hp = C.sb(es, "hp", [8, 4], F32)
        pm = [C.ps(es, f"pm{i}", [128, 512], F32) for i in range(6)]
        po = [C.ps(es, f"po{i}", [128, 512], F32) for i in range(2)]
        ones_bf = consts["ones_bf"]
        for tap in range(4):
            K.dma("sp", gcr[:, tap, :], ins["gdn_conv"][l, tap, :].rearrange("(j p) -> j p", p=128), [], [gcr.b], gcr.b)
        for tap in range(4):
            K.op("pe", lambda e, tap=tap: e.transpose(out=pm[0][:, tap * 32:tap * 32 + 24], in_=gcr[:, tap, :],
                                                      identity=consts["identf"][0:24, 0:24]),
                 [gcr.b, consts["identf"].b], [pm[0].b], sig=(tap == 3))
        K.op("act", lambda e: e.copy(out=gcw[:], in_=pm[0][:, 0:128].rearrange("p (t n) -> p t n", n=32)[:, :, 0:24]),
             [pm[0].b], [gcw.b])
        K.dma("sp", nrm[:, 0:1], ins["sb_q_norm"][l, :].rearrange("(p o) -> p o", o=1), [], [nrm.b], nrm.b)
        K.dma("sp", nrm[:, 1:2], ins["sb_k_norm"][l, :].rearrange("(p o) -> p o", o=1), [], [nrm.b], nrm.b)
        K.dma("sp", hp[:, 0:1], ins["gdn_a_log"][l, :].rearrange("(p o) -> p o", o=1), [], [hp.b], hp.b)
        K.dma("sp", hp[:, 1:2], ins["gdn_dt_bias"][l, :].rearrange("(p o) -> p o", o=1), [], [hp.b], hp.b)
        K.op("act", lambda e: e.activation(out=hp[:, 2:3], in_=hp[:, 0:1], func=AF.Exp), [hp.b], [hp.b])
        K.op("dve", lambda e: e.tensor_scalar(out=hp[:, 2:3], in0=hp[:, 2:3], scalar1=-1.0, scalar2=None, op0=ALU.mult),
             [hp.b], [hp.b])
        for r in raw:
            K.op("pool", lambda e, r=r: e.memset(r[:, 0:3], 0.0), [], [r.b])
        st = {"pi": 0, "ti": 0}

        def mm_tile(w_, co, M=128):
            ps = []
            for tc in range(4):
                p_ = pm[st["pi"] % 6]
                st["pi"] += 1
                for kc in range(KC):
                    K.op("pe", lambda e, p_=p_, w_=w_, kc=kc, co=co, tc=tc, M=M: e.matmul(
                        p_[0:M, :], lhsT=w_[:, kc, co:co + M], rhs=hT[:, kc, tc * 512:(tc + 1) * 512],
                        start=(kc == 0), stop=(kc == KC - 1)),
                         [w_.b, hT.bs[tc]], [p_.b], sig=(kc == KC - 1))
                ps.append(p_)
            return ps

        def l2_or_rms(src, out_, scale, bias, gain_ap):
            K.op("act", lambda e: e.activation(out=ysq[:], in_=src[:, 0:S], func=AF.Square), [src.b], [ysq.b])
            for tc in range(4):
                p_ = po[tc % 2]
                K.op("pe", lambda e, p_=p_, tc=tc: e.matmul(p_[:], lhsT=ones_bf[:], rhs=ysq[:, tc * 512:(tc + 1) * 512],
                                                            start=True, stop=True), [ones_bf.b, ysq.b], [p_.b])
                K.op("act", lambda e, p_=p_, tc=tc: e.activation(out=lnv[:, tc * 512:(tc + 1) * 512], in_=p_[:], func=AF.Ln,
                                                                 scale=scale, bias=consts["bias"][:, bias:bias + 1]),
                     [p_.b, consts["bias"].b], [lnv.b])
            K.op("act", lambda e: e.activation(out=lnv[:], in_=lnv[:], func=AF.Exp, scale=-0.5), [lnv.b], [lnv.b])
            if gain_ap is None:
                K.op("dve", lambda e: e.tensor_tensor(out=out_[:], in0=src[:, 0:S], in1=lnv[:], op=ALU.mult),
                     [src.b, lnv.b], [out_.b])
            else:
                K.op("dve", lambda e: e.scalar_tensor_tensor(out=out_[:], in0=src[:, 0:S], scalar=gain_ap, in1=lnv[:],
                                                             op0=ALU.mult, op1=ALU.mult), [src.b, lnv.b, nrm.b], [out_.b])

        groups = []
        for kind, c0 in [("gq", 0), ("gk", 1024), ("gv", 2048), ("gz", 3072), ("sq", 4112), ("sk", 5136), ("sv", 6160)]:
            groups += [(kind, c0, 0), (kind, c0 + 512, 1)]
        for gi, (kind, c0, half) in enumerate(groups):
            w_ = wsl[gi % 3]
            if gi + 2 < len(groups):
                load_w(C, wsl[(gi + 2) % 3], W, groups[gi + 2][1], 512, 0)
            if kind == "gv" and half == 0:
                load_w(C, wba, W, 4096, 16, 0)
            if kind == "sv":
                if st.get("pending"):
                    st.pop("pending")()
                for t in range(NT):
                    sv_ = svt[t % 2]
                    p_ = pm[st["pi"] % 6]
                    st["pi"] += 1
                    for kc in range(KC):
                        K.op("pe", lambda e, p_=p_, w_=w_, kc=kc, t=t: e.matmul(
                            p_[:], lhsT=hT[:, kc, t * 128:(t + 1) * 128], rhs=w_[:, kc, :],
                            start=(kc == 0), stop=(kc == KC - 1)),
                             [w_.b, hT.bs[t // 4]], [p_.b], sig=(kc == KC - 1))
                    K.op("act", lambda e, p_=p_, sv_=sv_: e.copy(out=sv_[:], in_=p_[:]), [p_.b], [sv_.b])
                    K.dma("sp", scr["sv"][t][:, half * 512:(half + 1) * 512], sv_[:], [sv_.b], [scr["sv_b"]], sv_.b)
                continue
            for hh in range(half * 4, half * 4 + 4):
                ti = st["ti"]
                st["ti"] += 1
                ps = mm_tile(w_, (hh % 4) * 128)
                if st.get("pending"):
                    st.pop("pending")()
                r_, a_, o_ = raw[ti % 2], acc[ti % 2], ob[ti % 2]
                if kind in ("gq", "gk", "gv"):
                    ct = {"gq": 0, "gk": 8, "gv": 16}[kind] + hh
                    for tc in range(4):
                        K.op("act", lambda e, p_=ps[tc], r_=r_, tc=tc: e.copy(out=r_[:, 3 + tc * 512:3 + (tc + 1) * 512], in_=p_[:]),
                             [ps[tc].b], [r_.b])
                    K.op("act", lambda e, r_=r_, a_=a_, ct=ct: e.activation(out=a_[:], in_=r_[:, 3:3 + S], func=AF.Identity,
                                                                          scale=gcw[:, 3, ct:ct + 1]), [r_.b, gcw.b], [a_.b])
                    for tap in range(3):
                        K.op("dve", lambda e, r_=r_, a_=a_, ct=ct, tap=tap: e.scalar_tensor_tensor(
                            out=a_[:], in0=r_[:, tap:tap + S], scalar=gcw[:, tap, ct:ct + 1], in1=a_[:],
                            op0=ALU.mult, op1=ALU.add), [r_.b, gcw.b, a_.b], [a_.b])
                    if kind == "gv":
                        K.op("act", lambda e, a_=a_, o_=o_: e.activation(out=o_[:], in_=a_[:], func=AF.Silu), [a_.b], [o_.b])
                    else:
                        K.op("act", lambda e, a_=a_: e.activation(out=a_[:], in_=a_[:], func=AF.Silu), [a_.b], [a_.b])
                        def part2(a_=a_, o_=o_, kind=kind, hh=hh):
                            if kind == "gq":
                                l2_or_rms(a_, o_, 128.0, 0, None)
                            else:
                                l2_or_rms(a_, o_, 1.0, 1, None)
                            K.dma("sp", scr[kind][hh], o_[:], [o_.b], [scr[kind + "_b"]], o_.b)
                        st["pending"] = part2
                    if kind == "gv":
                        K.dma("sp", scr[kind][hh], o_[:], [o_.b], [scr[kind + "_b"]], o_.b)
                elif kind == "gz":
                    for tc in range(4):
                        K.op("act", lambda e, p_=ps[tc], o_=o_, tc=tc: e.activation(
                            out=o_[:, tc * 512:(tc + 1) * 512], in_=p_[:], func=AF.Silu), [ps[tc].b], [o_.b])
                    K.dma("sp", scr["gz"][hh], o_[:], [o_.b], [scr["gz_b"]], o_.b)
                else:
                    for tc in range(4):
                        K.op("act", lambda e, p_=ps[tc], a_=a_, tc=tc: e.copy(out=a_[:, tc * 512:(tc + 1) * 512], in_=p_[:]),
                             [ps[tc].b], [a_.b])
                    def part2(a_=a_, o_=o_, kind=kind, hh=hh):
                        if kind == "sq":
                            l2_or_rms(a_, o_, 1.0, 2, nrm[:, 0:1])
                        else:
                            l2_or_rms(a_, o_, 1.0 / 128.0, 3, nrm[:, 1:2])
                        K.dma("sp", scr[kind][hh], o_[:], [o_.b], [scr[kind + "_b"]], o_.b)
                    st["pending"] = part2
            if kind == "gz" and half == 1:
                if st.get("pending"):
                    st.pop("pending")()
                bb, ee, cc, rm = acc[0], acc[1], lnv, ysq
                for which in range(2):
                    ps = mm_tile(wba, which * 8, M=8)
                    for tc in range(4):
                        sl = slice(tc * 512, (tc + 1) * 512)
                        if which == 0:
                            K.op("act", lambda e, p_=ps[tc], sl=sl: e.activation(out=bb[0:8, sl], in_=p_[0:8, :], func=AF.Sigmoid),
                                 [ps[tc].b], [bb.b])
                        else:
                            K.op("act", lambda e, p_=ps[tc], sl=sl: e.activation(out=ee[0:8, sl], in_=p_[0:8, :], func=AF.Exp,
                                                                              bias=hp[:, 1:2]), [ps[tc].b, hp.b], [ee.b])
                K.op("act", lambda e: e.activation(out=ee[0:8, :], in_=ee[0:8, :], func=AF.Ln, bias=consts["bias"][0:8, 4:5]),
                     [ee.b, consts["bias"].b], [ee.b])
                K.op("dve", lambda e: e.tensor_scalar(out=ee[0:8, :], in0=ee[0:8, :], scalar1=hp[:, 2:3], scalar2=None,
                                                      op0=ALU.mult), [ee.b, hp.b], [ee.b])
                K.op("pool", lambda e: e.memset(rm[0:8, :], 1.0), [], [rm.b])
                K.op("pool", lambda e: e.memset(rm[0:8, :].rearrange("p (c n) -> p c n", n=128)[:, :, 0:1], 0.0), [rm.b], [rm.b])
                K.op("dve", lambda e: e.tensor_tensor_scan(out=cc[0:8, :], data0=rm[0:8, :], data1=ee[0:8, :], initial=0.0,
                                                           op0=ALU.mult, op1=ALU.add), [ee.b, rm.b], [cc.b])
                K.dma("sp", scr["beta"], bb[0:8, :], [bb.b], [scr["beta_b"]], bb.b)
                K.dma("sp", scr["gcum"], cc[0:8, :], [cc.b], [scr["gcum_b"]], cc.b)
        K.barrier()


GDN_STOP = 99


class _Stop(Exception):
    pass


_KREF = []


def _chk(k):
    if GDN_STOP <= k:
        _KREF[0].muted = True


def phase_gdn(C, l, ins, consts, scr):
    _KREF[:] = [C.K]
    _phase_gdn(C, l, ins, consts, scr)
    C.K.muted = False
    C.K.barrier()


def _phase_gdn(C, l, ins, consts, scr):
    K, nc = C.K, C.nc
    U32 = mybir.dt.uint32
    ident, onesf, identf, ones_bf = consts["ident"], consts["onesf"], consts["identf"], consts["ones_bf"]
    with ExitStack() as es:
        gT = C.sb(es, "gT", [8, S], F32)
        bT = C.sb(es, "bT", [8, S], F32)
        tk = C.sb(es, "tk", [128, 6, 128], F32)
        ogain = C.sb(es, "ogain", [128, 1], F32)
        masks = C.sb(es, "masks", [128, 14, 128], F32)
        masks8 = C.sb(es, "masks8", [128, 14, 512], mybir.dt.uint8)
        sel = C.sb(es, "sel", [128, 128], F32)
        zer = C.sb(es, "zer", [128, 128], F32)
        mneg = C.sb(es, "mneg", [128, 128], F32)
        B = [C.ps(es, f"B{i}", [128, 512], F32) for i in range(8)]
        K.dma("sp", gT[:], scr["gcum"], [scr["gcum_b"]], [gT.b], gT.b)
        K.dma("sp", bT[:], scr["beta"], [scr["beta_b"]], [bT.b], bT.b)
        K.dma("sp", ogain[:], ins["gdn_o_norm"][l, :].rearrange("(p o) -> p o", o=1), [], [ogain.b], ogain.b)
        K.op("pool", lambda e: e.memset(zer[:], 0.0), [], [zer.b])
        K.op("pool", lambda e: e.affine_select(out=mneg[:], in_=zer[:], pattern=[[1, 128]], compare_op=ALU.is_ge,
                                               fill=-30000.0, base=0, channel_multiplier=-1), [zer.b], [mneg.b])
        K.op("pool", lambda e: e.affine_select(out=sel[:], in_=onesf[:], pattern=[[0, 128]], compare_op=ALU.is_ge,
                                               fill=0.0, base=-127, channel_multiplier=1), [onesf.b], [sel.b])
        for lv in range(7):
            n = 1 << lv
            nb = 128 // (2 * n)
            specs = [
                (lv, [(-n, 1, [[-2 * n, nb], [0, 2], [0, n]]), (2 * n - 1, -1, [[2 * n, nb], [0, 2], [0, n]]),
                      (0, 0, [[0, nb], [-1, 2], [0, n]])]),
                (7 + lv, [(0, 1, [[-2 * n, nb], [0, 2], [0, n]]), (n - 1, -1, [[2 * n, nb], [0, 2], [0, n]]),
                          (-1, 0, [[0, nb], [1, 2], [0, n]])]),
            ]
            for mi, passes in specs:
                for pi, (base, cm, pat) in enumerate(passes):
                    src = onesf if pi == 0 else masks
                    K.op("pool", lambda e, mi=mi, base=base, cm=cm, pat=pat, pi=pi: e.affine_select(
                        out=masks[:, mi, :], in_=(onesf[:] if pi == 0 else masks[:, mi, :]), pattern=pat,
                        compare_op=ALU.is_ge, fill=0.0, base=base, channel_multiplier=cm),
                         [src.b], [masks.b])
        K.op("dve", lambda e: e.tensor_copy(out=masks8[:].rearrange("p m (r n) -> p m r n", n=128),
                                            in_=masks[:].unsqueeze(2).to_broadcast([128, 14, 4, 128])), [masks.b], [masks8.b])
        for which, srcT in ((0, gT), (1, bT)):
            for c in range(NT):
                K.op("pe", lambda e, which=which, srcT=srcT, c=c: e.transpose(
                    out=B[which][:, c * 8:c * 8 + 8], in_=srcT[:, c * 128:(c + 1) * 128],
                    identity=identf[0:8, 0:8]), [srcT.b, identf.b], [B[which].b], sig=(c == NT - 1))
            K.op("act", lambda e, which=which: e.copy(out=tk[:, which, :], in_=B[which][:, 0:128]),
                 [B[which].b], [tk.b])
        K.op("dve", lambda e: e.tensor_scalar(out=tk[:, 2, :], in0=tk[:, 1, :], scalar1=-1.0, scalar2=None, op0=ALU.mult),
             [tk.b], [tk.b])
        K.op("act", lambda e: e.activation(out=tk[:, 3, :], in_=tk[:, 0, :], func=AF.Exp), [tk.b], [tk.b])
        K.op("pe", lambda e: e.matmul(B[2][:, 0:128], lhsT=sel[:], rhs=tk[:, 0, :], start=True, stop=True),
             [sel.b, tk.b], [B[2].b])
        K.op("act", lambda e: e.activation(out=tk[:, 5, :], in_=B[2][:, 0:128], func=AF.Exp), [B[2].b], [tk.b])
        K.op("dve", lambda e: e.tensor_tensor(out=tk[:, 4, :], in0=B[2][:, 0:128], in1=tk[:, 0, :], op=ALU.subtract),
             [B[2].b, tk.b], [tk.b])
        K.op("act", lambda e: e.activation(out=tk[:, 4, :], in_=tk[:, 4, :], func=AF.Exp), [tk.b], [tk.b])
        _chk(1)

        def col(which, h):
            return tk[:, which, :].rearrange("p (c h) -> p c h", h=8)[:, :, h:h + 1]

        def v3(ap):
            return ap.rearrange("p (c n) -> p c n", n=128)

        with ExitStack() as eg:
            slots = []
            for hi in range(4):
                sl = {nm: C.sb(eg, f"{nm}{hi}", [128, S], BF16, nbufs=16) for nm in ("wT", "ub", "kd", "qg", "AT", "oT")}
                sl["S32"] = C.sb(eg, f"S32{hi}", [128, 128], F32)
                sl["Sbf"] = C.sb(eg, f"Sbf{hi}", [128, 128], BF16)
                sl["vn"] = [C.sb(eg, f"vn{hi}_{i}", [128, 128], BF16) for i in range(2)]
                slots.append(sl)
            for grp in range(2):
                heads = list(range(grp * 4, grp * 4 + 4))
                per = {h: slots[h % 4] for h in heads}
                if grp == 0:
                  qT = C.sb(eg, "qT", [128, S], BF16)
                  kT = C.sb(eg, "kT", [128, S], BF16)
                  vT = C.sb(eg, "vT", [128, S], BF16)
                  Rb = C.sb(eg, "Rb", [128, S], F32)
                  E = C.sb(eg, "E", [128, S], F32)
                  dU = C.sb(eg, "dU", [128, S], BF16)
                  LnT = C.sb(eg, "LnT", [128, S], BF16, nbufs=4)
                  Dm = C.sb(eg, "Dm", [128, S], BF16, nbufs=4)
                  DTm = C.sb(eg, "DTm", [128, S], BF16, nbufs=4)
                  Ysb = C.sb(eg, "Ysb", [128, S], BF16, nbufs=4)
                  kg = C.sb(eg, "kg", [128, S], BF16)
                  vtok = C.sb(eg, "vtok", [128, S], BF16)
                  zs = C.sb(eg, "zs", [128, S], BF16)
                for h in heads:
                    P = per[h]
                    K.dma("sp", qT[:], scr["gq"][h], [scr["gq_b"]], [qT.b], qT.b)
                    K.dma("sp", kT[:], scr["gk"][h], [scr["gk_b"]], [kT.b], kT.b)
                    K.dma("sp", vT[:], scr["gv"][h], [scr["gv_b"]], [vT.b], vT.b)
                    K.dma("sp", Rb[:], scr["gcum"][h, :].partition_broadcast(128), [scr["gcum_b"]], [Rb.b], Rb.b)
                    K.op("act", lambda e: e.activation(out=E[:], in_=Rb[:], func=AF.Exp), [Rb.b], [E.b])
                    K.op("dve", lambda e, P=P: e.tensor_tensor(out=P["qg"][:], in0=qT[:], in1=E[:], op=ALU.mult),
                         [qT.b, E.b], P["qg"].bs)
                    K.op("dve", lambda e, h=h: e.tensor_tensor(out=v3(E[:]), in0=v3(Rb[:]),
                                                               in1=col(0, h).to_broadcast([128, NT, 128]), op=ALU.subtract),
                         [Rb.b, tk.b], [E.b])
                    K.op("dve", lambda e: e.tensor_tensor(out=v3(E[:]), in0=v3(E[:]),
                                                          in1=mneg[:].unsqueeze(1).to_broadcast([128, NT, 128]), op=ALU.add),
                         [E.b, mneg.b], [E.b])
                    K.op("act", lambda e: e.activation(out=dU[:], in_=E[:], func=AF.Exp), [E.b], [dU.b])
                    for q in range(4):
                        for i in range(4):
                            cs = slice((q * 4 + i) * 128, (q * 4 + i + 1) * 128)
                            K.op("pe", lambda e, q=q, i=i, cs=cs: e.matmul(
                                B[q][:, i * 128:(i + 1) * 128], lhsT=kT[:, cs], rhs=kT[:, cs], start=True, stop=True),
                                 [kT.b], [B[q].b], sig=(i == 3))
                        for i in range(4):
                            c = q * 4 + i
                            cs = slice(c * 128, (c + 1) * 128)
                            K.op("dve", lambda e, q=q, i=i, cs=cs, c=c, h=h: e.scalar_tensor_tensor(
                                out=LnT[:, cs], in0=B[q][:, i * 128:(i + 1) * 128],
                                scalar=tk[:, 2, c * 8 + h:c * 8 + h + 1], in1=dU[:, cs], op0=ALU.mult, op1=ALU.mult),
                                 [B[q].b, tk.b, dU.b], [LnT.bs[q]])
                    for q in range(4):
                        for i in range(4):
                            cs = slice((q * 4 + i) * 128, (q * 4 + i + 1) * 128)
                            K.op("pe", lambda e, q=q, i=i, cs=cs: e.matmul(
                                B[4 + q][:, i * 128:(i + 1) * 128], lhsT=kT[:, cs], rhs=qT[:, cs], start=True, stop=True),
                                 [kT.b, qT.b], [B[4 + q].b], sig=(i == 3))
                        qs = slice(q * 512, (q + 1) * 512)
                        K.op("dve", lambda e, q=q, qs=qs, P=P: e.tensor_tensor(
                            out=P["AT"][:, qs], in0=B[4 + q][:], in1=dU[:, qs], op=ALU.mult),
                             [B[4 + q].b, dU.b], P["AT"].bs[q * 4:q * 4 + 4])
                    _chk(2)
                    for src, b0 in ((kT, 0), (vT, 2)):
                        for hb in range(2):
                            pv = B[b0 + hb][:].bitcast(BF16)
                            for i in range(8):
                                cs = slice((hb * 8 + i) * 128, (hb * 8 + i + 1) * 128)
                                K.op("pe", lambda e, pv=pv, i=i, cs=cs, src=src: e.transpose(
                                    out=pv[:, i * 128:(i + 1) * 128], in_=src[:, cs], identity=ident[:]),
                                     [src.b, ident.b], [B[b0 + hb].b], sig=(i == 7))
                    for hb in range(2):
                        hs_ = slice(hb * 1024, (hb + 1) * 1024)
                        pk = B[hb][:].bitcast(BF16)
                        pvv = B[2 + hb][:].bitcast(BF16)
                        K.op("dve", lambda e, h=h, hb=hb, hs_=hs_, pk=pk: e.tensor_tensor(
                            out=v3(kg[:, hs_]), in0=v3(pk), in1=col(3, h)[:, hb * 8:(hb + 1) * 8, :].to_broadcast([128, 8, 128]),
                            op=ALU.mult), [B[hb].b, tk.b], [kg.b])
                        K.op("dve", lambda e, h=h, hb=hb, hs_=hs_, pk=pk, P=P: e.tensor_tensor(
                            out=v3(P["kd"][:, hs_]), in0=v3(pk), in1=col(4, h)[:, hb * 8:(hb + 1) * 8, :].to_broadcast([128, 8, 128]),
                            op=ALU.mult), [B[hb].b, tk.b], P["kd"].bs[hb * 8:(hb + 1) * 8])
                        K.op("act", lambda e, hs_=hs_, pvv=pvv: e.copy(out=vtok[:, hs_], in_=pvv), [B[2 + hb].b], [vtok.b])
                    _chk(3)
                    K.op("dve", lambda e: e.tensor_copy(out=v3(Dm[:]), in_=ident[:].unsqueeze(1).to_broadcast([128, NT, 128])),
                         [ident.b], Dm.bs)
                    K.op("dve", lambda e: e.tensor_copy(out=v3(DTm[:]), in_=ident[:].unsqueeze(1).to_broadcast([128, NT, 128])),
                         [ident.b], DTm.bs)
                    for lv in range(7):
                        for q in range(4):
                            s_ = q % 2
                            qs = slice(q * 512, (q + 1) * 512)
                            Yp, Zp, ZTp = B[s_ * 3], B[s_ * 3 + 1], B[s_ * 3 + 2]
                            for i in range(4):
                                cs = slice(q * 512 + i * 128, q * 512 + (i + 1) * 128)
                                K.op("pe", lambda e, cs=cs, i=i, Yp=Yp: e.matmul(
                                    Yp[:, i * 128:(i + 1) * 128], lhsT=LnT[:, cs], rhs=Dm[:, cs],
                                    start=True, stop=True), [LnT.bs[q], Dm.bs[q]], [Yp.b], sig=(i == 3))
                            K.op("act", lambda e, qs=qs, Yp=Yp: e.copy(out=Ysb[:, qs], in_=Yp[:]), [Yp.b], [Ysb.bs[q]])
                            for i in range(4):
                                cs = slice(q * 512 + i * 128, q * 512 + (i + 1) * 128)
                                K.op("pe", lambda e, cs=cs, i=i, Zp=Zp: e.matmul(
                                    Zp[:, i * 128:(i + 1) * 128], lhsT=DTm[:, cs], rhs=Ysb[:, cs],
                                    start=True, stop=True), [DTm.bs[q], Ysb.bs[q]], [Zp.b], sig=(i == 3))
                            for i in range(4):
                                cs = slice(q * 512 + i * 128, q * 512 + (i + 1) * 128)
                                K.op("pe", lambda e, cs=cs, i=i, ZTp=ZTp: e.matmul(
                                    ZTp[:, i * 128:(i + 1) * 128], lhsT=Ysb[:, cs], rhs=DTm[:, cs],
                                    start=True, stop=True), [DTm.bs[q], Ysb.bs[q]], [ZTp.b], sig=(i == 3))
                            K.op("dve", lambda e, qs=qs, lv=lv, Zp=Zp: e.copy_predicated(
                                out=Dm[:, qs], mask=masks8[:, lv, :], data=Zp[:]), [Zp.b, masks8.b], [Dm.bs[q]])
                            K.op("dve", lambda e, qs=qs, lv=lv, ZTp=ZTp: e.copy_predicated(
                                out=DTm[:, qs], mask=masks8[:, 7 + lv, :], data=ZTp[:]), [ZTp.b, masks8.b], [DTm.bs[q]])
                    _chk(4)
                    for q in range(4):
                        qs = slice(q * 512, (q + 1) * 512)
                        for i in range(4):
                            cs = slice((q * 4 + i) * 128, (q * 4 + i + 1) * 128)
                            K.op("pe", lambda e, q=q, i=i, cs=cs: e.matmul(
                                B[q][:, i * 128:(i + 1) * 128], lhsT=DTm[:, cs], rhs=vtok[:, cs], start=True, stop=True),
                                 [DTm.bs[q], vtok.b], [B[q].b], sig=(i == 3))
                        K.op("dve", lambda e, q=q, qs=qs, P=P, h=h: e.tensor_tensor(
                            out=v3(P["ub"][:, qs]), in0=v3(B[q][:]),
                            in1=col(1, h)[:, q * 4:(q + 1) * 4, :].to_broadcast([128, 4, 128]), op=ALU.mult),
                             [B[q].b, tk.b], P["ub"].bs[q * 4:q * 4 + 4])
                    for q in range(4):
                        qs = slice(q * 512, (q + 1) * 512)
                        for i in range(4):
                            cs = slice((q * 4 + i) * 128, (q * 4 + i + 1) * 128)
                            K.op("pe", lambda e, q=q, i=i, cs=cs: e.matmul(
                                B[4 + q][:, i * 128:(i + 1) * 128], lhsT=kg[:, cs], rhs=DTm[:, cs], start=True, stop=True),
                                 [DTm.bs[q], kg.b], [B[4 + q].b], sig=(i == 3))
                        K.op("act", lambda e, q=q, qs=qs, P=P: e.copy(out=P["wT"][:, qs], in_=B[4 + q][:]),
                             [B[4 + q].b], P["wT"].bs[q * 4:q * 4 + 4])
                    K.op("pool", lambda e, P=P: e.memset(P["S32"][:], 0.0), [], [P["S32"].b])
                    K.op("pool", lambda e, P=P: e.memset(P["Sbf"][:], 0.0), [], [P["Sbf"].b])
                _chk(5)
                for c in range(NT):
                    cs = slice(c * 128, (c + 1) * 128)
                    par = (c % 2) * 3
                    Bw, Bo, Bs = B[par], B[par + 1], B[par + 2]
                    for hi, h in enumerate(heads):
                        P = per[h]
                        ss = slice(hi * 128, (hi + 1) * 128)
                        K.op("pe", lambda e, P=P, ss=ss, cs=cs, Bw=Bw: e.matmul(Bw[:, ss], lhsT=P["wT"][:, cs], rhs=P["Sbf"][:],
                                                                               start=True, stop=True),
                             [P["wT"].bs[c], P["Sbf"].b], [Bw.b], sig=(hi == 3))
                    for hi, h in enumerate(heads):
                        P = per[h]
                        ss = slice(hi * 128, (hi + 1) * 128)
                        idx = c * 8 + h
                        vn = P["vn"][c % 2]
                        K.op("dve", lambda e, P=P, ss=ss, cs=cs, idx=idx, vn=vn, Bw=Bw: e.scalar_tensor_tensor(
                            out=vn[:], in0=Bw[:, ss], scalar=tk[:, 2, idx:idx + 1], in1=P["ub"][:, cs],
                            op0=ALU.mult, op1=ALU.add), [Bw.b, tk.b, P["ub"].bs[c]], [vn.b])
                    for hi, h in enumerate(heads):
                        P = per[h]
                        ss = slice(hi * 128, (hi + 1) * 128)
                        vn = P["vn"][c % 2]
                        K.op("pe", lambda e, P=P, ss=ss, cs=cs, Bo=Bo: e.matmul(Bo[:, ss], lhsT=P["Sbf"][:], rhs=P["qg"][:, cs],
                                                                               start=True, stop=False),
                             [P["Sbf"].b, P["qg"].bs[c]], [Bo.b], sig=False)
                        K.op("pe", lambda e, P=P, ss=ss, cs=cs, vn=vn, Bo=Bo: e.matmul(Bo[:, ss], lhsT=vn[:], rhs=P["AT"][:, cs],
                                                                                      start=False, stop=True),
                             [vn.b, P["AT"].bs[c]], [Bo.b], sig=(hi == 3))
                    for hi, h in enumerate(heads):
                        P = per[h]
                        ss = slice(hi * 128, (hi + 1) * 128)
                        vn = P["vn"][c % 2]
                        K.op("pe", lambda e, P=P, ss=ss, cs=cs, vn=vn, Bs=Bs: e.matmul(Bs[:, ss], lhsT=P["kd"][:, cs], rhs=vn[:],
                                                                                      start=True, stop=True),
                             [vn.b, P["kd"].bs[c]], [Bs.b], sig=(hi == 3))
                    for hi, h in enumerate(heads):
                        P = per[h]
                        ss = slice(hi * 128, (hi + 1) * 128)
                        idx = c * 8 + h
                        K.op("dve", lambda e, P=P, ss=ss, idx=idx, Bs=Bs: e.scalar_tensor_tensor(
                            out=P["S32"][:], in0=P["S32"][:], scalar=tk[:, 5, idx:idx + 1], in1=Bs[:, ss],
                            op0=ALU.mult, op1=ALU.add), [P["S32"].b, tk.b, Bs.b], [P["S32"].b])
                        K.op("act", lambda e, P=P: e.copy(out=P["Sbf"][:], in_=P["S32"][:]), [P["S32"].b], [P["Sbf"].b])
                    for hi, h in enumerate(heads):
                        P = per[h]
                        ss = slice(hi * 128, (hi + 1) * 128)
                        K.op("act", lambda e, P=P, ss=ss, cs=cs, Bo=Bo: e.copy(out=P["oT"][:, cs], in_=Bo[:, ss]),
                             [Bo.b], [P["oT"].bs[c]])
                _chk(6)
                for h in heads:
                    P = per[h]
                    K.dma("sp", zs[:], scr["gz"][h], [scr["gz_b"]], [zs.b], zs.b)
                    K.op("act", lambda e, P=P: e.activation(out=dU[:], in_=P["oT"][:], func=AF.Square), P["oT"].bs, [dU.b])
                    for tc in range(4):
                        ts_ = slice(tc * 512, (tc + 1) * 512)
                        Bn = B[6 + tc % 2]
                        K.op("pe", lambda e, ts_=ts_, Bn=Bn: e.matmul(Bn[:], lhsT=ones_bf[:], rhs=dU[:, ts_], start=True, stop=True),
                             [ones_bf.b, dU.b], [Bn.b])
                        K.op("act", lambda e, ts_=ts_, Bn=Bn: e.activation(
                            out=E[:, ts_], in_=Bn[:], func=AF.Ln, scale=1.0 / 128.0,
                            bias=consts["bias"][:, 3:4]), [Bn.b, consts["bias"].b], [E.b])
                    K.op("act", lambda e: e.activation(out=E[:], in_=E[:], func=AF.Exp, scale=-0.5), [E.b], [E.b])
                    K.op("dve", lambda e, P=P: e.scalar_tensor_tensor(out=LnT[:], in0=P["oT"][:], scalar=ogain[:, 0:1], in1=E[:],
                                                                      op0=ALU.mult, op1=ALU.mult),
                         P["oT"].bs + [ogain.b, E.b], LnT.bs)
                    K.op("dve", lambda e: e.tensor_tensor(out=Dm[:], in0=LnT[:], in1=zs[:], op=ALU.mult),
                         LnT.bs + [zs.b], Dm.bs)
                    K.dma("sp", scr["mix"][h], Dm[:], Dm.bs, [scr["mix_b"]], Dm.bs[0])
                K.barrier()


def phase_sb(C, l, ins, consts, scr):
    K, nc = C.K, C.nc
    ones_bf, ident = consts["ones_bf"], consts["ident"]
    with ExitStack() as es:
        qT = [C.sb(es, f"sqT{i}", [128, S], BF16) for i in range(2)]
        kT = [C.sb(es, f"skT{i}", [128, S], BF16) for i in range(2)]
        nkT = [C.sb(es, f"snkT{i}", [128, S], BF16) for i in range(2)]
        vt = [C.sb(es, f"svt{i}", [128, NT, 128], BF16) for i in range(2)]
        e1 = [C.sb(es, f"e1_{i}", [128, 512], F32) for i in range(2)]
        sp = [C.sb(es, f"sp{i}", [128, 512], BF16) for i in range(2)]
        sps = [C.sb(es, f"sps{i}", [128, 512], BF16) for i in range(2)]
        ww = [C.sb(es, f"ww{i}", [128, 512], BF16) for i in range(2)]
        yraw = [C.sb(es, f"yraw{i}", [128, S], F32) for i in range(2)]
        ysq = C.sb(es, "ysq", [128, S], BF16)
        rn = C.sb(es, "rn", [128, S], F32)
        ymx = [C.sb(es, f"ymx{i}", [128, S], BF16) for i in range(2)]
        mneg = C.sb(es, "mnegs", [128, 4, 512], BF16)
        mpos = C.sb(es, "mposs", [128, 4, 512], BF16)
        uin = C.sb(es, "uin", [128, 128], BF16)
        og = C.sb(es, "og", [128, 1], F32)
        Bz = [C.ps(es, f"Bz{i}", [128, 512], F32) for i in range(4)]
        Bn = [C.ps(es, f"Bn{i}", [128, 512], F32) for i in range(2)]
        nuin = C.sb(es, "nuin", [128, 128], BF16)
        nones = C.sb(es, "nones", [128, 128], BF16)
        Bo = [C.ps(es, f"Bo{i}", [128, 512], F32) for i in range(2)]
        onesw = C.sb(es, "onesw", [128, 512], F32)
        zerw = C.sb(es, "zerw", [128, 512], F32)
        K.op("pool", lambda e: e.memset(onesw[:], 1.0), [], [onesw.b])
        K.op("pool", lambda e: e.memset(zerw[:], 0.0), [], [zerw.b])
        for k in range(4):
            K.op("pool", lambda e, k=k: e.affine_select(out=mneg[:, k, :], in_=zerw[:], pattern=[[1, 512]],
                                                        compare_op=ALU.is_gt, fill=-30000.0, base=-128 * k, channel_multiplier=-1),
                 [zerw.b], [mneg.b])
            K.op("pool", lambda e, k=k: e.affine_select(out=mpos[:, k, :], in_=zerw[:], pattern=[[1, 512]],
                                                        compare_op=ALU.is_gt, fill=30000.0, base=-128 * k, channel_multiplier=-1),
                 [zerw.b], [mpos.b])
        K.op("pool", lambda e: e.affine_select(out=uin[:], in_=onesw[:, 0:128], pattern=[[-1, 128]], compare_op=ALU.is_ge,
                                               fill=0.0, base=0, channel_multiplier=1), [onesw.b], [uin.b])
        K.dma("sp", og[:], ins["sb_o_norm"][l, :].rearrange("(p o) -> p o", o=1), [], [og.b], og.b)
        K.op("pool", lambda e: e.tensor_scalar(out=nuin[:], in0=uin[:], scalar1=-1.0, scalar2=None, op0=ALU.mult), [uin.b], [nuin.b])
        K.op("pool", lambda e: e.memset(nones[:], -1.0), [], [nones.b])

        blocks = []
        for h in range(H):
            for tc in range(4):
                nb = 4 * tc + 4
                for b in range(nb - 1, -1, -1):
                    blocks.append((h, tc, b, b == nb - 1, b == 0))

        def load(h):
            i = h % 2
            K.dma("sp", qT[i][:], scr["sq"][h], [scr["sq_b"]], [qT[i].b], qT[i].b)
            K.dma("sp", kT[i][:], scr["sk"][h], [scr["sk_b"]], [kT[i].b], kT[i].b)
            K.dma("sp", vt[i][:], scr["sv"][:, :, h * 128:(h + 1) * 128].rearrange("t p d -> p t d"),
                  [scr["sv_b"]], [vt[i].b], vt[i].b)
            K.op("pool", lambda e: e.tensor_scalar(out=nkT[i][:], in0=kT[i][:], scalar1=-1.0, scalar2=None, op0=ALU.mult),
                 [kT[i].b], [nkT[i].b])

        def info(n):
            h, tc, b, first, last = blocks[n]
            return h, tc, b, first, last, h % 2, n % 4, slice(tc * 512, (tc + 1) * 512), b - 4 * tc

        def a1(n):
            h, tc, b, first, last, i, j, ts_, k = info(n)
            K.op("pe", lambda e: e.matmul(Bz[j][:], lhsT=kT[i][:, b * 128:(b + 1) * 128], rhs=qT[i][:, ts_],
                                          start=True, stop=(k < 0)), [kT[i].b, qT[i].b], [Bz[j].b], sig=(k < 0))
            if k >= 0:
                K.op("pe", lambda e: e.matmul(Bz[j][:], lhsT=ident[:], rhs=mneg[:, k, :], start=False, stop=True),
                     [ident.b, mneg.b], [Bz[j].b])

        def a2(n):
            h, tc, b, first, last, i, j, ts_, k = info(n)
            K.op("act", lambda e: e.activation(out=e1[n % 2][:], in_=Bz[j][:], func=AF.Exp), [Bz[j].b], [e1[n % 2].b])
            K.op("act", lambda e: e.activation(out=sp[n % 2][:], in_=e1[n % 2][:], func=AF.Ln, bias=consts["bias"][:, 4:5]),
                 [e1[n % 2].b, consts["bias"].b], [sp[n % 2].b])

        def b1(n):
            h, tc, b, first, last, i, j, ts_, k = info(n)
            sp_ = sp[n % 2]
            K.op("pe", lambda e: e.matmul(Bz[j][:], lhsT=nuin[:], rhs=sp_[:], start=False, stop=first, skip_group_check=True),
                 [nuin.b, sp_.b], [Bz[j].b], sig=first)
            if not first:
                K.op("pe", lambda e: e.matmul(Bz[j][:], lhsT=nones[:], rhs=sps[(n - 1) % 2][:], start=False, stop=True,
                                              skip_group_check=True),
                     [nones.b, sps[(n - 1) % 2].b], [Bz[j].b])
            if not last:
                if first:
                    K.op("dve", lambda e: e.tensor_copy(out=sps[n % 2][:], in_=sp_[:]), [sp_.b], [sps[n % 2].b])
                else:
                    K.op("dve", lambda e: e.tensor_tensor(out=sps[n % 2][:], in0=sps[(n - 1) % 2][:], in1=sp_[:], op=ALU.add),
                         [sps[(n - 1) % 2].b, sp_.b], [sps[n % 2].b])

        def b2(n):
            h, tc, b, first, last, i, j, ts_, k = info(n)
            o_ = Bo[tc % 2]
            K.op("act", lambda e: e.activation(out=ww[n % 2][:], in_=Bz[j][:], func=AF.Exp), [Bz[j].b], [ww[n % 2].b])
            K.op("pe", lambda e: e.matmul(o_[:], lhsT=vt[i][:, b, :], rhs=ww[n % 2][:], start=first, stop=last),
                 [vt[i].b, ww[n % 2].b], [o_.b], sig=last)
            if last:
                y_ = yraw[h % 2]
                K.op("act", lambda e: e.copy(out=y_[:, ts_], in_=o_[:]), [o_.b], [y_.b])
                if tc == 3:
                    finish(h)

        def finish(h):
            y_, m_ = yraw[h % 2], ymx[h % 2]
            K.op("act", lambda e: e.activation(out=ysq[:], in_=y_[:], func=AF.Square), [y_.b], [ysq.b])
            for tc in range(4):
                ts_ = slice(tc * 512, (tc + 1) * 512)
                p_ = Bn[tc % 2]
                K.op("pe", lambda e, p_=p_, ts_=ts_: e.matmul(p_[:], lhsT=ones_bf[:], rhs=ysq[:, ts_], start=True, stop=True),
                     [ones_bf.b, ysq.b], [p_.b])
                K.op("act", lambda e, p_=p_, ts_=ts_: e.activation(out=rn[:, ts_], in_=p_[:], func=AF.Ln, scale=1.0 / 128.0,
                                                                   bias=consts["bias"][:, 3:4]), [p_.b, consts["bias"].b], [rn.b])
            K.op("act", lambda e: e.activation(out=rn[:], in_=rn[:], func=AF.Exp, scale=-0.5), [rn.b], [rn.b])
            K.op("dve", lambda e: e.scalar_tensor_tensor(out=m_[:], in0=y_[:], scalar=og[:, 0:1], in1=rn[:],
                                                         op0=ALU.mult, op1=ALU.mult), [y_.b, og.b, rn.b], [m_.b])
            K.dma("sp", scr["mix"][8 + h], m_[:], [m_.b], [scr["mix_b"]], m_.b)

        NB = len(blocks)
        load(0)
        load(1)
        a1(0)
        a1(1)
        for n in range(NB):
            h, tc, b, first, last = blocks[n]
            if n + 2 < NB:
                h2 = blocks[n + 2][0]
                if h2 != blocks[n + 1][0] and h2 + 1 < H:
                    pass
                a1(n + 2)
            a2(n)
            b1(n)
            if n > 0:
                b2(n - 1)
                if blocks[n - 1][4] and blocks[n - 1][1] == 3 and blocks[n - 1][0] + 2 < H:
                    load(blocks[n - 1][0] + 2)
        b2(NB - 1)
        K.barrier()


def phase_outproj(C, l, x_in, x_in_buf, x_out, x_out_buf, ins, consts, scr):
    K, nc = C.K, C.nc
    with ExitStack() as es:
        mx = C.sb(es, "mx", [128, KC, S], BF16)
        wo = C.sb(es, "wo", [128, KC, D], BF16, nbufs=4)
        xt = [C.sb(es, f"xo_in{i}", [128, 512], F32) for i in range(4)]
        xo = [C.sb(es, f"xo_out{i}", [128, 512], F32) for i in range(4)]
        pp = [C.ps(es, f"po{i}", [128, 512], F32) for i in range(8)]
        for kc in range(KC):
            K.dma("sp", mx[:, kc, :], scr["mix"][kc], [scr["mix_b"]], [mx.b], mx.b, grp=True)
        alloc_stage(C, es)
        for cq in range(4):
            load_w(C, wo, ins["w_out"][l], cq * 512, 512, cq * 512, wbuf=wo.bs[cq])
        n = 0
        for cq in range(4):
            cs_ = slice(cq * 512, (cq + 1) * 512)
            for t in range(NT):
                x_, o_ = xt[n % 4], xo[n % 4]
                p_ = pp[n % 8]
                n += 1
                K.dma("pool", x_[:], x_in[t * 128:(t + 1) * 128, cs_], [x_in_buf], [x_.b], x_.b)
                for kc in range(KC):
                    K.op("pe", lambda e, p_=p_, kc=kc, t=t, cs_=cs_: e.matmul(
                        p_[:], lhsT=mx[:, kc, t * 128:(t + 1) * 128], rhs=wo[:, kc, cs_],
                        start=(kc == 0), stop=(kc == KC - 1)), [mx.b, wo.bs[cq]], [p_.b], sig=(kc == KC - 1))
                K.op("dve", lambda e, p_=p_, x_=x_, o_=o_: e.tensor_tensor(out=o_[:], in0=p_[:], in1=x_[:], op=ALU.add),
                     [p_.b, x_.b], [o_.b])
                K.dma("pool", x_out[t * 128:(t + 1) * 128, cs_], o_[:], [o_.b], [x_out_buf], o_.b)
        K.barrier()


def make_consts(C, es):
    K, nc = C.K, C.nc
    c = {}
    c["neghalf"] = C.sb(es, "neghalf", [128, 1], F32)
    K.op("pool", lambda e: e.memset(c["neghalf"][:], -0.5), [], [c["neghalf"].b])
    onesf = C.sb(es, "onesf", [128, 128], F32)
    K.op("pool", lambda e: e.memset(onesf[:], 1.0), [], [onesf.b])
    identf = C.sb(es, "identf", [128, 128], F32)
    K.op("pool", lambda e: e.affine_select(out=identf[:], in_=onesf[:], pattern=[[1, 128]],
                                           compare_op=ALU.is_equal, fill=0.0, base=0, channel_multiplier=-1),
         [onesf.b], [identf.b])
    c["ident"] = C.sb(es, "ident", [128, 128], BF16)
    K.op("pool", lambda e: e.tensor_copy(out=c["ident"][:], in_=identf[:]), [identf.b], [c["ident"].b])
    c["onesf"] = onesf
    c["identf"] = identf
    c["ones_bf"] = C.sb(es, "ones_bf", [128, 128], BF16)
    K.op("pool", lambda e: e.memset(c["ones_bf"][:], 1.0), [], [c["ones_bf"].b])
    c["bias"] = C.sb(es, "biasc", [128, 8], F32)
    for i, v in enumerate([128.0 * L2_EPS, L2_EPS, 128.0 * RMS_EPS, RMS_EPS, 1.0]):
        K.op("pool", lambda e, i=i, v=v: e.memset(c["bias"][:, i:i + 1], v), [], [c["bias"].b])
    return c


PARAMS = [("attn_norm", [DEPTH, D]), ("w_in", [DEPTH, D, IN_COLS]), ("gdn_conv", [DEPTH, 4, 3072]),
          ("gdn_a_log", [DEPTH, H]), ("gdn_dt_bias", [DEPTH, H]), ("gdn_o_norm", [DEPTH, HD]),
          ("sb_q_norm", [DEPTH, HD]), ("sb_k_norm", [DEPTH, HD]), ("sb_o_norm", [DEPTH, HD]),
          ("w_out", [DEPTH, D, D]), ("ffn_norm", [DEPTH, D]), ("w_up", [DEPTH, D, 2 * D_FF]),
          ("ffn_conv", [DEPTH, 3, 2 * D_FF]), ("ffn_conv_bias", [DEPTH, 2 * D_FF]),
          ("w_down", [DEPTH, D_FF, D])]


def build(mode="full"):
    nc = bass.Bass("TRN2", target_bir_lowering=False)
    only = mode.startswith("only")
    small = {"attn_norm", "gdn_conv", "gdn_a_log", "gdn_dt_bias", "gdn_o_norm", "sb_q_norm", "sb_k_norm", "sb_o_norm",
             "ffn_norm", "ffn_conv", "ffn_conv_bias"}
    ins = {"x": nc.dram_tensor("x", [S, D], F32, kind="ExternalInput").ap()}
    for name, shape in PARAMS:
        if only and name not in small:
            continue
        ins[name] = nc.dram_tensor(name, shape, F32, kind="ExternalInput").ap()
    y = nc.dram_tensor("y", [S, D], F32, kind="ExternalOutput").ap()
    dbg = "ExternalOutput" if mode.startswith("dbg") else "Internal"
    sin = "ExternalInput" if only else dbg
    scratch = {}
    for nm in ("gq", "gk", "gv", "gz", "sq", "sk"):
        scratch[nm] = nc.dram_tensor(nm + "_scr", [H, 128, S], BF16, kind=sin).ap()
        scratch[nm + "_b"] = Buf(nm + "_scr")
    scratch["sv"] = nc.dram_tensor("sv_scr", [NT, 128, 1024], BF16, kind=sin).ap()
    scratch["sv_b"] = Buf("sv_scr")
    for nm in ("beta", "gcum"):
        scratch[nm] = nc.dram_tensor(nm + "_scr", [H, S], F32, kind=sin).ap()
        scratch[nm + "_b"] = Buf(nm + "_scr")
    scratch["mix"] = nc.dram_tensor("mix_scr", [16, 128, S], BF16, kind=("ExternalOutput" if only else dbg)).ap()
    scratch["mix_b"] = Buf("mix_scr")
    scratch.update({
        "a": nc.dram_tensor("a_scr", [NT, 128, NFF, 128], BF16).ap(),
        "a_buf": Buf("a_scr"),
        "x1": nc.dram_tensor("x1_scr", [S, D], F32).ap(),
        "x2": nc.dram_tensor("x2_scr", [S, D], F32).ap(),
    })
    K = Sched(nc)
    with ExitStack() as es:
        C = Ctx(nc, K, es)
        consts = make_consts(C, es)
        xb, yb = Buf("xin"), Buf("yout")
        if mode == "ffn0":
            phase_ffn(C, 0, ins["x"], xb, y, yb, ins, consts, scratch)
        if mode in ("dbg_inproj", "dbg_gdn"):
            phase_inproj(C, 0, ins["x"], xb, ins, consts, scratch)
        if mode in ("dbg_gdn", "only_gdn"):
            phase_gdn(C, 0, ins, consts, scratch)
        if mode == "only_sb":
            phase_sb(C, 0, ins, consts, scratch)
        if mode == "full":
            x1b, x2b = Buf("x1s"), Buf("x2s")
            cur, curb = ins["x"], xb
            for l in range(DEPTH):
                K.scope = f"inproj{l}"
                phase_inproj(C, l, cur, curb, ins, consts, scratch)
                K.scope = f"gdn{l}"
                phase_gdn(C, l, ins, consts, scratch)
                K.scope = f"sb{l}"
                phase_sb(C, l, ins, consts, scratch)
                K.scope = f"outproj{l}"
                phase_outproj(C, l, cur, curb, scratch["x1"], x1b, ins, consts, scratch)
                K.scope = f"ffn{l}"
                if l == DEPTH - 1:
                    phase_ffn(C, l, scratch["x1"], x1b, y, yb, ins, consts, scratch)
                else:
                    phase_ffn(C, l, scratch["x1"], x1b, scratch["x2"], x2b, ins, consts, scratch)
                    cur, curb = scratch["x2"], x2b
        K.barrier()
        with nc.Block() as block:
            @block.tensor
            def _(e):
                K.emit("pe", e)

            @block.scalar
            def _(e):
                K.emit("act", e)

            @block.vector
            def _(e):
                K.emit("dve", e)

            @block.gpsimd
            def _(e):
                K.emit("pool", e)

            @block.sync
            def _(e):
                K.emit("sp", e)
    return nc, K


def kernel(**inputs):
    nc, K = build("full")
    x = np.ascontiguousarray(inputs["x"], dtype=np.float32)
    in_maps = []
    for c in range(8):
        m = {"x": x[c]}
        for name, _ in PARAMS:
            m[name] = np.ascontiguousarray(inputs[name], dtype=np.float32)
        in_maps.append(m)
    res = run_bass_kernel_spmd(nc, in_maps, core_ids=list(range(8)))
    return np.stack([r["y"] for r in res.results], axis=0)
```
